# Optimizing a Trainium2 kernel written in Bass

```python
import jax, jax.numpy as jnp
from jax import lax
import numpy as np

D_MODEL = 1024
BATCH = 8
SEQ = 4096
DEPTH = 1

GRID_W = 64
CTX_LEN = 256
D_SSD = 1024
SSD_HEAD_DIM = 64
SSD_HEADS = D_SSD // SSD_HEAD_DIM
SSD_GROUPS = 2
SSD_HPG = SSD_HEADS // SSD_GROUPS
SSD_STATE = 128
SSD_CONV = 5
SSD_CHUNK = 128
SSD_GN = SSD_GROUPS * SSD_STATE
XBC_DIM = D_SSD + 2 * SSD_GN
D_GM = 1024
GM_GROUPS = 8
GM_GROUP_DIM = D_GM // GM_GROUPS
GM_CHUNK = 128
ROWS_PER_CHUNK = GM_CHUNK // GRID_W
D_FF = -(-8 * D_MODEL // (3 * 256)) * 256
D_PROJ = D_SSD + XBC_DIM + 2 * SSD_HEADS + 2 * D_GM + 2 * D_MODEL
ALPHA = (2 * DEPTH) ** 0.25
BETA = (8 * DEPTH) ** -0.25
LN_EPS = 1e-5

kernel_name = 'hybrid_ssd_gmlp_dit_block'


def layer_norm(x, g, b):
    xf = x.astype(jnp.float32)
    mu = jnp.mean(xf, axis=-1, keepdims=True)
    var = jnp.mean(jnp.square(xf - mu), axis=-1, keepdims=True)
    return ((xf - mu) * lax.rsqrt(var + LN_EPS) * g + b).astype(x.dtype)


def gated_rms_norm(y, z, g):
    h = (y * jax.nn.silu(z)).astype(jnp.float32)
    h = h * lax.rsqrt(jnp.mean(jnp.square(h), axis=-1, keepdims=True) + LN_EPS)
    return (h * g).astype(y.dtype)


def dwconv_centred(x, w, b):
    ch = x.shape[-1]
    pad = SSD_CONV // 2
    y = lax.conv_general_dilated(x, w[:, None, :], window_strides=(1,), padding=[(pad, pad)],
                                 dimension_numbers=('NWC', 'WIO', 'NWC'), feature_group_count=ch)
    return y + b


def split_proj(p):
    o1 = D_SSD
    o2 = o1 + XBC_DIM
    o3 = o2 + 2 * SSD_HEADS
    o4 = o3 + D_GM
    o5 = o4 + D_GM
    return p[..., :o1], p[..., o1:o2], p[..., o2:o3], p[..., o3:o4], p[..., o4:o5], p[..., o5:]


def ssd_scan(xs, dt, A, Bm, Cm, h0):
    bsz, L, _ = xs.shape
    nc = L // SSD_CHUNK
    x = xs.reshape(bsz, nc, SSD_CHUNK, SSD_GROUPS, SSD_HPG, SSD_HEAD_DIM)
    dt = dt.reshape(bsz, nc, SSD_CHUNK, SSD_GROUPS, SSD_HPG)
    Bm = Bm.reshape(bsz, nc, SSD_CHUNK, SSD_GROUPS, SSD_STATE)
    Cm = Cm.reshape(bsz, nc, SSD_CHUNK, SSD_GROUPS, SSD_STATE)
    acum = jnp.cumsum(dt * A.reshape(SSD_GROUPS, SSD_HPG), axis=2)
    seg = acum[:, :, :, None] - acum[:, :, None, :]
    tri = jnp.tril(jnp.ones((SSD_CHUNK, SSD_CHUNK), dtype=bool))[:, :, None, None]
    lmat = jnp.where(tri, jnp.exp(jnp.where(tri, seg, 0.0)), 0.0)
    cb = jnp.einsum('bcign,bcjgn->bcijg', Cm, Bm)
    wgt = cb[..., None] * lmat * dt[:, :, None]
    y_diag = jnp.einsum('bcijgh,bcjghp->bcighp', wgt, x)
    decay_to_end = jnp.exp(acum[:, :, -1:] - acum)
    states = jnp.einsum('bcjgn,bcjgh,bcjghp->bcghpn', Bm, decay_to_end * dt, x)
    chunk_decay = jnp.exp(acum[:, :, -1])

    def step(h, inp):
        dec, st = inp
        h_new = (dec[..., None, None] * h + st).astype(h.dtype)
        return h_new, h

    h_final, h_prev = lax.scan(step, h0.astype(states.dtype),
                               (jnp.moveaxis(chunk_decay, 1, 0), jnp.moveaxis(states, 1, 0)))
    h_prev = jnp.moveaxis(h_prev, 0, 1)
    y_off = jnp.einsum('bcign,bcghpn,bcigh->bcighp', Cm, h_prev, jnp.exp(acum))
    return (y_diag + y_off).reshape(bsz, L, D_SSD), h_final


def ssd_branch(z, xbc, dt_raw, conv_w, conv_b, dt_bias, a_log, d_skip, norm_g, h0_f, h0_b):
    bsz, L, _ = xbc.shape
    xbc = jax.nn.silu(dwconv_centred(xbc, conv_w, conv_b))
    xs, Bm, Cm = xbc[..., :D_SSD], xbc[..., D_SSD:D_SSD + SSD_GN], xbc[..., D_SSD + SSD_GN:]
    dt = jax.nn.softplus(dt_raw.reshape(bsz, L, 2, SSD_HEADS) + dt_bias)
    A = -jnp.exp(a_log)
    y_f, s_f = ssd_scan(xs, dt[:, :, 0], A[0], Bm, Cm, h0_f)
    y_b, s_b = ssd_scan(jnp.flip(xs, 1), jnp.flip(dt[:, :, 1], 1), A[1],
                        jnp.flip(Bm, 1), jnp.flip(Cm, 1), h0_b)
    skip = (xs.reshape(bsz, L, SSD_HEADS, SSD_HEAD_DIM) * (d_skip[0] + d_skip[1])[:, None]).reshape(bsz, L, D_SSD)
    y = y_f + jnp.flip(y_b, 1) + skip
    return gated_rms_norm(y, z, norm_g), s_f, s_b


def spatial_gating(u, v, n_chunks, g, b, w_s, b_s):
    bsz = u.shape[0]
    v = layer_norm(v, g, b).reshape(bsz, n_chunks, GM_CHUNK, GM_GROUPS, GM_GROUP_DIM)
    mixed = jnp.einsum('gpq,bnqgc->bnpgc', w_s, v) + b_s.T[:, :, None]
    return u * mixed.reshape(u.shape)


def branch_merge(y_ssd, u, v, gates, n_chunks, gm_g, gm_b, w_s, b_s, b_gate, w_ssd_proj, w_gm_proj, w_out):
    y_gm = spatial_gating(jax.nn.gelu(u), jax.nn.gelu(v), n_chunks, gm_g, gm_b, w_s, b_s)
    g = jax.nn.sigmoid(gates + b_gate)
    merged = g[..., :D_MODEL] * (y_ssd @ w_ssd_proj) + g[..., D_MODEL:] * (y_gm @ w_gm_proj)
    return merged @ w_out


def swiglu(h, w1, w3, w2):
    return (jax.nn.silu(h @ w1) * (h @ w3)) @ w2


def setup_inputs(seed: int = 0) -> dict:
    key = jax.random.key(seed)
    ks = jax.random.split(key, 32)
    f32 = jnp.float32

    def nrm(k, shape, scale=1.0):
        return jax.random.normal(k, shape, f32) * scale

    dt0 = jnp.exp(jax.random.uniform(ks[8], (DEPTH, 2, SSD_HEADS), f32, np.log(1e-3), np.log(1e-1)))
    return {
        'x': nrm(ks[0], (BATCH, SEQ, D_MODEL)),
        'c': nrm(ks[1], (BATCH, D_MODEL)),
        'ctx': nrm(ks[2], (BATCH, CTX_LEN, D_MODEL)),
        'c_ctx': nrm(ks[3], (D_MODEL,)),
        'ln0_g': 1.0 + nrm(ks[4], (D_MODEL,), 0.01),
        'ln0_b': nrm(ks[5], (D_MODEL,), 0.01),
        'w_ada': nrm(ks[6], (DEPTH, D_MODEL, 6 * D_MODEL), 0.3 * D_MODEL ** -0.5),
        'b_ada': nrm(ks[7], (DEPTH, 6 * D_MODEL), 0.01),
        'w_in': nrm(ks[9], (DEPTH, D_MODEL, D_PROJ), D_MODEL ** -0.5),
        'conv_w': nrm(ks[10], (DEPTH, SSD_CONV, XBC_DIM), SSD_CONV ** -0.5),
        'conv_b': nrm(ks[11], (DEPTH, XBC_DIM), 0.01),
        'dt_bias': dt0 + jnp.log(-jnp.expm1(-dt0)),
        'a_log': jnp.log(jax.random.uniform(ks[12], (DEPTH, 2, SSD_HEADS), f32, 1.0, 16.0)),
        'd_skip': 1.0 + nrm(ks[13], (DEPTH, 2, SSD_HEADS), 0.01),
        'ssd_norm_g': 1.0 + nrm(ks[14], (DEPTH, D_SSD), 0.01),
        'gm_norm_g': 1.0 + nrm(ks[15], (DEPTH, D_GM), 0.01),
        'gm_norm_b': nrm(ks[16], (DEPTH, D_GM), 0.01),
        'w_spatial': nrm(ks[17], (DEPTH, GM_GROUPS, GM_CHUNK, GM_CHUNK), GM_CHUNK ** -0.5),
        'b_spatial': 1.0 + nrm(ks[18], (DEPTH, GM_GROUPS, GM_CHUNK), 0.01),
        'b_gate': nrm(ks[19], (DEPTH, 2 * D_MODEL), 0.01),
        'w_ssd_proj': nrm(ks[20], (DEPTH, D_SSD, D_MODEL), BETA * D_SSD ** -0.5),
        'w_gm_proj': nrm(ks[21], (DEPTH, D_GM, D_MODEL), BETA * D_GM ** -0.5),
        'w_out': nrm(ks[22], (DEPTH, D_MODEL, D_MODEL), BETA * D_MODEL ** -0.5),
        'ln1_g': 1.0 + nrm(ks[23], (DEPTH, D_MODEL), 0.01),
        'ln1_b': nrm(ks[24], (DEPTH, D_MODEL), 0.01),
        'w_ff1': nrm(ks[25], (DEPTH, D_MODEL, D_FF), BETA * D_MODEL ** -0.5),
        'w_ff3': nrm(ks[26], (DEPTH, D_MODEL, D_FF), BETA * D_MODEL ** -0.5),
        'w_ff2': nrm(ks[27], (DEPTH, D_FF, D_MODEL), BETA * D_FF ** -0.5),
        'ln2_g': 1.0 + nrm(ks[28], (DEPTH, D_MODEL), 0.01),
        'ln2_b': nrm(ks[29], (DEPTH, D_MODEL), 0.01),
    }


def reference(x, c, ctx, c_ctx, ln0_g, ln0_b, w_ada, b_ada, w_in, conv_w, conv_b, dt_bias, a_log, d_skip,
              ssd_norm_g, gm_norm_g, gm_norm_b, w_spatial, b_spatial, b_gate, w_ssd_proj, w_gm_proj, w_out,
              ln1_g, ln1_b, w_ff1, w_ff3, w_ff2, ln2_g, ln2_b):
    x = layer_norm(x, ln0_g, ln0_b)
    ctx_h = layer_norm(ctx, ln0_g, ln0_b)
    rows = x.shape[1] // GRID_W
    n_lat_chunks = rows // ROWS_PER_CHUNK
    n_ctx_chunks = ctx.shape[1] // GM_CHUNK
    zero_state = jnp.zeros((ctx.shape[0], SSD_GROUPS, SSD_HPG, SSD_HEAD_DIM, SSD_STATE), x.dtype)
    for l in range(DEPTH):
        mod_x = (jax.nn.silu(c) @ w_ada[l] + b_ada[l])[:, None, :]
        mod_c = jax.nn.silu(c_ctx) @ w_ada[l] + b_ada[l]
        sh1x, sc1x, g1x, sh2x, sc2x, g2x = jnp.split(mod_x, 6, axis=-1)
        sh1c, sc1c, g1c, sh2c, sc2c, g2c = jnp.split(mod_c, 6, axis=-1)
        ssd_p = (conv_w[l], conv_b[l], dt_bias[l], a_log[l], d_skip[l], ssd_norm_g[l])
        mrg_p = (gm_norm_g[l], gm_norm_b[l], w_spatial[l], b_spatial[l], b_gate[l],
                 w_ssd_proj[l], w_gm_proj[l], w_out[l])
        zc, xbcc, dtc, uc, vc, gc = split_proj((ctx_h * (1.0 + sc1c) + sh1c) @ w_in[l])
        yc, s_f, s_b = ssd_branch(zc, xbcc, dtc, *ssd_p, zero_state, zero_state)
        zx, xbcx, dtx, ux, vx, gx = split_proj((x * (1.0 + sc1x) + sh1x) @ w_in[l])
        yx, _, _ = ssd_branch(zx, xbcx, dtx, *ssd_p, s_f, s_b)
        out_x = branch_merge(yx, ux, vx, gx, n_lat_chunks, *mrg_p)
        x = layer_norm(ALPHA * x + g1x * out_x, ln1_g[l], ln1_b[l])
        x = layer_norm(ALPHA * x + g2x * swiglu(x * (1.0 + sc2x) + sh2x, w_ff1[l], w_ff3[l], w_ff2[l]),
                       ln2_g[l], ln2_b[l])
        if l < DEPTH - 1:
            out_c = branch_merge(yc, uc, vc, gc, n_ctx_chunks, *mrg_p)
            ctx_h = layer_norm(ALPHA * ctx_h + g1c * out_c, ln1_g[l], ln1_b[l])
            ctx_h = layer_norm(ALPHA * ctx_h + g2c * swiglu(ctx_h * (1.0 + sc2c) + sh2c, w_ff1[l], w_ff3[l], w_ff2[l]),
                               ln2_g[l], ln2_b[l])
    return x
```

```python
import os
from contextlib import ExitStack
import numpy as np
import concourse.bass as bass
import concourse.mybir as mybir
from concourse.bass_utils import run_bass_kernel_spmd

F32 = mybir.dt.float32
F32R = mybir.dt.float32r
BF16 = mybir.dt.bfloat16
AF = mybir.ActivationFunctionType
ALU = mybir.AluOpType

D = 1024
SEQ = 4096
CTX = 256
NCH = SEQ // 128
H = 16
P = 64
NST = 128
DFF = 2816
NFF = DFF // 128
XBC = 1536
DPROJ = 6688
O_Z, O_XBC, O_DT, O_U, O_V, O_G = 0, 1024, 2560, 2592, 3616, 4640
ALPHA = 2.0 ** 0.25
EPS = 1e-5
DEBUG = bool(int(os.environ.get("KDEBUG", "0")))
NCH_RUN = int(os.environ.get("KNCH", str(NCH)))
KPASS = int(os.environ.get("KPASS", "3"))
NFILL = int(os.environ.get("KFILL", "3"))


class Buf:
    def __init__(self, t):
        self.t = t
        self.w = None
        self.pw = []
        self.r = {}
        self.psum = False

    def __getitem__(self, idx):
        return self.t[idx]


class Eng:
    def __init__(self, e, sem, name):
        self.e, self.sem, self.name = e, sem, name
        self.count = 0
        self.seen = {}
        self.pend_r, self.pend_w = [], []


class KB:
    def __init__(self):
        self.nc = bass.Bass("TRN2", target_bir_lowering=False)
        nc = self.nc
        self.es = ExitStack()
        self.stacks = [self.es]
        self.engs = {}
        for name, e in (("pe", nc.tensor), ("act", nc.scalar), ("dve", nc.vector), ("pool", nc.gpsimd), ("sp", nc.sync)):
            sem = self.es.enter_context(nc.semaphore("s_" + name))
            self.engs[name] = Eng(e, sem, name)
        self.dpool = {}
        for q, n in (("sp", 20), ("pool", 20), ("act", 6)):
            self.dpool[q] = [[self.es.enter_context(nc.semaphore(f"d_{q}{i}")), 0] for i in range(n)]
        self.dnext = {q: 0 for q in self.dpool}
        self.uid = 0
        self.dbg = []
        self.fill = None

    def sb(self, shape, dt, name=None):
        self.uid += 1
        t = self.stacks[-1].enter_context(self.nc.sbuf_tensor(f"{name or 't'}_{self.uid}", list(shape), dt))
        return Buf(t)

    def ps(self, name):
        self.uid += 1
        t = self.stacks[-1].enter_context(self.nc.psum_tensor(f"{name}_{self.uid}", [128, 512], F32))
        b = Buf(t)
        b.psum = True
        return b

    def dram(self, name, shape, dt, kind="Internal"):
        return Buf(self.nc.dram_tensor(name, list(shape), dt, kind=kind).ap())

    def push(self):
        print("SBUF remaining at push:", self.nc.sbuf_bytes_remaining)
        st = ExitStack()
        self.stacks.append(st)
        return st

    def pop(self):
        self.barrier()
        self.stacks.pop().close()

    def _waits(self, E, r, w, partial=False):
        deps = {}
        raw = set()
        for b in r:
            if b.w is not None:
                deps[(b.w[0], b.w[1])] = b.w
                raw.add((b.w[0], b.w[1]))
            for t in b.pw:
                deps[(t[0], t[1])] = t
            if b.psum:
                for t in b.r.values():
                    if t[2] is not E:
                        deps[(t[0], t[1])] = t
        for b in w:
            if b.w is not None:
                deps[(b.w[0], b.w[1])] = b.w
            if not partial:
                for t in b.pw:
                    deps[(t[0], t[1])] = t
            for t in b.r.values():
                deps[(t[0], t[1])] = t
        for key, (sem, val, owner) in deps.items():
            if owner is E:
                if E.name == "pe":
                    continue
                if key not in raw:
                    continue
            if E.seen.get(sem, 0) >= val:
                continue
            if E.name == "pe" and self.fill is not None and owner is not None:
                for v in range(max(E.seen.get(sem, 0) + 1, val - NFILL), val):
                    E.e.wait_ge(sem, v)
                    self.nc.tensor.matmul(self.fill[0], self.fill[1], self.fill[2], start=True, stop=True)
            E.e.wait_ge(sem, val)
            E.seen[sem] = val

    def op(self, eng, fn, r=(), w=(), inc=True):
        E = self.engs[eng]
        self._waits(E, r, w)
        ins = fn()
        E.pend_r.extend(r)
        E.pend_w.extend(w)
        if inc:
            E.count += 1
            ins.then_inc(E.sem, 1)
            tok = (E.sem, E.count, E)
            for b in E.pend_r:
                b.r[E.sem] = tok
            for b in E.pend_w:
                b.w = tok
                b.pw = []
                b.r = {}
            E.pend_r, E.pend_w = [], []
        return ins

    def dma(self, q, out, in_, r=(), w=(), partial=False, **kw):
        E = self.engs[q]
        self._waits(E, r, w, partial)
        pool = self.dpool[q]
        i = self.dnext[q]
        self.dnext[q] = (i + 1) % len(pool)
        sem, val = pool[i]
        if val > 0 and E.seen.get(sem, 0) < val:
            E.e.wait_ge(sem, val)
            E.seen[sem] = val
        ins = E.e.dma_start(out=out, in_=in_, **kw)
        val += 16
        pool[i][1] = val
        ins.then_inc(sem, 16)
        tok = (sem, val, None)
        for b in r:
            b.r[sem] = tok
        for b in w:
            if partial:
                b.pw.append(tok)
            else:
                b.w = tok
                b.pw = []
            b.r = {}

    def barrier(self):
        toks = [(E.sem, E.count) for E in self.engs.values() if E.count > 0]
        for pool in self.dpool.values():
            toks += [(s, v) for s, v in pool if v > 0]
        for E in self.engs.values():
            assert not E.pend_r and not E.pend_w, E.name
            for sem, val in toks:
                if sem is E.sem or E.seen.get(sem, 0) >= val:
                    continue
                E.e.wait_ge(sem, val)
                E.seen[sem] = val

    def act(self, out, in_, func, r, w, bias=None, scale=None, accum=None):
        kw = {}
        if bias is not None:
            kw["bias"] = bias
        if scale is not None:
            kw["scale"] = scale
        if accum is not None:
            kw["accum_out"] = accum
        return self.op("act", lambda: self.nc.scalar.activation(out=out, in_=in_, func=func, **kw), r, w)

    def tt(self, eng, out, in0, in1, op, r, w):
        e = self.engs[eng].e
        return self.op(eng, lambda: e.tensor_tensor(out=out, in0=in0, in1=in1, op=op), r, w)

    def ts(self, eng, out, in0, s1, s2, op0, op1, r, w):
        e = self.engs[eng].e
        if s2 is None:
            return self.op(eng, lambda: e.tensor_scalar(out=out, in0=in0, scalar1=s1, scalar2=None, op0=op0), r, w)
        return self.op(eng, lambda: e.tensor_scalar(out=out, in0=in0, scalar1=s1, scalar2=s2, op0=op0, op1=op1), r, w)

    def stt(self, out, in0, scalar, in1, op0, op1, r, w):
        return self.op("dve", lambda: self.nc.vector.scalar_tensor_tensor(
            out=out, in0=in0, scalar=scalar, in1=in1, op0=op0, op1=op1), r, w)

    def copy(self, eng, out, in_, r, w):
        e = self.engs[eng].e
        if eng == "act":
            return self.act(out, in_, AF.Copy, r, w)
        return self.op(eng, lambda: e.tensor_copy(out=out, in_=in_), r, w)

    def memset(self, eng, ap, val, w):
        e = self.engs[eng].e
        return self.op(eng, lambda: e.memset(ap, val), (), w)

    def mm(self, out, lhsT, rhs, start, stop, r, w, inc):
        return self.op("pe", lambda: self.nc.tensor.matmul(out, lhsT, rhs, start=start, stop=stop), r, w, inc)

    def tr(self, out, in_, ident, r, w, inc):
        return self.op("pe", lambda: self.nc.tensor.transpose(out, in_, ident), r, w, inc)

    def dump(self, name, buf, ap, shape, dt=F32):
        if not DEBUG:
            return
        d = self.dram("dbg_" + name, shape, dt, kind="ExternalOutput")
        self.dma("sp", d.t, ap, r=[buf], w=[d])
        self.dbg.append("dbg_" + name)


def bc(ap, shape):
    return ap.to_broadcast(list(shape))


def build():
    k = KB()
    nc = k.nc

    def din(name, shape, dt=F32):
        return k.dram(name, shape, dt, kind="ExternalInput")

    x_d = din("x", [SEQ, D])
    ctx_d = din("ctx", [CTX, D])
    cT_d = din("cT", [128, 8, 2])
    cm_d = din("cm", [128, 5, 128])
    wada_d = din("w_ada", [D, 6 * D])
    badac_d = din("b_ada_c", [128, 48])
    colp_d = din("colp", [128, 4, 8])
    convw_d = din("convw_c", [128, 12, 5])
    convb_d = din("convb_c", [128, 12])
    win_d = din("w_in", [D, DPROJ])
    rows_d = din("rows", [9, D])
    bgate_d = din("b_gate", [1, 2 * D])
    small_d = din("small", [1, 96])
    wspT_d = din("wspT", [128, 8, 128])
    wsp_d = din("wsp", [128, 8, 128])
    bsp_d = din("bsp_c", [128, 8])
    wssd_d = din("w_ssd", [D, D])
    wgm_d = din("w_gm", [D, D])
    wout_d = din("w_out", [D, D])
    wff1_d = din("w_ff1", [D, DFF])
    wff3_d = din("w_ff3", [D, DFF])
    wff2_d = din("w_ff2", [DFF, D])
    out_d = k.dram("out", [SEQ, D], F32, kind="ExternalOutput")

    xs_s = [k.dram(f"xs_s{c}", [128, D], BF16) for c in range(NCH)]
    bt_s = [k.dram(f"bt_s{c}", [128, 256], BF16) for c in range(NCH)]
    bc_s = [k.dram(f"bc_s{c}", [128, 4, 128], BF16) for c in range(NCH)]
    hp_s = [k.dram(f"hp_s{c}", [128, D], BF16) for c in range(NCH)]
    g2_s = [k.dram(f"g2_s{c}", [128, D], BF16) for c in range(NCH)]
    x1_s = [k.dram(f"x1_s{c}", [128, D], F32) for c in range(NCH)]

    cm = k.sb([128, 5, 128], F32, "cm")
    k.dma("sp", cm.t[:], cm_d.t, w=[cm])
    IDN, MLE, MGE, MGT, MLT = range(5)
    ident_bf = k.sb([128, 128], BF16, "identbf")
    k.copy("dve", ident_bf.t[:], cm.t[:, IDN, :], [cm], [ident_bf])
    mgt_r = k.sb([128, 128], F32R, "mgtr")
    mlt_r = k.sb([128, 128], F32R, "mltr")
    k.copy("dve", mgt_r.t[:], cm.t[:, MGT, :], [cm], [mgt_r])
    k.copy("dve", mlt_r.t[:], cm.t[:, MLT, :], [cm], [mlt_r])
    ones_f = k.sb([128, 128], F32, "onesf")
    k.memset("dve", ones_f.t[:], 1.0, [ones_f])
    ones_bf = k.sb([2, 128], BF16, "onesbf")
    k.memset("dve", ones_bf.t[:], 1.0, [ones_bf])

    colp = k.sb([128, 4, 8], F32, "colp")
    k.dma("sp", colp.t[:], colp_d.t, w=[colp])
    convw = k.sb([128, 12, 5], F32, "convw")
    k.dma("sp", convw.t[:], convw_d.t, w=[convw])
    convb = k.sb([128, 12], F32, "convb")
    k.dma("sp", convb.t[:], convb_d.t, w=[convb])
    smallr = k.sb([128, 96], F32, "smallr")
    k.dma("sp", smallr.t[:], small_d.t.partition_broadcast(128), w=[smallr])
    arow = k.sb([128, 32], F32, "arow")
    k.act(arow.t[:], smallr.t[:, 32:64], AF.Exp, [smallr], [arow])
    k.ts("dve", arow.t[:], arow.t[:], -1.0, None, ALU.mult, None, [arow], [arow])
    dsum = k.sb([128, 16], F32, "dsum")
    k.tt("dve", dsum.t[:], smallr.t[:, 64:80], smallr.t[:, 80:96], ALU.add, [smallr], [dsum])
    dh = k.sb([128, H, 128], BF16, "dh")
    k.tt("dve", dh.t[:], bc(cm.t[:, IDN, :].unsqueeze(1), [128, H, 128]),
         bc(dsum.t[:].unsqueeze(2), [128, H, 128]), ALU.mult, [cm, dsum], [dh])
    dt_all = k.sb([128, NCH, 32], F32, "dtall")
    run_f = k.sb([128, D], F32, "runf")
    run_b = k.sb([128, D], F32, "runb")
    modc = k.sb([128, 48, 2], F32, "modc")
    cols = k.sb([128, 6, 8], F32, "cols")
    S1X, B1X, S1C, B1C, S2, B2 = range(6)
    lntmp = k.sb([128, 1], F32, "lntmp")
    epsc = k.sb([128, 1], F32, "epsc")
    k.memset("dve", epsc.t[:], EPS, [epsc])

    pss = [k.ps(f"ps{i}") for i in range(8)]
    fill_spec = (pss[7].t[:, 0:256], ident_bf.t[:], dh.t[:, 0:2, :].rearrange("p a i -> p (a i)"))

    st = k.push()
    w1 = k.sb([128, 8, 1568 + 2048 + 1024], BF16, "w1")
    W_XBC, W_DT, W_U, W_V, W_G2 = 0, 1536, 1568, 2592, 3616
    for kk in range(8):
        rs_ = slice(kk * 128, (kk + 1) * 128)
        k.dma("pool", w1.t[:, kk, 0:1568], win_d.t[rs_, O_XBC:O_XBC + 1568], r=[win_d], w=[w1], partial=True)
    for kk in range(8):
        rs_ = slice(kk * 128, (kk + 1) * 128)
        k.dma("pool", w1.t[:, kk, 1568:3616], win_d.t[rs_, O_U:O_U + 2048], r=[win_d], w=[w1], partial=True)
        k.dma("pool", w1.t[:, kk, 3616:4640], win_d.t[rs_, O_G + D:O_G + 2 * D], r=[win_d], w=[w1], partial=True)
    wgm = k.sb([128, 8, D], BF16, "wgm")
    for kk in range(8):
        k.dma("pool", wgm.t[:, kk, :], wgm_d.t[kk * 128:(kk + 1) * 128, :], w=[wgm], partial=True)
    wspT = k.sb([128, 8, 128], BF16, "wspT")
    k.dma("pool", wspT.t[:], wspT_d.t, w=[wspT])
    st = k.push()
    cT = k.sb([128, 8, 2], F32, "cT")
    k.dma("sp", cT.t[:], cT_d.t, w=[cT])
    scT = k.sb([128, 8, 2], F32, "scT")
    k.act(scT.t[:], cT.t[:], AF.Silu, [cT], [scT])
    badac = k.sb([128, 48], F32, "badac")
    k.dma("sp", badac.t[:], badac_d.t, w=[badac])
    wst = [k.sb([128, 8, 512], F32, f"wst{i}") for i in range(4)]
    wada_v = wada_d.t.rearrange("(k p) n -> p k n", p=128)
    modrow = k.sb([2, 6 * D], F32, "modrow")
    for cb in range(12):
        wb = wst[cb % 4]
        k.dma("sp", wb.t[:], wada_v[:, :, cb * 512:(cb + 1) * 512], r=[wada_d], w=[wb])
        pb = pss[cb % 2]
        for kk in range(8):
            k.mm(pb.t[0:2, :], scT.t[:, kk, :], wb.t[:, kk, :], kk == 0, kk == 7, [scT, wb], [pb], inc=(kk == 7))
        k.copy("act", modrow.t[:, cb * 512:(cb + 1) * 512], pb.t[0:2, :], [pb], [modrow])
    pm = pss[2]
    for j in range(48):
        k.mm(pm.t[:, 2 * j:2 * j + 2], modrow.t[0:2, j * 128:(j + 1) * 128], cm.t[0:2, IDN, 0:2], True, True,
             [modrow, cm], [pm], inc=(j == 47))
    k.tt("dve", modc.t[:], pm.t[:, 0:96].rearrange("p (j m) -> p j m", m=2),
         bc(badac.t[:].unsqueeze(2), [128, 48, 2]), ALU.add, [pm, badac], [modc])
    tmp8 = k.sb([128, 8], F32, "tmp8")
    for (si, bi, m, g_i, b_i, sc_o, sh_o) in ((S1X, B1X, 0, 0, 1, 8, 0), (S1C, B1C, 1, 0, 1, 8, 0), (S2, B2, 0, 2, 3, 32, 24)):
        k.ts("dve", tmp8.t[:], modc.t[:, sc_o:sc_o + 8, m], 1.0, None, ALU.add, None, [modc], [tmp8])
        k.tt("dve", cols.t[:, si, :], tmp8.t[:], colp.t[:, g_i, :], ALU.mult, [tmp8, colp], [cols])
        k.tt("dve", cols.t[:, bi, :], tmp8.t[:], colp.t[:, b_i, :], ALU.mult, [tmp8, colp], [cols])
        k.tt("dve", cols.t[:, bi, :], cols.t[:, bi, :], modc.t[:, sh_o:sh_o + 8, m], ALU.add, [cols, modc], [cols])
    k.pop()
    k.dump("modc", modc, modc.t[:].rearrange("p j m -> p (j m)"), [128, 96])

    def make_row_psum(off, name, banks):
        dg = k.sb([128, 128], F32, name + "dg")
        for c in range(8):
            k.ts("dve", dg.t[:], cm.t[:, IDN, :], modc.t[:, off + c, 0:1], None, ALU.mult, None, [cm, modc], [dg])
            pb = banks[c // 4]
            k.mm(pb.t[:, (c % 4) * 128:(c % 4 + 1) * 128], ones_f.t[:], dg.t[:], True, True, [ones_f, dg], [pb], True)

    def make_row(off, name):
        row = k.sb([128, D], F32, name)
        dg = k.sb([128, 128], F32, name + "dg")
        for c in range(8):
            k.ts("dve", dg.t[:], cm.t[:, IDN, :], modc.t[:, off + c, 0:1], None, ALU.mult, None, [cm, modc], [dg])
            pb = pss[1 + (c // 4)]
            k.mm(pb.t[:, (c % 4) * 128:(c % 4 + 1) * 128], ones_f.t[:], dg.t[:], True, True, [ones_f, dg], [pb], True)
        k.copy("act", row.t[:, 0:512], pss[1].t[:], [pss[1]], [row])
        k.copy("act", row.t[:, 512:1024], pss[2].t[:], [pss[2]], [row])
        return row

    def load_w(dst, src_ap, k_chunks, eng="pool"):
        for kk in range(k_chunks):
            k.dma(eng, dst.t[:, kk, :], src_ap[kk * 128:(kk + 1) * 128, :], w=[dst], partial=True)

    def bias_rows(src_row_ap, n, scale, name):
        rows = k.sb([2, n], BF16, name)
        k.push()
        b32 = k.sb([1, n], F32, name + "32")
        k.dma("sp", b32.t[:], src_row_ap, w=[b32])
        if scale != 1.0:
            k.ts("dve", b32.t[:], b32.t[:], float(scale), None, ALU.mult, None, [b32], [b32])
        k.copy("dve", rows.t[0:1, :], b32.t[:], [b32], [rows])
        lo32 = k.sb([1, n], F32, name + "lo32")
        k.tt("dve", lo32.t[:], b32.t[:], rows.t[0:1, :], ALU.subtract, [b32, rows], [lo32])
        lo = k.sb([1, n], BF16, name + "lo")
        k.copy("dve", lo.t[:], lo32.t[:], [lo32], [lo])
        k.dma("sp", rows.t[1:2, :], lo.t[:], r=[lo], w=[rows])
        k.pop()
        return rows

    def rsqrt_act(out_buf, in_ap, in_buf, scale):
        k.act(lntmp.t[:], in_ap, AF.Ln, [in_buf, epsc], [lntmp], bias=epsc.t[:], scale=float(scale))
        k.act(out_buf.t[:], lntmp.t[:], AF.Exp, [lntmp], [out_buf], scale=-0.5)

    def ln_stats(src, tmp_st, tmp_mv, rs, nm):
        for i in range(2):
            k.op("dve", lambda i=i: nc.vector.bn_stats(out=tmp_st.t[:, i, :], in_=src.t[:, i * 512:(i + 1) * 512]), [src], [tmp_st])
        k.op("dve", lambda: nc.vector.bn_aggr(out=tmp_mv.t[:], in_=tmp_st.t[:].rearrange("p a b -> p (a b)")), [tmp_st], [tmp_mv])
        rsqrt_act(rs, tmp_mv.t[:, 1:2], tmp_mv, 1.0)
        k.ts("dve", nm.t[:], tmp_mv.t[:, 0:1], rs.t[:], -1.0, ALU.mult, ALU.mult, [tmp_mv, rs], [nm])

    def proj_tok(hT, W, c0, n, pb, brow=None, b0=0):
        for kk in range(8):
            last = (kk == 7 and brow is None)
            k.mm(pb.t[:, 0:n], hT.t[:, kk, :], W.t[:, kk, c0:c0 + n], kk == 0, last, [hT, W], [pb], inc=last)
        if brow is not None:
            k.mm(pb.t[:, 0:n], ones_bf.t[0:2, :], brow.t[0:2, b0:b0 + n], False, True, [ones_bf, brow], [pb], inc=True)

    def transpose_to(dstT, src, pb, eng):
        pv = pb.t[:].bitcast(BF16)
        for c in range(8):
            k.tr(pv[:, c * 128:(c + 1) * 128], src.t[:, c * 128:(c + 1) * 128], ident_bf.t[:], [src, ident_bf], [pb], inc=(c == 7))
        k.copy(eng, dstT.t[:].rearrange("p a t -> p (a t)"), pv, [pb], [dstT])

    class LN0:
        def __init__(self, nh=2, all_dve=False):
            self.nh = nh
            self.all_dve = all_dve
            self.xt = [k.sb([128, D], F32, "xt") for _ in range(2)]
            self.xh = [k.sb([128, D], F32, "xh") for _ in range(nh)]
            self.hT = [k.sb([128, 8, 128], BF16, "hT") for _ in range(2)]
            self.st = k.sb([128, 2, 6], F32, "lnst")
            self.mv = k.sb([128, 2], F32, "lnmv")
            self.rs = k.sb([128, 1], F32, "lnrs")
            self.nm = k.sb([128, 1], F32, "lnnm")
            self.i = 0

        def run(self, src_buf, src_ap, si, bi, pbanks):
            s = self.i % 2
            self.i += 1
            xt, xh, hT = self.xt[s], self.xh[s % self.nh], self.hT[s]
            k.dma("sp", xt.t[:], src_ap, r=[src_buf], w=[xt])
            ln_stats(xt, self.st, self.mv, self.rs, self.nm)
            k.act(xh.t[:], xt.t[:], AF.Identity, [xt, self.rs, self.nm], [xh], bias=self.nm.t[:], scale=self.rs.t[:])
            for c in range(8):
                pb = pbanks[c // 4]
                k.tr(pb.t[:, (c % 4) * 128:(c % 4 + 1) * 128], xh.t[:, c * 128:(c + 1) * 128], cm.t[:, IDN, :],
                     [xh, cm], [pb], inc=(c % 4 == 3))
            for c in range(8):
                pb = pbanks[c // 4]
                src = pb.t[:, (c % 4) * 128:(c % 4 + 1) * 128]
                if c < 4 or self.all_dve:
                    k.ts("dve", hT.t[:, c, :], src, cols.t[:, si, c:c + 1], cols.t[:, bi, c:c + 1], ALU.mult, ALU.add, [pb, cols], [hT])
                else:
                    k.act(hT.t[:, c, :], src, AF.Identity, [pb, cols], [hT], bias=cols.t[:, bi, c:c + 1], scale=cols.t[:, si, c:c + 1])
            return xh, hT

    gmgrow = k.sb([128, D], F32, "gmgrow")
    k.dma("sp", gmgrow.t[:], rows_d.t[3:4, :].partition_broadcast(128), w=[gmgrow])
    biasM = k.sb([128, 8, 128], F32, "biasM")
    k.dma("sp", biasM.t[:].rearrange("p g c -> p (g c)"), rows_d.t[4:5, :].partition_broadcast(128), w=[biasM])
    gv = k.sb([128, D], F32, "gv")
    wsp_v = gv.t[:].rearrange("p (g q) -> p g q", g=8)
    k.dma("sp", wsp_v, wsp_d.t, w=[gv])
    rsum = k.sb([128, 8], F32, "rsum")
    k.op("dve", lambda: nc.vector.reduce_sum(out=rsum.t[:], in_=wsp_v, axis=mybir.AxisListType.X), [gv], [rsum])
    bspc = k.sb([128, 8], F32, "bspc")
    k.dma("sp", bspc.t[:], bsp_d.t, w=[bspc])
    k.tt("dve", biasM.t[:], biasM.t[:], bc(rsum.t[:].unsqueeze(2), [128, 8, 128]), ALU.mult, [biasM, rsum], [biasM])
    k.tt("dve", biasM.t[:], biasM.t[:], bc(bspc.t[:].unsqueeze(2), [128, 8, 128]), ALU.add, [biasM, bspc], [biasM])
    dtb_rows = bias_rows(small_d.t[0:1, 0:32], 32, 1.0, "dtb")
    bg2_rows = bias_rows(bgate_d.t[0:1, D:2 * D], D, 1.0, "bg2")

    if NFILL > 0:
        k.fill = fill_spec
    ln0 = LN0(2, all_dve=True)
    pre = [k.sb([128, 12, 132], BF16, f"pre{i}") for i in range(3)]
    dgw = k.sb([128, 12, 5, 128], BF16, "dgw")
    for cc in range(12):
        for tap in range(5):
            k.ts("pool" if (cc + tap) % 2 else "dve", dgw.t[:, cc, tap, :], cm.t[:, IDN, :], convw.t[:, cc, tap:tap + 1], None,
                 ALU.mult, None, [cm, convw], [dgw])
    xbcT = k.sb([128, 12, 128], BF16, "xbcT")
    xs_tok = k.sb([128, D], BF16, "xs_tok")
    b_tok = k.sb([128, 256], BF16, "b_tok")
    a_sb = k.sb([128, 32], F32, "a_sb")
    ex = k.sb([128, 96], F32, "ex")
    ddv = k.sb([128, 32], F32, "ddv")
    xdd = k.sb([128, D], BF16, "xdd")
    hp_bf = k.sb([128, D], BF16, "hp_bf")
    e_t = k.sb([128, 32], F32, "e_t")
    decf1 = k.sb([128, 16], F32, "decf1")
    gu = k.sb([128, D], F32, "gu")
    sf1 = gu
    vhat = k.sb([128, D], BF16, "vhat")
    g2 = k.sb([128, D], F32, "g2")
    ygm = gv
    ygm_bf = k.sb([128, D], BF16, "ygm_bf")
    ygmT = k.sb([128, 8, 128], BF16, "ygmT")
    G2 = k.sb([128, D], BF16, "G2")
    vst = k.sb([128, 2, 6], F32, "vst")
    vmv = k.sb([128, 2], F32, "vmv")
    vrs = k.sb([128, 1], F32, "vrs")
    vnm = k.sb([128, 1], F32, "vnm")

    k.memset("dve", run_f.t[:], 0.0, [run_f])
    k.memset("dve", run_b.t[:], 0.0, [run_b])
    print("SBUF free in pass 1:", nc.sbuf_bytes_remaining)

    def small_exps(dt_ap, dt_buf):
        k.tt("dve", a_sb.t[:], dt_ap, arow.t[:], ALU.mult, [dt_buf, arow], [a_sb])
        pb = pss[4]
        specs = ((MLE, 0), (MGE, 16), (MGT, 0), (MLT, 16))
        for i, (mi, ao) in enumerate(specs):
            k.mm(pb.t[:, i * 16:(i + 1) * 16], cm.t[:, mi, :], a_sb.t[:, ao:ao + 16], True, True, [cm, a_sb], [pb], inc=False)
        k.mm(pb.t[:, 64:96], ones_f.t[:], a_sb.t[:, 0:32], True, True, [ones_f, a_sb], [pb], inc=True)
        k.act(ex.t[:], pb.t[:, 0:96], AF.Exp, [pb], [ex])

    order1 = [("c", 1), ("c", 0)] + [("x", cc_) for cc_ in range(NCH_RUN - 1, -1, -1)]
    lncache = {}

    def ln_for(i):
        if i >= len(order1) or i in lncache:
            return
        kind, cc_ = order1[i]
        if kind == "c":
            lncache[i] = ln0.run(ctx_d, ctx_d.t[cc_ * 128:(cc_ + 1) * 128, :], S1C, B1C, (pss[0], pss[1]))
        else:
            lncache[i] = ln0.run(x_d, x_d.t[cc_ * 128:(cc_ + 1) * 128, :], S1X, B1X, (pss[0], pss[1]))

    def s1(seq_buf, seq_ap, c, n, si, bi, slot_of, is_ctx):
        oi = order1.index(("c" if is_ctx else "x", c))
        ln_for(oi)
        xh, hT = lncache.pop(oi)
        for cc in range(12):
            pb = pss[2 + cc // 4]
            for kk in range(8):
                k.mm(pb.t[:, (cc % 4) * 128:(cc % 4 + 1) * 128], w1.t[:, kk, W_XBC + cc * 128:W_XBC + (cc + 1) * 128],
                     hT.t[:, kk, :], kk == 0, kk == 7, [w1, hT], [pb], inc=(kk == 7))
        me = pre[slot_of(c)]
        for q in range(3):
            pv = pss[2 + q].t[:].rearrange("p (a t) -> p a t", a=4)
            k.copy("dve", me.t[:, q * 4:(q + 1) * 4, 2:130], pv, [pss[2 + q]], [me])
            KH = os.environ.get("KHALO", "")
            if c + 1 < n and KH != "skipL":
                k.copy("dve", pre[slot_of(c + 1)].t[:, q * 4:(q + 1) * 4, 0:2], pv[:, :, 126:128], [pss[2 + q]], [pre[slot_of(c + 1)]])
            if c - 1 >= 0 and KH != "skipR":
                k.copy("dve", pre[slot_of(c - 1)].t[:, q * 4:(q + 1) * 4, 130:132], pv[:, :, 0:2], [pss[2 + q]], [pre[slot_of(c - 1)]])
        if c == n - 1:
            k.memset("dve", me.t[:, :, 130:132], 0.0, [me])
        if c == 0:
            k.memset("dve", me.t[:, :, 0:2], 0.0, [me])
        yield "halo"
        pb = pss[5]
        proj_tok(hT, w1, W_DT, 32, pb, dtb_rows, 0)
        k.act(e_t.t[:], pb.t[:, 0:32], AF.Exp, [pb], [e_t])
        dts = dtc_all if is_ctx else dt_all
        k.act(dts.t[:, c, :], e_t.t[:], AF.Ln, [e_t], [dts], bias=1.0)
        if is_ctx or os.environ.get("KSKIP_GM"):
            ln_for(oi + 1)
            return
        yield
        for half in range(2):
            pb = pss[5 + half]
            proj_tok(hT, w1, W_U + half * 512, 512, pb)
            k.act(gu.t[:, half * 512:(half + 1) * 512], pb.t[:], AF.Gelu_apprx_tanh, [pb], [gu])
        yield
        for half in range(2):
            pb = pss[5 + half]
            proj_tok(hT, w1, W_V + half * 512, 512, pb)
            k.act(gv.t[:, half * 512:(half + 1) * 512], pb.t[:], AF.Gelu_apprx_tanh, [pb], [gv])
        yield
        ln_stats(gv, vst, vmv, vrs, vnm)
        k.act(vhat.t[:], gv.t[:], AF.Identity, [gv, vrs, vnm], [vhat], bias=vnm.t[:], scale=vrs.t[:])
        for half in range(2):
            pb = pss[5 + half]
            proj_tok(hT, w1, W_G2 + half * 512, 512, pb, bg2_rows, half * 512)
            k.act(g2.t[:, half * 512:(half + 1) * 512], pb.t[:], AF.Sigmoid, [pb], [g2])
        yield
        ln_for(oi + 1)
        yield
        for g in range(8):
            pb = pss[5 + g // 4]
            k.mm(pb.t[:, (g % 4) * 128:(g % 4 + 1) * 128], wspT.t[:, g, :], vhat.t[:, g * 128:(g + 1) * 128],
                 True, True, [wspT, vhat], [pb], inc=(g % 4 == 3))
        bM = biasM.t[:].rearrange("p g c -> p (g c)")
        for half in range(2):
            sl = slice(half * 512, (half + 1) * 512)
            pb = pss[5 + half]
            k.tt("dve", ygm.t[:, sl], pb.t[:], gmgrow.t[:, sl], ALU.mult, [pb, gmgrow], [ygm])
            k.tt("pool", ygm.t[:, sl], ygm.t[:, sl], bM[:, sl], ALU.add, [ygm, biasM], [ygm])
            k.tt("pool", ygm_bf.t[:, sl], ygm.t[:, sl], gu.t[:, sl], ALU.mult, [ygm, gu], [ygm_bf])
        yield
        transpose_to(ygmT, ygm_bf, pss[5], "dve")
        for half in range(2):
            pb = pss[5 + half]
            for kk in range(8):
                k.mm(pb.t[:], ygmT.t[:, kk, :], wgm.t[:, kk, half * 512:(half + 1) * 512], kk == 0, kk == 7,
                     [ygmT, wgm], [pb], inc=(kk == 7))
            k.tt("dve", G2.t[:, half * 512:(half + 1) * 512], pb.t[:], g2.t[:, half * 512:(half + 1) * 512], ALU.mult, [pb, g2], [G2])
        k.dma("pool", g2_s[c].t, G2.t[:], r=[G2], w=[g2_s[c]])

    def post(c, n, slot_of, is_ctx):
        if os.environ.get("KSKIP_POST") and not is_ctx:
            return
        me = pre[slot_of(c)]
        for cc in range(12):
            pb = pss[2 + cc // 4]
            for tap in range(5):
                k.mm(pb.t[:, (cc % 4) * 128:(cc % 4 + 1) * 128], dgw.t[:, cc, tap, :], me.t[:, cc, tap:tap + 128],
                     tap == 0, tap == 4, [dgw, me], [pb], inc=(tap == 4 and cc % 4 == 3))
        for cc in range(12):
            pb = pss[2 + cc // 4]
            k.act(xbcT.t[:, cc, :], pb.t[:, (cc % 4) * 128:(cc % 4 + 1) * 128], AF.Silu, [pb, convb], [xbcT], bias=convb.t[:, cc:cc + 1])
        yield

        pv0 = pss[0].t[:].bitcast(BF16)
        pv1 = pss[1].t[:].bitcast(BF16)
        for cix in range(8):
            k.tr(pv0[:, cix * 128:(cix + 1) * 128], xbcT.t[:, cix, :], ident_bf.t[:], [xbcT, ident_bf], [pss[0]], inc=(cix == 7))
        for cix in range(2):
            k.tr(pv1[:, cix * 128:(cix + 1) * 128], xbcT.t[:, 8 + cix, :], ident_bf.t[:], [xbcT, ident_bf], [pss[1]], inc=(cix == 1))
        k.copy("dve", xs_tok.t[:], pv0, [pss[0]], [xs_tok])
        k.copy("act", b_tok.t[:], pv1[:, 0:256], [pss[1]], [b_tok])
        if not is_ctx:
            k.dma("pool", xs_s[c].t, xs_tok.t[:], r=[xs_tok], w=[xs_s[c]])
            k.dma("pool", bt_s[c].t, b_tok.t[:], r=[b_tok], w=[bt_s[c]])
            k.dma("pool", bc_s[c].t, xbcT.t[:, 8:12, :], r=[xbcT], w=[bc_s[c]])
        yield
        dts = dtc_all if is_ctx else dt_all
        small_exps(dts.t[:, c, :], dts)
        k.tt("dve", ddv.t[:, 16:32], ex.t[:, 48:64], dts.t[:, c, 16:32], ALU.mult, [ex, dts], [ddv])
        if is_ctx:
            k.tt("dve", ddv.t[:, 0:16], ex.t[:, 32:48], dts.t[:, c, 0:16], ALU.mult, [ex, dts], [ddv])
        dirs = ((1, run_b),) + (((0, run_f),) if is_ctx else ())
        for d, run in dirs:
            yield
            k.tt("dve", xdd.t[:].rearrange("p (h q) -> p h q", h=H), xs_tok.t[:].rearrange("p (h q) -> p h q", h=H),
                 bc(ddv.t[:, d * 16:(d + 1) * 16].unsqueeze(2), [128, H, P]), ALU.mult, [xs_tok, ddv], [xdd])
            for g in range(2):
                pb = pss[2 + g]
                k.mm(pb.t[:], b_tok.t[:, g * 128:(g + 1) * 128], xdd.t[:, g * 512:(g + 1) * 512], True, True,
                     [b_tok, xdd], [pb], inc=True)
            dec = ex.t[:, 64 + d * 16:80 + d * 16]
            if is_ctx and d == 0:
                if c == 1:
                    for g in range(2):
                        k.copy("act", sf1.t[:, g * 512:(g + 1) * 512], pss[2 + g].t[:], [pss[2 + g]], [sf1])
                    k.copy("dve", decf1.t[:], dec, [ex], [decf1])
                else:
                    for g in range(2):
                        sl = slice(g * 512, (g + 1) * 512)
                        k.tt("dve", run_f.t[:, sl].rearrange("p (h q) -> p h q", h=8), pss[2 + g].t[:].rearrange("p (h q) -> p h q", h=8),
                             bc(decf1.t[:, g * 8:(g + 1) * 8].unsqueeze(2), [128, 8, P]), ALU.mult, [pss[2 + g], decf1], [run_f])
                        k.tt("dve", run_f.t[:, sl], run_f.t[:, sl], sf1.t[:, sl], ALU.add, [run_f, sf1], [run_f])
                continue
            if not is_ctx:
                k.copy("dve", hp_bf.t[:], run.t[:], [run], [hp_bf])
                k.dma("pool", hp_s[c].t, hp_bf.t[:], r=[hp_bf], w=[hp_s[c]])
            k.tt("pool", run.t[:].rearrange("p (h q) -> p h q", h=H), run.t[:].rearrange("p (h q) -> p h q", h=H),
                 bc(dec.unsqueeze(2), [128, H, P]), ALU.mult, [run, ex], [run])
            for g in range(2):
                sl = slice(g * 512, (g + 1) * 512)
                k.tt("dve", run.t[:, sl], run.t[:, sl], pss[2 + g].t[:], ALU.add, [run, pss[2 + g]], [run])

    def run_all(g):
        for _ in g:
            pass

    def interleave(ga, gb):
        da = db = False
        while not (da and db):
            if not da:
                try:
                    next(ga)
                except StopIteration:
                    da = True
            if not db:
                try:
                    next(gb)
                except StopIteration:
                    db = True

    dtc_all = k.sb([128, 2, 32], F32, "dtcall")
    for c in (1, 0):
        run_all(s1(ctx_d, ctx_d.t, c, 2, S1C, B1C, lambda cc: cc % 3, True))
        if c + 1 < 2:
            run_all(post(c + 1, 2, lambda cc: cc % 3, True))
    run_all(post(0, 2, lambda cc: cc % 3, True))
    k.dump("s_f", run_f, run_f.t[:], [128, D])
    k.dump("s_b", run_b, run_b.t[:], [128, D])
    NR = NCH_RUN
    KSTOP = int(os.environ.get("KSTOP", "1000"))
    steps = 0
    for c in range(NR - 1, -1, -1):
        if steps >= KSTOP:
            break
        ga = s1(x_d, x_d.t, c, NR, S1X, B1X, lambda cc: cc % 3, False)
        next(ga)
        steps += 1
        if c + 1 < NR:
            interleave(ga, post(c + 1, NR, lambda cc: cc % 3, False))
            steps += 1
        else:
            run_all(ga)
    if steps < KSTOP:
        run_all(post(0, NR, lambda cc: cc % 3, False))
    k.dump("dt_all", dt_all, dt_all.t[:].rearrange("p c d -> p (c d)"), [128, NCH * 32])
    k.pop()

    if KPASS < 2:
        k.barrier()
        k.es.close()
        return nc, k.dbg
    st = k.push()
    bg1_rows = bias_rows(bgate_d.t[0:1, 0:D], D, 1.0, "bg1")
    b0_rows = bias_rows(rows_d.t[1:2, :], D, ALPHA, "b0r")
    a0row = k.sb([128, D], F32, "a0row")
    k.dma("sp", a0row.t[:], rows_d.t[0:1, :].partition_broadcast(128), w=[a0row])
    k.ts("dve", a0row.t[:], a0row.t[:], float(ALPHA), None, ALU.mult, None, [a0row], [a0row])
    ngrow = k.sb([128, D], F32, "ngrow")
    k.dma("sp", ngrow.t[:], rows_d.t[2:3, :].partition_broadcast(128), w=[ngrow])
    make_row_psum(16, "g1row", (pss[4], pss[5]))
    wout = k.sb([128, 8, D], BF16, "wout")
    load_w(wout, wout_d.t, 8)
    w2 = k.sb([128, 8, 2048], BF16, "w2")
    for kk in range(8):
        rs_ = slice(kk * 128, (kk + 1) * 128)
        k.dma("pool", w2.t[:, kk, 0:1024], win_d.t[rs_, O_Z:O_Z + D], r=[win_d], w=[w2], partial=True)
        k.dma("pool", w2.t[:, kk, 1024:2048], win_d.t[rs_, O_G:O_G + D], r=[win_d], w=[w2], partial=True)
    wssd = k.sb([128, 8, D], BF16, "wssd")
    load_w(wssd, wssd_d.t, 8)

    def scale_wout():
        for kk in range(8):
            for half in range(2):
                sl = slice(half * 512, (half + 1) * 512)
                k.tt("dve", wout.t[:, kk, sl], wout.t[:, kk, sl], pss[4 + half].t[:], ALU.mult, [wout, pss[4 + half]], [wout])

    ln0 = LN0()
    xs_l = [k.sb([128, D], BF16, "xs_l") for _ in range(2)]
    bt_l = [k.sb([128, 256], BF16, "bt_l") for _ in range(2)]
    bc_l = [k.sb([128, 4, 128], BF16, "bc_l") for _ in range(2)]
    hp_l = [k.sb([128, D], BF16, "hp_l") for _ in range(2)]
    g2_l = [k.sb([128, D], BF16, "g2_l") for _ in range(2)]
    a_sb = k.sb([128, 32], F32, "a_sb2")
    ex = k.sb([128, 96], F32, "ex2")
    ddv = k.sb([128, 32], F32, "ddv2")
    sz_l = [k.sb([128, D], BF16, "sz") for _ in range(2)]
    g1_l = [k.sb([128, D], BF16, "g1") for _ in range(2)]
    Rf = k.sb([128, H, 128], F32R, "Rf")
    Rb = Rf
    Ef = k.sb([128, H, 128], F32, "Ef")
    cbm = k.sb([128, 2, 2, 128], F32, "cbm")
    wgt = [k.sb([128, H, 128], BF16, f"wgt{d}") for d in range(2)]
    xdt = [k.sb([128, D], BF16, f"xdt{d}") for d in range(2)]
    xdd = k.sb([128, D], BF16, "xdd2")
    runf_bf = k.sb([128, D], BF16, "runf_bf")
    t1 = k.sb([128, D], F32, "t1")
    t2 = k.sb([128, D], F32, "t2")
    hh = k.sb([128, D], F32, "hh")
    sq = t2
    ss = k.sb([128, 1], F32, "ss")
    rstd = k.sb([128, 1], F32, "rstd")
    yg = k.sb([128, D], BF16, "yg")
    ygT = k.sb([128, 8, 128], BF16, "ygT")
    m1 = hh
    mg = k.sb([128, D], BF16, "mg")
    mgT = k.sb([128, 8, 128], BF16, "mgT")
    r1 = t1
    x1h = [k.sb([128, D], F32, "x1h") for _ in range(2)]
    lst = k.sb([128, 2, 6], F32, "lst")
    lmv = k.sb([128, 2], F32, "lmv")
    lrs = k.sb([128, 1], F32, "lrs")
    lnm = k.sb([128, 1], F32, "lnm")

    k.copy("act", runf_bf.t[:], run_f.t[:], [run_f], [runf_bf])
    print("SBUF free in pass 2:", nc.sbuf_bytes_remaining)

    def h3(ap, h=H):
        return ap.rearrange("p (h q) -> p h q", h=h)

    KSTOP2 = int(os.environ.get("KSTOP2", "1000"))
    ln2cache = {}

    def front2a(c):
        ln2cache[c] = ln0.run(x_d, x_d.t[c * 128:(c + 1) * 128, :], S1X, B1X, (pss[0], pss[1]))

    def front2b(c):
        xh, hT = ln2cache[c]
        sz, g1 = sz_l[c % 2], g1_l[c % 2]
        for half in range(2):
            pb = pss[half]
            proj_tok(hT, w2, half * 512, 512, pb)
            k.act(sz.t[:, half * 512:(half + 1) * 512], pb.t[:], AF.Silu, [pb], [sz])
        for half in range(2):
            pb = pss[half]
            proj_tok(hT, w2, 1024 + half * 512, 512, pb, bg1_rows, half * 512)
            k.act(g1.t[:, half * 512:(half + 1) * 512], pb.t[:], AF.Sigmoid, [pb], [g1])

    front2a(0)
    scale_wout()
    front2b(0)
    for c in range(min(NR, KSTOP2)):
        s = c % 2
        xs, bt, bcl, hp, g2l = xs_l[s], bt_l[s], bc_l[s], hp_l[s], g2_l[s]
        k.dma("sp", xs.t[:], xs_s[c].t, r=[xs_s[c]], w=[xs])
        k.dma("sp", bt.t[:], bt_s[c].t, r=[bt_s[c]], w=[bt])
        k.dma("sp", bcl.t[:], bc_s[c].t, r=[bc_s[c]], w=[bcl])
        k.dma("sp", hp.t[:], hp_s[c].t, r=[hp_s[c]], w=[hp])
        k.dma("sp", g2l.t[:], g2_s[c].t, r=[g2_s[c]], w=[g2l])
        xh, hT = ln2cache.pop(c)
        sz, g1 = sz_l[s], g1_l[s]
        xo = x1h[s]
        k.tt("pool", xo.t[:], xh.t[:], a0row.t[:], ALU.mult, [xh, a0row], [xo])
        if c + 1 < min(NR, KSTOP2):
            front2a(c + 1)
        dtc = dt_all.t[:, c, :]
        k.tt("dve", a_sb.t[:], dtc, arow.t[:], ALU.mult, [dt_all, arow], [a_sb])
        pb = pss[6]
        for i, (mi, ao) in enumerate(((MLE, 0), (MGE, 16), (MGT, 0), (MLT, 16))):
            k.mm(pb.t[:, i * 16:(i + 1) * 16], cm.t[:, mi, :], a_sb.t[:, ao:ao + 16], True, True, [cm, a_sb], [pb], inc=False)
        k.mm(pb.t[:, 64:96], ones_f.t[:], a_sb.t[:, 0:32], True, True, [ones_f, a_sb], [pb], inc=True)
        k.act(ex.t[:], pb.t[:, 0:96], AF.Exp, [pb], [ex])
        pb = pss[6]
        for g in range(2):
            k.mm(pb.t[:, 256 + g * 128:256 + (g + 1) * 128], bcl.t[:, g, :], bcl.t[:, 2 + g, :], True, True, [bcl], [pb], inc=(g == 1))
        pv = pb.t[:, 256:512].rearrange("p (g i) -> p g i", g=2)
        k.tt("dve", cbm.t[:, 0, :, :], pv, bc(cm.t[:, MLE, :].unsqueeze(1), [128, 2, 128]), ALU.mult, [pb, cm], [cbm])
        k.tt("dve", cbm.t[:, 1, :, :], pv, bc(cm.t[:, MGE, :].unsqueeze(1), [128, 2, 128]), ALU.mult, [pb, cm], [cbm])
        for d in range(2):
            k.tt("pool", h3(xdt[d].t[:]), h3(xs.t[:]), bc(dtc[:, d * 16:(d + 1) * 16].unsqueeze(2), [128, H, P]),
                 ALU.mult, [xs, dt_all], [xdt[d]])
        for d, (R, Lm) in enumerate(((Rf, mgt_r), (Rb, mlt_r))):
            k.tt("pool", R.t[:], bc(a_sb.t[:, d * 16:(d + 1) * 16].unsqueeze(2), [128, H, 128]),
                 bc(cm.t[:, MGE if d else MLE, :].unsqueeze(1), [128, H, 128]), ALU.mult, [a_sb, cm], [R])
            R2 = R.t[:].rearrange("p h i -> p (h i)")
            E2 = Ef.t[:].rearrange("p h i -> p (h i)")
            for q in range(4):
                pb = pss[2 + q]
                k.mm(pb.t[:], Lm.t[:], R2[:, q * 512:(q + 1) * 512], True, True, [Lm, R], [pb], inc=True)
                k.act(E2[:, q * 512:(q + 1) * 512], pb.t[:], AF.Exp, [pb], [Ef])
            for g in range(2):
                k.tt("dve", wgt[d].t[:, g * 8:(g + 1) * 8, :], Ef.t[:, g * 8:(g + 1) * 8, :],
                     bc(cbm.t[:, d, g, :].unsqueeze(1), [128, 8, 128]), ALU.mult, [Ef, cbm], [wgt[d]])
        if c + 1 < min(NR, KSTOP2):
            front2b(c + 1)
        for h in range(H):
            pb = pss[h // 8]
            o = pb.t[:, (h % 8) * 64:(h % 8 + 1) * 64]
            cs = slice(h * 64, (h + 1) * 64)
            k.mm(o, wgt[0].t[:, h, :], xdt[0].t[:, cs], True, False, [wgt[0], xdt[0]], [pb], inc=False)
            k.mm(o, wgt[1].t[:, h, :], xdt[1].t[:, cs], False, False, [wgt[1], xdt[1]], [pb], inc=False)
            k.mm(o, dh.t[:, h, :], xs.t[:, cs], False, True, [dh, xs], [pb], inc=(h % 8 == 7))
        for g in range(2):
            k.mm(pss[2 + g].t[:], bcl.t[:, 2 + g, :], runf_bf.t[:, g * 512:(g + 1) * 512], True, True, [bcl, runf_bf], [pss[2 + g]], inc=True)
            k.mm(pss[4 + g].t[:], bcl.t[:, 2 + g, :], hp.t[:, g * 512:(g + 1) * 512], True, True, [bcl, hp], [pss[4 + g]], inc=True)
        for g in range(2):
            sl = slice(g * 512, (g + 1) * 512)
            k.tt("dve", h3(t1.t[:, sl], 8), h3(pss[2 + g].t[:], 8), bc(ex.t[:, g * 8:(g + 1) * 8].unsqueeze(2), [128, 8, P]),
                 ALU.mult, [pss[2 + g], ex], [t1])
            k.tt("dve", h3(t2.t[:, sl], 8), h3(pss[4 + g].t[:], 8), bc(ex.t[:, 16 + g * 8:16 + (g + 1) * 8].unsqueeze(2), [128, 8, P]),
                 ALU.mult, [pss[4 + g], ex], [t2])
            k.tt("pool", t1.t[:, sl], t1.t[:, sl], t2.t[:, sl], ALU.add, [t1, t2], [t1])
            k.tt("dve", t1.t[:, sl], t1.t[:, sl], pss[g].t[:], ALU.add, [t1, pss[g]], [t1])
        if c == 0:
            k.dump("y0", t1, t1.t[:], [128, D])
        k.tt("dve", hh.t[:], t1.t[:], sz.t[:], ALU.mult, [t1, sz], [hh])
        k.act(sq.t[:], hh.t[:], AF.Square, [hh], [sq, ss], accum=ss.t[:])
        rsqrt_act(rstd, ss.t[:], ss, 1.0 / D)
        k.stt(yg.t[:], hh.t[:], rstd.t[:], ngrow.t[:], ALU.mult, ALU.mult, [hh, rstd, ngrow], [yg])

        transpose_to(ygT, yg, pss[6], "act")
        k.tt("dve", ddv.t[:, 0:16], ex.t[:, 32:48], dtc[:, 0:16], ALU.mult, [ex, dt_all], [ddv])
        k.tt("pool", h3(xdd.t[:]), h3(xs.t[:]), bc(ddv.t[:, 0:16].unsqueeze(2), [128, H, P]), ALU.mult, [xs, ddv], [xdd])
        for g in range(2):
            k.mm(pss[2 + g].t[:], bt.t[:, g * 128:(g + 1) * 128], xdd.t[:, g * 512:(g + 1) * 512], True, True, [bt, xdd], [pss[2 + g]], inc=True)
        k.tt("pool", h3(run_f.t[:]), h3(run_f.t[:]), bc(ex.t[:, 64:80].unsqueeze(2), [128, H, P]), ALU.mult, [run_f, ex], [run_f])
        for g in range(2):
            sl = slice(g * 512, (g + 1) * 512)
            k.tt("dve", run_f.t[:, sl], run_f.t[:, sl], pss[2 + g].t[:], ALU.add, [run_f, pss[2 + g]], [run_f])
        k.copy("act", runf_bf.t[:], run_f.t[:], [run_f], [runf_bf])
        for half in range(2):
            sl = slice(half * 512, (half + 1) * 512)
            pb = pss[4 + half]
            for kk in range(8):
                k.mm(pb.t[:], ygT.t[:, kk, :], wssd.t[:, kk, sl], kk == 0, kk == 7, [ygT, wssd], [pb], inc=(kk == 7))
            k.tt("dve", m1.t[:, sl], pb.t[:], g1.t[:, sl], ALU.mult, [pb, g1], [m1])
            k.tt("dve", mg.t[:, sl], m1.t[:, sl], g2l.t[:, sl], ALU.add, [m1, g2l], [mg])

        transpose_to(mgT, mg, pss[6], "dve")
        for half in range(2):
            sl = slice(half * 512, (half + 1) * 512)
            pb = pss[half]
            for kk in range(8):
                k.mm(pb.t[:], mgT.t[:, kk, :], wout.t[:, kk, sl], kk == 0, False, [mgT, wout], [pb], inc=False)
            k.mm(pb.t[:], ones_bf.t[0:2, :], b0_rows.t[0:2, sl], False, True, [ones_bf, b0_rows], [pb], inc=True)
            k.tt("dve", r1.t[:, sl], xo.t[:, sl], pb.t[:], ALU.add, [xo, pb], [r1])
        if c == 0:
            k.dump("r1", r1, r1.t[:], [128, D])
        ln_stats(r1, lst, lmv, lrs, lnm)
        k.act(xo.t[:], r1.t[:], AF.Identity, [r1, lrs, lnm], [xo], bias=lnm.t[:], scale=lrs.t[:])
        k.dma("pool", x1_s[c].t, xo.t[:], r=[xo], w=[x1_s[c]])
    k.pop()

    k.fill = None
    if KPASS < 3:
        k.barrier()
        k.es.close()
        return nc, k.dbg
    st = k.push()
    b1_rows = bias_rows(rows_d.t[6:7, :], D, ALPHA, "b1r")
    a1row = k.sb([128, D], F32, "a1row")
    k.dma("sp", a1row.t[:], rows_d.t[5:6, :].partition_broadcast(128), w=[a1row])
    k.ts("dve", a1row.t[:], a1row.t[:], float(ALPHA), None, ALU.mult, None, [a1row], [a1row])
    l2g = k.sb([128, D], F32, "l2g")
    k.dma("sp", l2g.t[:], rows_d.t[7:8, :].partition_broadcast(128), w=[l2g])
    l2b = k.sb([128, D], F32, "l2b")
    k.dma("sp", l2b.t[:], rows_d.t[8:9, :].partition_broadcast(128), w=[l2b])
    make_row_psum(40, "g2row", (pss[6], pss[7]))
    wf1 = k.sb([128, 8, DFF], BF16, "wf1")
    wf3 = k.sb([128, 8, DFF], BF16, "wf3")
    wf2 = k.sb([128, NFF, D], BF16, "wf2")
    FG = ((0, 6), (6, 12), (12, 17), (17, 22))
    wf1g = [Buf(wf1.t) for _ in FG]
    wf3g = [Buf(wf3.t) for _ in FG]
    for gi, (f0, f1) in enumerate(FG):
        for kk in range(8):
            k.dma("pool", wf1.t[:, kk, f0 * 128:f1 * 128], wff1_d.t[kk * 128:(kk + 1) * 128, f0 * 128:f1 * 128], w=[wf1g[gi]], partial=True)
        for kk in range(8):
            k.dma("pool", wf3.t[:, kk, f0 * 128:f1 * 128], wff3_d.t[kk * 128:(kk + 1) * 128, f0 * 128:f1 * 128], w=[wf3g[gi]], partial=True)
    load_w(wf2, wff2_d.t, NFF)

    def fgrp(f):
        return [gi for gi, (f0, f1) in enumerate(FG) if f0 <= f < f1][0]

    def scale_wf2():
        for kk in range(NFF):
            for half in range(2):
                sl = slice(half * 512, (half + 1) * 512)
                k.tt("dve", wf2.t[:, kk, sl], wf2.t[:, kk, sl], pss[6 + half].t[:], ALU.mult, [wf2, pss[6 + half]], [wf2])
    TB = 2
    NB = NR // TB
    x1t = [[k.sb([128, D], F32, "x1t") for _ in range(TB)] for _ in range(2)]
    xmT = [k.sb([128, 8, TB * 128], BF16, "xmT") for _ in range(2)]
    hid = k.sb([128, NFF, TB * 128], BF16, "hid")
    sl1 = [k.sb([128, TB * 128], F32, "sl1") for _ in range(2)]
    fst = k.sb([128, 2, 6], F32, "fst")
    fmv = k.sb([128, 2], F32, "fmv")
    frs = k.sb([128, 1], F32, "frs")
    fnm = k.sb([128, 1], F32, "fnm")
    NT = TB * 128
    print("SBUF free in pass 3:", nc.sbuf_bytes_remaining)

    def ffn_front(b):
        s = b % 2
        for t in range(TB):
            c = b * TB + t
            xt = x1t[s][t]
            k.dma("sp", xt.t[:], x1_s[c].t, r=[x1_s[c]], w=[xt])
            for cc in range(8):
                pb = pss[cc // 4]
                k.tr(pb.t[:, (cc % 4) * 128:(cc % 4 + 1) * 128], xt.t[:, cc * 128:(cc + 1) * 128], cm.t[:, IDN, :],
                     [xt, cm], [pb], inc=(cc % 4 == 3))
            for cc in range(8):
                pb = pss[cc // 4]
                src = pb.t[:, (cc % 4) * 128:(cc % 4 + 1) * 128]
                dst = xmT[s].t[:, cc, t * 128:(t + 1) * 128]
                if cc < 4:
                    k.ts("dve", dst, src, cols.t[:, S2, cc:cc + 1], cols.t[:, B2, cc:cc + 1], ALU.mult, ALU.add, [pb, cols], [xmT[s]])
                else:
                    k.act(dst, src, AF.Identity, [pb, cols], [xmT[s]], bias=cols.t[:, B2, cc:cc + 1], scale=cols.t[:, S2, cc:cc + 1])
            k.tt("pool", xt.t[:], xt.t[:], a1row.t[:], ALU.mult, [xt, a1row], [xt])

    def ffn_w13(b):
        s = b % 2
        for f in range(NFF):
            p1 = pss[2 + (f % 2) * 2]
            p3 = pss[3 + (f % 2) * 2]
            for kk in range(8):
                k.mm(p1.t[:, 0:NT], wf1.t[:, kk, f * 128:(f + 1) * 128], xmT[s].t[:, kk, :], kk == 0, kk == 7, [wf1g[fgrp(f)], xmT[s]], [p1], inc=(kk == 7))
            for kk in range(8):
                k.mm(p3.t[:, 0:NT], wf3.t[:, kk, f * 128:(f + 1) * 128], xmT[s].t[:, kk, :], kk == 0, kk == 7, [wf3g[fgrp(f)], xmT[s]], [p3], inc=(kk == 7))
            sv = sl1[f % 2]
            k.act(sv.t[:], p1.t[:, 0:NT], AF.Silu, [p1], [sv])
            k.tt("dve", hid.t[:, f, :], sv.t[:], p3.t[:, 0:NT], ALU.mult, [sv, p3], [hid])

    def ffn_w2(b):
        s = b % 2
        for t in range(TB):
            c = b * TB + t
            xt = x1t[s][t]
            r2 = xt
            for half in range(2):
                sl = slice(half * 512, (half + 1) * 512)
                pb = pss[6 + half]
                for f in range(NFF):
                    k.mm(pb.t[:], hid.t[:, f, t * 128:(t + 1) * 128], wf2.t[:, f, sl], f == 0, False, [hid, wf2], [pb], inc=False)
                k.mm(pb.t[:], ones_bf.t[0:2, :], b1_rows.t[0:2, sl], False, True, [ones_bf, b1_rows], [pb], inc=True)
                k.tt("dve", r2.t[:, sl], r2.t[:, sl], pb.t[:], ALU.add, [r2, pb], [r2])
            ln_stats(r2, fst, fmv, frs, fnm)
            o = r2
            k.act(o.t[:], r2.t[:], AF.Identity, [r2, frs, fnm], [o], bias=fnm.t[:], scale=frs.t[:])
            k.tt("pool", o.t[:], o.t[:], l2g.t[:], ALU.mult, [o, l2g], [o])
            k.tt("dve", o.t[:], o.t[:], l2b.t[:], ALU.add, [o, l2b], [o])
            k.dma("pool", out_d.t[c * 128:(c + 1) * 128, :], o.t[:], r=[o], w=[out_d])

    if NB > 0:
        ffn_front(0)
    for b in range(NB):
        ffn_w13(b)
        if b + 1 < NB:
            ffn_front(b + 1)
        if b == 0:
            scale_wf2()
        ffn_w2(b)
    k.pop()
    k.barrier()
    k.es.close()
    return nc, k.dbg


def _prep(inputs):
    f = lambda a: np.ascontiguousarray(np.asarray(a, dtype=np.float32))
    i = {kk: f(v) for kk, v in inputs.items()}
    kk = np.arange(128)
    cm = np.stack([np.eye(128), kk[:, None] <= kk[None, :], kk[:, None] >= kk[None, :],
                   kk[:, None] > kk[None, :], kk[:, None] < kk[None, :]], axis=1).astype(np.float32)
    col = lambda v: f(v.reshape(-1, 128).T)
    shared = dict(
        cm=f(cm), w_ada=i["w_ada"][0], b_ada_c=col(i["b_ada"][0]),
        colp=f(np.stack([col(i["ln0_g"]), col(i["ln0_b"]), col(i["ln1_g"][0]), col(i["ln1_b"][0])], axis=1)),
        convw_c=f(i["conv_w"][0].T.reshape(12, 128, 5).transpose(1, 0, 2)),
        convb_c=col(i["conv_b"][0]), w_in=i["w_in"][0],
        rows=f(np.stack([i["ln0_g"], i["ln0_b"], i["ssd_norm_g"][0], i["gm_norm_g"][0], i["gm_norm_b"][0],
                         i["ln1_g"][0], i["ln1_b"][0], i["ln2_g"][0], i["ln2_b"][0]])),
        b_gate=f(i["b_gate"][0][None, :]),
        small=f(np.concatenate([i["dt_bias"][0].reshape(-1), i["a_log"][0].reshape(-1), i["d_skip"][0].reshape(-1)])[None, :]),
        wspT=f(i["w_spatial"][0].transpose(2, 0, 1)), wsp=f(i["w_spatial"][0].transpose(1, 0, 2)),
        bsp_c=f(i["b_spatial"][0].T),
        w_ssd=i["w_ssd_proj"][0], w_gm=i["w_gm_proj"][0], w_out=i["w_out"][0],
        w_ff1=i["w_ff1"][0], w_ff3=i["w_ff3"][0], w_ff2=i["w_ff2"][0],
    )
    maps = []
    for b in range(i["x"].shape[0]):
        m = dict(shared)
        m["x"] = i["x"][b]
        m["ctx"] = i["ctx"][b]
        m["cT"] = f(np.stack([i["c"][b], i["c_ctx"]], axis=1).reshape(8, 128, 2).transpose(1, 0, 2))
        maps.append(m)
    return maps


def kernel(**inputs):
    maps = _prep(inputs)
    nc, _ = build()
    n = len(maps)
    res = run_bass_kernel_spmd(nc, maps, core_ids=list(range(n)))
    return np.stack([np.asarray(r["out"], dtype=np.float32) for r in res.results], axis=0)
```

```python
import os
from contextlib import ExitStack
import numpy as np
import concourse.bass as bass
import concourse.mybir as mybir
from concourse.bass_utils import run_bass_kernel_spmd

F32 = mybir.dt.float32
F32R = mybir.dt.float32r
BF16 = mybir.dt.bfloat16
AF = mybir.ActivationFunctionType
ALU = mybir.AluOpType

D = 1024
SEQ = 4096
CTX = 256
NCH = SEQ // 128
H = 16
P = 64
NST = 128
DFF = 2816
NFF = DFF // 128
XBC = 1536
DPROJ = 6688
O_Z, O_XBC, O_DT, O_U, O_V, O_G = 0, 1024, 2560, 2592, 3616, 4640
ALPHA = 2.0 ** 0.25
EPS = 1e-5
DEBUG = bool(int(os.environ.get("KDEBUG", "0")))
NCH_RUN = int(os.environ.get("KNCH", str(NCH)))
KPASS = int(os.environ.get("KPASS", "3"))
NFILL = int(os.environ.get("KFILL", "3"))


class Buf:
    def __init__(self, t):
        self.t = t
        self.w = None
        self.pw = []
        self.r = {}
        self.psum = False

    def __getitem__(self, idx):
        return self.t[idx]


class Eng:
    def __init__(self, e, sem, name):
        self.e, self.sem, self.name = e, sem, name
        self.count = 0
        self.seen = {}
        self.pend_r, self.pend_w = [], []


class KB:
    def __init__(self):
        self.nc = bass.Bass("TRN2", target_bir_lowering=False)
        nc = self.nc
        self.es = ExitStack()
        self.stacks = [self.es]
        self.engs = {}
        for name, e in (("pe", nc.tensor), ("act", nc.scalar), ("dve", nc.vector), ("pool", nc.gpsimd), ("sp", nc.sync)):
            sem = self.es.enter_context(nc.semaphore("s_" + name))
            self.engs[name] = Eng(e, sem, name)
        self.dpool = {}
        for q, n in (("sp", 20), ("pool", 20), ("act", 6)):
            self.dpool[q] = [[self.es.enter_context(nc.semaphore(f"d_{q}{i}")), 0] for i in range(n)]
        self.dnext = {q: 0 for q in self.dpool}
        self.uid = 0
        self.dbg = []
        self.fill = None

    def sb(self, shape, dt, name=None):
        self.uid += 1
        t = self.stacks[-1].enter_context(self.nc.sbuf_tensor(f"{name or 't'}_{self.uid}", list(shape), dt))
        return Buf(t)

    def ps(self, name):
        self.uid += 1
        t = self.stacks[-1].enter_context(self.nc.psum_tensor(f"{name}_{self.uid}", [128, 512], F32))
        b = Buf(t)
        b.psum = True
        return b

    def dram(self, name, shape, dt, kind="Internal"):
        return Buf(self.nc.dram_tensor(name, list(shape), dt, kind=kind).ap())

    def push(self):
        print("SBUF remaining at push:", self.nc.sbuf_bytes_remaining)
        st = ExitStack()
        self.stacks.append(st)
        return st

    def pop(self):
        self.barrier()
        self.stacks.pop().close()

    def _waits(self, E, r, w, partial=False):
        deps = {}
        raw = set()
        for b in r:
            if b.w is not None:
                deps[(b.w[0], b.w[1])] = b.w
                raw.add((b.w[0], b.w[1]))
            for t in b.pw:
                deps[(t[0], t[1])] = t
            if b.psum:
                for t in b.r.values():
                    if t[2] is not E:
                        deps[(t[0], t[1])] = t
        for b in w:
            if b.w is not None:
                deps[(b.w[0], b.w[1])] = b.w
            if not partial:
                for t in b.pw:
                    deps[(t[0], t[1])] = t
            for t in b.r.values():
                deps[(t[0], t[1])] = t
        for key, (sem, val, owner) in deps.items():
            if owner is E:
                if E.name == "pe":
                    continue
                if key not in raw:
                    continue
            if E.seen.get(sem, 0) >= val:
                continue
            if E.name == "pe" and self.fill is not None and owner is not None:
                for v in range(max(E.seen.get(sem, 0) + 1, val - NFILL), val):
                    E.e.wait_ge(sem, v)
                    self.nc.tensor.matmul(self.fill[0], self.fill[1], self.fill[2], start=True, stop=True)
            E.e.wait_ge(sem, val)
            E.seen[sem] = val

    def op(self, eng, fn, r=(), w=(), inc=True):
        E = self.engs[eng]
        self._waits(E, r, w)
        ins = fn()
        E.pend_r.extend(r)
        E.pend_w.extend(w)
        if inc:
            E.count += 1
            ins.then_inc(E.sem, 1)
            tok = (E.sem, E.count, E)
            for b in E.pend_r:
                b.r[E.sem] = tok
            for b in E.pend_w:
                b.w = tok
                b.pw = []
                b.r = {}
            E.pend_r, E.pend_w = [], []
        return ins

    def dma(self, q, out, in_, r=(), w=(), partial=False, **kw):
        E = self.engs[q]
        self._waits(E, r, w, partial)
        pool = self.dpool[q]
        i = self.dnext[q]
        self.dnext[q] = (i + 1) % len(pool)
        sem, val = pool[i]
        if val > 0 and E.seen.get(sem, 0) < val:
            E.e.wait_ge(sem, val)
            E.seen[sem] = val
        ins = E.e.dma_start(out=out, in_=in_, **kw)
        val += 16
        pool[i][1] = val
        ins.then_inc(sem, 16)
        tok = (sem, val, None)
        for b in r:
            b.r[sem] = tok
        for b in w:
            if partial:
                b.pw.append(tok)
            else:
                b.w = tok
                b.pw = []
            b.r = {}

    def barrier(self):
        toks = [(E.sem, E.count) for E in self.engs.values() if E.count > 0]
        for pool in self.dpool.values():
            toks += [(s, v) for s, v in pool if v > 0]
        for E in self.engs.values():
            assert not E.pend_r and not E.pend_w, E.name
            for sem, val in toks:
                if sem is E.sem or E.seen.get(sem, 0) >= val:
                    continue
                E.e.wait_ge(sem, val)
                E.seen[sem] = val

    def act(self, out, in_, func, r, w, bias=None, scale=None, accum=None):
        kw = {}
        if bias is not None:
            kw["bias"] = bias
        if scale is not None:
            kw["scale"] = scale
        if accum is not None:
            kw["accum_out"] = accum
        return self.op("act", lambda: self.nc.scalar.activation(out=out, in_=in_, func=func, **kw), r, w)

    def tt(self, eng, out, in0, in1, op, r, w):
        e = self.engs[eng].e
        return self.op(eng, lambda: e.tensor_tensor(out=out, in0=in0, in1=in1, op=op), r, w)

    def ts(self, eng, out, in0, s1, s2, op0, op1, r, w):
        e = self.engs[eng].e
        if s2 is None:
            return self.op(eng, lambda: e.tensor_scalar(out=out, in0=in0, scalar1=s1, scalar2=None, op0=op0), r, w)
        return self.op(eng, lambda: e.tensor_scalar(out=out, in0=in0, scalar1=s1, scalar2=s2, op0=op0, op1=op1), r, w)

    def stt(self, out, in0, scalar, in1, op0, op1, r, w):
        return self.op("dve", lambda: self.nc.vector.scalar_tensor_tensor(
            out=out, in0=in0, scalar=scalar, in1=in1, op0=op0, op1=op1), r, w)

    def copy(self, eng, out, in_, r, w):
        e = self.engs[eng].e
        if eng == "act":
            return self.act(out, in_, AF.Copy, r, w)
        return self.op(eng, lambda: e.tensor_copy(out=out, in_=in_), r, w)

    def memset(self, eng, ap, val, w):
        e = self.engs[eng].e
        return self.op(eng, lambda: e.memset(ap, val), (), w)

    def mm(self, out, lhsT, rhs, start, stop, r, w, inc):
        return self.op("pe", lambda: self.nc.tensor.matmul(out, lhsT, rhs, start=start, stop=stop), r, w, inc)

    def tr(self, out, in_, ident, r, w, inc):
        return self.op("pe", lambda: self.nc.tensor.transpose(out, in_, ident), r, w, inc)

    def dump(self, name, buf, ap, shape, dt=F32):
        if not DEBUG:
            return
        d = self.dram("dbg_" + name, shape, dt, kind="ExternalOutput")
        self.dma("sp", d.t, ap, r=[buf], w=[d])
        self.dbg.append("dbg_" + name)


def bc(ap, shape):
    return ap.to_broadcast(list(shape))


def build():
    k = KB()
    nc = k.nc

    def din(name, shape, dt=F32):
        return k.dram(name, shape, dt, kind="ExternalInput")

    x_d = din("x", [SEQ, D])
    ctx_d = din("ctx", [CTX, D])
    cT_d = din("cT", [128, 8, 2])
    cm_d = din("cm", [128, 5, 128])
    wada_d = din("w_ada", [D, 6 * D])
    badac_d = din("b_ada_c", [128, 48])
    colp_d = din("colp", [128, 4, 8])
    convw_d = din("convw_c", [128, 12, 5])
    convb_d = din("convb_c", [128, 12])
    win_d = din("w_in", [D, DPROJ])
    rows_d = din("rows", [9, D])
    bgate_d = din("b_gate", [1, 2 * D])
    small_d = din("small", [1, 96])
    wspT_d = din("wspT", [128, 8, 128])
    wsp_d = din("wsp", [128, 8, 128])
    bsp_d = din("bsp_c", [128, 8])
    wssd_d = din("w_ssd", [D, D])
    wgm_d = din("w_gm", [D, D])
    wout_d = din("w_out", [D, D])
    wff1_d = din("w_ff1", [D, DFF])
    wff3_d = din("w_ff3", [D, DFF])
    wff2_d = din("w_ff2", [DFF, D])
    out_d = k.dram("out", [SEQ, D], F32, kind="ExternalOutput")

    xs_s = [k.dram(f"xs_s{c}", [128, D], BF16) for c in range(NCH)]
    bt_s = [k.dram(f"bt_s{c}", [128, 256], BF16) for c in range(NCH)]
    bc_s = [k.dram(f"bc_s{c}", [128, 4, 128], BF16) for c in range(NCH)]
    hp_s = [k.dram(f"hp_s{c}", [128, D], BF16) for c in range(NCH)]
    g2_s = [k.dram(f"g2_s{c}", [128, D], BF16) for c in range(NCH)]
    x1_s = [k.dram(f"x1_s{c}", [128, D], F32) for c in range(NCH)]

    cm = k.sb([128, 5, 128], F32, "cm")
    k.dma("sp", cm.t[:], cm_d.t, w=[cm])
    IDN, MLE, MGE, MGT, MLT = range(5)
    ident_bf = k.sb([128, 128], BF16, "identbf")
    k.copy("dve", ident_bf.t[:], cm.t[:, IDN, :], [cm], [ident_bf])
    mgt_r = k.sb([128, 128], F32R, "mgtr")
    mlt_r = k.sb([128, 128], F32R, "mltr")
    k.copy("dve", mgt_r.t[:], cm.t[:, MGT, :], [cm], [mgt_r])
    k.copy("dve", mlt_r.t[:], cm.t[:, MLT, :], [cm], [mlt_r])
    ones_f = k.sb([128, 128], F32, "onesf")
    k.memset("dve", ones_f.t[:], 1.0, [ones_f])
    ones_bf = k.sb([2, 128], BF16, "onesbf")
    k.memset("dve", ones_bf.t[:], 1.0, [ones_bf])

    colp = k.sb([128, 4, 8], F32, "colp")
    k.dma("sp", colp.t[:], colp_d.t, w=[colp])
    convw = k.sb([128, 12, 5], F32, "convw")
    k.dma("sp", convw.t[:], convw_d.t, w=[convw])
    convb = k.sb([128, 12], F32, "convb")
    k.dma("sp", convb.t[:], convb_d.t, w=[convb])
    smallr = k.sb([128, 96], F32, "smallr")
    k.dma("sp", smallr.t[:], small_d.t.partition_broadcast(128), w=[smallr])
    arow = k.sb([128, 32], F32, "arow")
    k.act(arow.t[:], smallr.t[:, 32:64], AF.Exp, [smallr], [arow])
    k.ts("dve", arow.t[:], arow.t[:], -1.0, None, ALU.mult, None, [arow], [arow])
    dsum = k.sb([128, 16], F32, "dsum")
    k.tt("dve", dsum.t[:], smallr.t[:, 64:80], smallr.t[:, 80:96], ALU.add, [smallr], [dsum])
    dh = k.sb([128, H, 128], BF16, "dh")
    k.tt("dve", dh.t[:], bc(cm.t[:, IDN, :].unsqueeze(1), [128, H, 128]),
         bc(dsum.t[:].unsqueeze(2), [128, H, 128]), ALU.mult, [cm, dsum], [dh])
    dt_all = k.sb([128, NCH, 32], F32, "dtall")
    run_f = k.sb([128, D], F32, "runf")
    run_b = k.sb([128, D], F32, "runb")
    modc = k.sb([128, 48, 2], F32, "modc")
    cols = k.sb([128, 6, 8], F32, "cols")
    S1X, B1X, S1C, B1C, S2, B2 = range(6)
    lntmp = k.sb([128, 1], F32, "lntmp")
    epsc = k.sb([128, 1], F32, "epsc")
    k.memset("dve", epsc.t[:], EPS, [epsc])

    pss = [k.ps(f"ps{i}") for i in range(8)]
    fill_spec = (pss[7].t[:, 0:256], ident_bf.t[:], dh.t[:, 0:2, :].rearrange("p a i -> p (a i)"))

    st = k.push()
    w1 = k.sb([128, 8, 1568 + 2048 + 1024], BF16, "w1")
    W_XBC, W_DT, W_U, W_V, W_G2 = 0, 1536, 1568, 2592, 3616
    for kk in range(8):
        rs_ = slice(kk * 128, (kk + 1) * 128)
        k.dma("pool", w1.t[:, kk, 0:1568], win_d.t[rs_, O_XBC:O_XBC + 1568], r=[win_d], w=[w1], partial=True)
    for kk in range(8):
        rs_ = slice(kk * 128, (kk + 1) * 128)
        k.dma("pool", w1.t[:, kk, 1568:3616], win_d.t[rs_, O_U:O_U + 2048], r=[win_d], w=[w1], partial=True)
        k.dma("pool", w1.t[:, kk, 3616:4640], win_d.t[rs_, O_G + D:O_G + 2 * D], r=[win_d], w=[w1], partial=True)
    wgm = k.sb([128, 8, D], BF16, "wgm")
    for kk in range(8):
        k.dma("pool", wgm.t[:, kk, :], wgm_d.t[kk * 128:(kk + 1) * 128, :], w=[wgm], partial=True)
    wspT = k.sb([128, 8, 128], BF16, "wspT")
    k.dma("pool", wspT.t[:], wspT_d.t, w=[wspT])
    st = k.push()
    cT = k.sb([128, 8, 2], F32, "cT")
    k.dma("sp", cT.t[:], cT_d.t, w=[cT])
    scT = k.sb([128, 8, 2], F32, "scT")
    k.act(scT.t[:], cT.t[:], AF.Silu, [cT], [scT])
    badac = k.sb([128, 48], F32, "badac")
    k.dma("sp", badac.t[:], badac_d.t, w=[badac])
    wst = [k.sb([128, 8, 512], F32, f"wst{i}") for i in range(4)]
    wada_v = wada_d.t.rearrange("(k p) n -> p k n", p=128)
    modrow = k.sb([2, 6 * D], F32, "modrow")
    for cb in range(12):
        wb = wst[cb % 4]
        k.dma("sp", wb.t[:], wada_v[:, :, cb * 512:(cb + 1) * 512], r=[wada_d], w=[wb])
        pb = pss[cb % 2]
        for kk in range(8):
            k.mm(pb.t[0:2, :], scT.t[:, kk, :], wb.t[:, kk, :], kk == 0, kk == 7, [scT, wb], [pb], inc=(kk == 7))
        k.copy("act", modrow.t[:, cb * 512:(cb + 1) * 512], pb.t[0:2, :], [pb], [modrow])
    pm = pss[2]
    for j in range(48):
        k.mm(pm.t[:, 2 * j:2 * j + 2], modrow.t[0:2, j * 128:(j + 1) * 128], cm.t[0:2, IDN, 0:2], True, True,
             [modrow, cm], [pm], inc=(j == 47))
    k.tt("dve", modc.t[:], pm.t[:, 0:96].rearrange("p (j m) -> p j m", m=2),
         bc(badac.t[:].unsqueeze(2), [128, 48, 2]), ALU.add, [pm, badac], [modc])
    tmp8 = k.sb([128, 8], F32, "tmp8")
    for (si, bi, m, g_i, b_i, sc_o, sh_o) in ((S1X, B1X, 0, 0, 1, 8, 0), (S1C, B1C, 1, 0, 1, 8, 0), (S2, B2, 0, 2, 3, 32, 24)):
        k.ts("dve", tmp8.t[:], modc.t[:, sc_o:sc_o + 8, m], 1.0, None, ALU.add, None, [modc], [tmp8])
        k.tt("dve", cols.t[:, si, :], tmp8.t[:], colp.t[:, g_i, :], ALU.mult, [tmp8, colp], [cols])
        k.tt("dve", cols.t[:, bi, :], tmp8.t[:], colp.t[:, b_i, :], ALU.mult, [tmp8, colp], [cols])
        k.tt("dve", cols.t[:, bi, :], cols.t[:, bi, :], modc.t[:, sh_o:sh_o + 8, m], ALU.add, [cols, modc], [cols])
    k.pop()
    k.dump("modc", modc, modc.t[:].rearrange("p j m -> p (j m)"), [128, 96])

    def make_row_psum(off, name, banks):
        dg = k.sb([128, 128], F32, name + "dg")
        for c in range(8):
            k.ts("dve", dg.t[:], cm.t[:, IDN, :], modc.t[:, off + c, 0:1], None, ALU.mult, None, [cm, modc], [dg])
            pb = banks[c // 4]
            k.mm(pb.t[:, (c % 4) * 128:(c % 4 + 1) * 128], ones_f.t[:], dg.t[:], True, True, [ones_f, dg], [pb], True)

    def make_row(off, name):
        row = k.sb([128, D], F32, name)
        dg = k.sb([128, 128], F32, name + "dg")
        for c in range(8):
            k.ts("dve", dg.t[:], cm.t[:, IDN, :], modc.t[:, off + c, 0:1], None, ALU.mult, None, [cm, modc], [dg])
            pb = pss[1 + (c // 4)]
            k.mm(pb.t[:, (c % 4) * 128:(c % 4 + 1) * 128], ones_f.t[:], dg.t[:], True, True, [ones_f, dg], [pb], True)
        k.copy("act", row.t[:, 0:512], pss[1].t[:], [pss[1]], [row])
        k.copy("act", row.t[:, 512:1024], pss[2].t[:], [pss[2]], [row])
        return row

    def load_w(dst, src_ap, k_chunks, eng="pool"):
        for kk in range(k_chunks):
            k.dma(eng, dst.t[:, kk, :], src_ap[kk * 128:(kk + 1) * 128, :], w=[dst], partial=True)

    def bias_rows(src_row_ap, n, scale, name):
        rows = k.sb([2, n], BF16, name)
        k.push()
        b32 = k.sb([1, n], F32, name + "32")
        k.dma("sp", b32.t[:], src_row_ap, w=[b32])
        if scale != 1.0:
            k.ts("dve", b32.t[:], b32.t[:], float(scale), None, ALU.mult, None, [b32], [b32])
        k.copy("dve", rows.t[0:1, :], b32.t[:], [b32], [rows])
        lo32 = k.sb([1, n], F32, name + "lo32")
        k.tt("dve", lo32.t[:], b32.t[:], rows.t[0:1, :], ALU.subtract, [b32, rows], [lo32])
        lo = k.sb([1, n], BF16, name + "lo")
        k.copy("dve", lo.t[:], lo32.t[:], [lo32], [lo])
        k.dma("sp", rows.t[1:2, :], lo.t[:], r=[lo], w=[rows])
        k.pop()
        return rows

    def rsqrt_act(out_buf, in_ap, in_buf, scale):
        k.act(lntmp.t[:], in_ap, AF.Ln, [in_buf, epsc], [lntmp], bias=epsc.t[:], scale=float(scale))
        k.act(out_buf.t[:], lntmp.t[:], AF.Exp, [lntmp], [out_buf], scale=-0.5)

    def ln_stats(src, tmp_st, tmp_mv, rs, nm):
        for i in range(2):
            k.op("dve", lambda i=i: nc.vector.bn_stats(out=tmp_st.t[:, i, :], in_=src.t[:, i * 512:(i + 1) * 512]), [src], [tmp_st])
        k.op("dve", lambda: nc.vector.bn_aggr(out=tmp_mv.t[:], in_=tmp_st.t[:].rearrange("p a b -> p (a b)")), [tmp_st], [tmp_mv])
        rsqrt_act(rs, tmp_mv.t[:, 1:2], tmp_mv, 1.0)
        k.ts("dve", nm.t[:], tmp_mv.t[:, 0:1], rs.t[:], -1.0, ALU.mult, ALU.mult, [tmp_mv, rs], [nm])

    def proj_tok(hT, W, c0, n, pb, brow=None, b0=0):
        for kk in range(8):
            last = (kk == 7 and brow is None)
            k.mm(pb.t[:, 0:n], hT.t[:, kk, :], W.t[:, kk, c0:c0 + n], kk == 0, last, [hT, W], [pb], inc=last)
        if brow is not None:
            k.mm(pb.t[:, 0:n], ones_bf.t[0:2, :], brow.t[0:2, b0:b0 + n], False, True, [ones_bf, brow], [pb], inc=True)

    def transpose_to(dstT, src, pb, eng):
        pv = pb.t[:].bitcast(BF16)
        for c in range(8):
            k.tr(pv[:, c * 128:(c + 1) * 128], src.t[:, c * 128:(c + 1) * 128], ident_bf.t[:], [src, ident_bf], [pb], inc=(c == 7))
        k.copy(eng, dstT.t[:].rearrange("p a t -> p (a t)"), pv, [pb], [dstT])

    class LN0:
        def __init__(self, nh=2, all_act=False):
            self.nh = nh
            self.all_act = all_act
            self.xt = [k.sb([128, D], F32, "xt") for _ in range(2)]
            self.xh = [k.sb([128, D], F32, "xh") for _ in range(nh)]
            self.hT = [k.sb([128, 8, 128], BF16, "hT") for _ in range(2)]
            self.st = k.sb([128, 2, 6], F32, "lnst")
            self.mv = k.sb([128, 2], F32, "lnmv")
            self.rs = k.sb([128, 1], F32, "lnrs")
            self.nm = k.sb([128, 1], F32, "lnnm")
            self.i = 0

        def run(self, src_buf, src_ap, si, bi, pbanks):
            s = self.i % 2
            self.i += 1
            xt, xh, hT = self.xt[s], self.xh[s % self.nh], self.hT[s]
            k.dma("sp", xt.t[:], src_ap, r=[src_buf], w=[xt])
            ln_stats(xt, self.st, self.mv, self.rs, self.nm)
            k.act(xh.t[:], xt.t[:], AF.Identity, [xt, self.rs, self.nm], [xh], bias=self.nm.t[:], scale=self.rs.t[:])
            for c in range(8):
                pb = pbanks[c // 4]
                k.tr(pb.t[:, (c % 4) * 128:(c % 4 + 1) * 128], xh.t[:, c * 128:(c + 1) * 128], cm.t[:, IDN, :],
                     [xh, cm], [pb], inc=(c % 4 == 3))
            for c in range(8):
                pb = pbanks[c // 4]
                src = pb.t[:, (c % 4) * 128:(c % 4 + 1) * 128]
                if c < 4 and not self.all_act:
                    k.ts("dve", hT.t[:, c, :], src, cols.t[:, si, c:c + 1], cols.t[:, bi, c:c + 1], ALU.mult, ALU.add, [pb, cols], [hT])
                else:
                    k.act(hT.t[:, c, :], src, AF.Identity, [pb, cols], [hT], bias=cols.t[:, bi, c:c + 1], scale=cols.t[:, si, c:c + 1])
            return xh, hT

    gmgrow = k.sb([128, D], F32, "gmgrow")
    k.dma("sp", gmgrow.t[:], rows_d.t[3:4, :].partition_broadcast(128), w=[gmgrow])
    biasM = k.sb([128, 8, 128], F32, "biasM")
    k.dma("sp", biasM.t[:].rearrange("p g c -> p (g c)"), rows_d.t[4:5, :].partition_broadcast(128), w=[biasM])
    gv = k.sb([128, D], F32, "gv")
    wsp_v = gv.t[:].rearrange("p (g q) -> p g q", g=8)
    k.dma("sp", wsp_v, wsp_d.t, w=[gv])
    rsum = k.sb([128, 8], F32, "rsum")
    k.op("dve", lambda: nc.vector.reduce_sum(out=rsum.t[:], in_=wsp_v, axis=mybir.AxisListType.X), [gv], [rsum])
    bspc = k.sb([128, 8], F32, "bspc")
    k.dma("sp", bspc.t[:], bsp_d.t, w=[bspc])
    k.tt("dve", biasM.t[:], biasM.t[:], bc(rsum.t[:].unsqueeze(2), [128, 8, 128]), ALU.mult, [biasM, rsum], [biasM])
    k.tt("dve", biasM.t[:], biasM.t[:], bc(bspc.t[:].unsqueeze(2), [128, 8, 128]), ALU.add, [biasM, bspc], [biasM])
    dtb_rows = bias_rows(small_d.t[0:1, 0:32], 32, 1.0, "dtb")
    bg2_rows = bias_rows(bgate_d.t[0:1, D:2 * D], D, 1.0, "bg2")

    if NFILL > 0:
        k.fill = fill_spec
    ln0 = LN0(2)
    pre = [k.sb([128, 12, 132], BF16, f"pre{i}") for i in range(3)]
    dgw = k.sb([128, 12, 5, 128], BF16, "dgw")
    for cc in range(12):
        for tap in range(5):
            k.ts("pool" if (cc + tap) % 2 else "dve", dgw.t[:, cc, tap, :], cm.t[:, IDN, :], convw.t[:, cc, tap:tap + 1], None,
                 ALU.mult, None, [cm, convw], [dgw])
    xbcT = k.sb([128, 12, 128], BF16, "xbcT")
    xs_tok = k.sb([128, D], BF16, "xs_tok")
    b_tok = k.sb([128, 256], BF16, "b_tok")
    a_sb = k.sb([128, 32], F32, "a_sb")
    ex = k.sb([128, 96], F32, "ex")
    ddv = k.sb([128, 32], F32, "ddv")
    xdd = k.sb([128, D], BF16, "xdd")
    hp_bf = k.sb([128, D], BF16, "hp_bf")
    e_t = k.sb([128, 32], F32, "e_t")
    decf1 = k.sb([128, 16], F32, "decf1")
    gu = k.sb([128, D], F32, "gu")
    sf1 = gu
    vhat = k.sb([128, D], BF16, "vhat")
    g2 = k.sb([128, D], F32, "g2")
    ygm = gv
    ygm_bf = k.sb([128, D], BF16, "ygm_bf")
    ygmT = k.sb([128, 8, 128], BF16, "ygmT")
    G2 = k.sb([128, D], BF16, "G2")
    vst = k.sb([128, 2, 6], F32, "vst")
    vmv = k.sb([128, 2], F32, "vmv")
    vrs = k.sb([128, 1], F32, "vrs")
    vnm = k.sb([128, 1], F32, "vnm")

    k.memset("dve", run_f.t[:], 0.0, [run_f])
    k.memset("dve", run_b.t[:], 0.0, [run_b])
    print("SBUF free in pass 1:", nc.sbuf_bytes_remaining)

    def small_exps(dt_ap, dt_buf):
        k.tt("dve", a_sb.t[:], dt_ap, arow.t[:], ALU.mult, [dt_buf, arow], [a_sb])
        pb = pss[4]
        specs = ((MLE, 0), (MGE, 16), (MGT, 0), (MLT, 16))
        for i, (mi, ao) in enumerate(specs):
            k.mm(pb.t[:, i * 16:(i + 1) * 16], cm.t[:, mi, :], a_sb.t[:, ao:ao + 16], True, True, [cm, a_sb], [pb], inc=False)
        k.mm(pb.t[:, 64:96], ones_f.t[:], a_sb.t[:, 0:32], True, True, [ones_f, a_sb], [pb], inc=True)
        k.act(ex.t[:], pb.t[:, 0:96], AF.Exp, [pb], [ex])

    order1 = [("c", 1), ("c", 0)] + [("x", cc_) for cc_ in range(NCH_RUN - 1, -1, -1)]
    lncache = {}

    def ln_for(i):
        if i >= len(order1) or i in lncache:
            return
        kind, cc_ = order1[i]
        if kind == "c":
            lncache[i] = ln0.run(ctx_d, ctx_d.t[cc_ * 128:(cc_ + 1) * 128, :], S1C, B1C, (pss[0], pss[1]))
        else:
            lncache[i] = ln0.run(x_d, x_d.t[cc_ * 128:(cc_ + 1) * 128, :], S1X, B1X, (pss[0], pss[1]))

    def s1(seq_buf, seq_ap, c, n, si, bi, slot_of, is_ctx):
        oi = order1.index(("c" if is_ctx else "x", c))
        ln_for(oi)
        xh, hT = lncache.pop(oi)
        for cc in range(12):
            pb = pss[2 + cc // 4]
            for kk in range(8):
                k.mm(pb.t[:, (cc % 4) * 128:(cc % 4 + 1) * 128], w1.t[:, kk, W_XBC + cc * 128:W_XBC + (cc + 1) * 128],
                     hT.t[:, kk, :], kk == 0, kk == 7, [w1, hT], [pb], inc=(kk == 7))
        me = pre[slot_of(c)]
        for q in range(3):
            pv = pss[2 + q].t[:].rearrange("p (a t) -> p a t", a=4)
            k.act(me.t[:, q * 4:(q + 1) * 4, 2:130], pv, AF.Copy, [pss[2 + q]], [me])
            KH = os.environ.get("KHALO", "")
            if c + 1 < n and KH != "skipL":
                k.copy("dve", pre[slot_of(c + 1)].t[:, q * 4:(q + 1) * 4, 0:2], pv[:, :, 126:128], [pss[2 + q]], [pre[slot_of(c + 1)]])
            if c - 1 >= 0 and KH != "skipR":
                k.copy("dve", pre[slot_of(c - 1)].t[:, q * 4:(q + 1) * 4, 130:132], pv[:, :, 0:2], [pss[2 + q]], [pre[slot_of(c - 1)]])
        if c == n - 1:
            k.memset("dve", me.t[:, :, 130:132], 0.0, [me])
        if c == 0:
            k.memset("dve", me.t[:, :, 0:2], 0.0, [me])
        yield "halo"
        pb = pss[5]
        proj_tok(hT, w1, W_DT, 32, pb, dtb_rows, 0)
        k.act(e_t.t[:], pb.t[:, 0:32], AF.Exp, [pb], [e_t])
        dts = dtc_all if is_ctx else dt_all
        k.act(dts.t[:, c, :], e_t.t[:], AF.Ln, [e_t], [dts], bias=1.0)
        if is_ctx or os.environ.get("KSKIP_GM"):
            ln_for(oi + 1)
            return
        yield
        for half in range(2):
            pb = pss[5 + half]
            proj_tok(hT, w1, W_U + half * 512, 512, pb)
            k.act(gu.t[:, half * 512:(half + 1) * 512], pb.t[:], AF.Gelu_apprx_tanh, [pb], [gu])
        yield
        for half in range(2):
            pb = pss[5 + half]
            proj_tok(hT, w1, W_V + half * 512, 512, pb)
            k.act(gv.t[:, half * 512:(half + 1) * 512], pb.t[:], AF.Gelu_apprx_tanh, [pb], [gv])
        yield
        ln_stats(gv, vst, vmv, vrs, vnm)
        k.act(vhat.t[:], gv.t[:], AF.Identity, [gv, vrs, vnm], [vhat], bias=vnm.t[:], scale=vrs.t[:])
        for half in range(2):
            pb = pss[5 + half]
            proj_tok(hT, w1, W_G2 + half * 512, 512, pb, bg2_rows, half * 512)
            k.act(g2.t[:, half * 512:(half + 1) * 512], pb.t[:], AF.Sigmoid, [pb], [g2])
        yield
        ln_for(oi + 1)
        yield
        for g in range(8):
            pb = pss[5 + g // 4]
            k.mm(pb.t[:, (g % 4) * 128:(g % 4 + 1) * 128], wspT.t[:, g, :], vhat.t[:, g * 128:(g + 1) * 128],
                 True, True, [wspT, vhat], [pb], inc=(g % 4 == 3))
        bM = biasM.t[:].rearrange("p g c -> p (g c)")
        for half in range(2):
            sl = slice(half * 512, (half + 1) * 512)
            pb = pss[5 + half]
            k.tt("dve", ygm.t[:, sl], pb.t[:], gmgrow.t[:, sl], ALU.mult, [pb, gmgrow], [ygm])
            k.tt("pool", ygm.t[:, sl], ygm.t[:, sl], bM[:, sl], ALU.add, [ygm, biasM], [ygm])
            k.tt("pool", ygm_bf.t[:, sl], ygm.t[:, sl], gu.t[:, sl], ALU.mult, [ygm, gu], [ygm_bf])
        yield
        transpose_to(ygmT, ygm_bf, pss[5], "dve")
        for half in range(2):
            pb = pss[5 + half]
            for kk in range(8):
                k.mm(pb.t[:], ygmT.t[:, kk, :], wgm.t[:, kk, half * 512:(half + 1) * 512], kk == 0, kk == 7,
                     [ygmT, wgm], [pb], inc=(kk == 7))
            k.tt("dve", G2.t[:, half * 512:(half + 1) * 512], pb.t[:], g2.t[:, half * 512:(half + 1) * 512], ALU.mult, [pb, g2], [G2])
        k.dma("pool", g2_s[c].t, G2.t[:], r=[G2], w=[g2_s[c]])

    def post(c, n, slot_of, is_ctx):
        if os.environ.get("KSKIP_POST") and not is_ctx:
            return
        me = pre[slot_of(c)]
        for cc in range(12):
            pb = pss[2 + cc // 4]
            for tap in range(5):
                k.mm(pb.t[:, (cc % 4) * 128:(cc % 4 + 1) * 128], dgw.t[:, cc, tap, :], me.t[:, cc, tap:tap + 128],
                     tap == 0, tap == 4, [dgw, me], [pb], inc=(tap == 4 and cc % 4 == 3))
        for cc in range(12):
            pb = pss[2 + cc // 4]
            k.act(xbcT.t[:, cc, :], pb.t[:, (cc % 4) * 128:(cc % 4 + 1) * 128], AF.Silu, [pb, convb], [xbcT], bias=convb.t[:, cc:cc + 1])
        yield

        pv0 = pss[0].t[:].bitcast(BF16)
        pv1 = pss[1].t[:].bitcast(BF16)
        for cix in range(8):
            k.tr(pv0[:, cix * 128:(cix + 1) * 128], xbcT.t[:, cix, :], ident_bf.t[:], [xbcT, ident_bf], [pss[0]], inc=(cix == 7))
        for cix in range(2):
            k.tr(pv1[:, cix * 128:(cix + 1) * 128], xbcT.t[:, 8 + cix, :], ident_bf.t[:], [xbcT, ident_bf], [pss[1]], inc=(cix == 1))
        k.copy("dve", xs_tok.t[:], pv0, [pss[0]], [xs_tok])
        k.copy("act", b_tok.t[:], pv1[:, 0:256], [pss[1]], [b_tok])
        if not is_ctx:
            k.dma("pool", xs_s[c].t, xs_tok.t[:], r=[xs_tok], w=[xs_s[c]])
            k.dma("pool", bt_s[c].t, b_tok.t[:], r=[b_tok], w=[bt_s[c]])
            k.dma("pool", bc_s[c].t, xbcT.t[:, 8:12, :], r=[xbcT], w=[bc_s[c]])
        yield
        dts = dtc_all if is_ctx else dt_all
        small_exps(dts.t[:, c, :], dts)
        k.tt("dve", ddv.t[:, 16:32], ex.t[:, 48:64], dts.t[:, c, 16:32], ALU.mult, [ex, dts], [ddv])
        if is_ctx:
            k.tt("dve", ddv.t[:, 0:16], ex.t[:, 32:48], dts.t[:, c, 0:16], ALU.mult, [ex, dts], [ddv])
        dirs = ((1, run_b),) + (((0, run_f),) if is_ctx else ())
        for d, run in dirs:
            yield
            k.tt("dve", xdd.t[:].rearrange("p (h q) -> p h q", h=H), xs_tok.t[:].rearrange("p (h q) -> p h q", h=H),
                 bc(ddv.t[:, d * 16:(d + 1) * 16].unsqueeze(2), [128, H, P]), ALU.mult, [xs_tok, ddv], [xdd])
            for g in range(2):
                pb = pss[2 + g]
                k.mm(pb.t[:], b_tok.t[:, g * 128:(g + 1) * 128], xdd.t[:, g * 512:(g + 1) * 512], True, True,
                     [b_tok, xdd], [pb], inc=True)
            dec = ex.t[:, 64 + d * 16:80 + d * 16]
            if is_ctx and d == 0:
                if c == 1:
                    for g in range(2):
                        k.copy("act", sf1.t[:, g * 512:(g + 1) * 512], pss[2 + g].t[:], [pss[2 + g]], [sf1])
                    k.copy("dve", decf1.t[:], dec, [ex], [decf1])
                else:
                    for g in range(2):
                        sl = slice(g * 512, (g + 1) * 512)
                        k.tt("dve", run_f.t[:, sl].rearrange("p (h q) -> p h q", h=8), pss[2 + g].t[:].rearrange("p (h q) -> p h q", h=8),
                             bc(decf1.t[:, g * 8:(g + 1) * 8].unsqueeze(2), [128, 8, P]), ALU.mult, [pss[2 + g], decf1], [run_f])
                        k.tt("dve", run_f.t[:, sl], run_f.t[:, sl], sf1.t[:, sl], ALU.add, [run_f, sf1], [run_f])
                continue
            if not is_ctx:
                k.copy("dve", hp_bf.t[:], run.t[:], [run], [hp_bf])
                k.dma("pool", hp_s[c].t, hp_bf.t[:], r=[hp_bf], w=[hp_s[c]])
            k.tt("pool", run.t[:].rearrange("p (h q) -> p h q", h=H), run.t[:].rearrange("p (h q) -> p h q", h=H),
                 bc(dec.unsqueeze(2), [128, H, P]), ALU.mult, [run, ex], [run])
            for g in range(2):
                sl = slice(g * 512, (g + 1) * 512)
                k.tt("dve", run.t[:, sl], run.t[:, sl], pss[2 + g].t[:], ALU.add, [run, pss[2 + g]], [run])

    def run_all(g):
        for _ in g:
            pass

    def interleave(ga, gb):
        da = db = False
        while not (da and db):
            if not da:
                try:
                    next(ga)
                except StopIteration:
                    da = True
            if not db:
                try:
                    next(gb)
                except StopIteration:
                    db = True

    dtc_all = k.sb([128, 2, 32], F32, "dtcall")
    for c in (1, 0):
        run_all(s1(ctx_d, ctx_d.t, c, 2, S1C, B1C, lambda cc: cc % 3, True))
        if c + 1 < 2:
            run_all(post(c + 1, 2, lambda cc: cc % 3, True))
    run_all(post(0, 2, lambda cc: cc % 3, True))
    k.dump("s_f", run_f, run_f.t[:], [128, D])
    k.dump("s_b", run_b, run_b.t[:], [128, D])
    NR = NCH_RUN
    KSTOP = int(os.environ.get("KSTOP", "1000"))
    steps = 0
    for c in range(NR - 1, -1, -1):
        if steps >= KSTOP:
            break
        ga = s1(x_d, x_d.t, c, NR, S1X, B1X, lambda cc: cc % 3, False)
        next(ga)
        steps += 1
        if c + 1 < NR:
            interleave(ga, post(c + 1, NR, lambda cc: cc % 3, False))
            steps += 1
        else:
            run_all(ga)
    if steps < KSTOP:
        run_all(post(0, NR, lambda cc: cc % 3, False))
    k.dump("dt_all", dt_all, dt_all.t[:].rearrange("p c d -> p (c d)"), [128, NCH * 32])
    k.pop()

    if KPASS < 2:
        k.barrier()
        k.es.close()
        return nc, k.dbg
    st = k.push()
    bg1_rows = bias_rows(bgate_d.t[0:1, 0:D], D, 1.0, "bg1")
    b0_rows = bias_rows(rows_d.t[1:2, :], D, ALPHA, "b0r")
    a0row = k.sb([128, D], F32, "a0row")
    k.dma("sp", a0row.t[:], rows_d.t[0:1, :].partition_broadcast(128), w=[a0row])
    k.ts("dve", a0row.t[:], a0row.t[:], float(ALPHA), None, ALU.mult, None, [a0row], [a0row])
    ngrow = k.sb([128, D], F32, "ngrow")
    k.dma("sp", ngrow.t[:], rows_d.t[2:3, :].partition_broadcast(128), w=[ngrow])
    make_row_psum(16, "g1row", (pss[4], pss[5]))
    wout = k.sb([128, 8, D], BF16, "wout")
    load_w(wout, wout_d.t, 8)
    w2 = k.sb([128, 8, 2048], BF16, "w2")
    for kk in range(8):
        rs_ = slice(kk * 128, (kk + 1) * 128)
        k.dma("pool", w2.t[:, kk, 0:1024], win_d.t[rs_, O_Z:O_Z + D], r=[win_d], w=[w2], partial=True)
        k.dma("pool", w2.t[:, kk, 1024:2048], win_d.t[rs_, O_G:O_G + D], r=[win_d], w=[w2], partial=True)
    wssd = k.sb([128, 8, D], BF16, "wssd")
    load_w(wssd, wssd_d.t, 8)

    def scale_wout():
        for kk in range(8):
            for half in range(2):
                sl = slice(half * 512, (half + 1) * 512)
                k.tt("dve", wout.t[:, kk, sl], wout.t[:, kk, sl], pss[4 + half].t[:], ALU.mult, [wout, pss[4 + half]], [wout])

    ln0 = LN0(all_act=True)
    xs_l = [k.sb([128, D], BF16, "xs_l") for _ in range(2)]
    bt_l = [k.sb([128, 256], BF16, "bt_l") for _ in range(2)]
    bc_l = [k.sb([128, 4, 128], BF16, "bc_l") for _ in range(2)]
    hp_l = [k.sb([128, D], BF16, "hp_l") for _ in range(2)]
    g2_l = [k.sb([128, D], BF16, "g2_l") for _ in range(2)]
    a_sb = k.sb([128, 32], F32, "a_sb2")
    ex = k.sb([128, 96], F32, "ex2")
    ddv = k.sb([128, 32], F32, "ddv2")
    sz_l = [k.sb([128, D], BF16, "sz") for _ in range(2)]
    g1_l = [k.sb([128, D], BF16, "g1") for _ in range(2)]
    Rf = k.sb([128, H, 128], F32R, "Rf")
    Rb = Rf
    Ef = k.sb([128, H, 128], F32, "Ef")
    cbm = k.sb([128, 2, 2, 128], F32, "cbm")
    wgt = [k.sb([128, H, 128], BF16, f"wgt{d}") for d in range(2)]
    xdt = [k.sb([128, D], BF16, f"xdt{d}") for d in range(2)]
    xdd = k.sb([128, D], BF16, "xdd2")
    runf_bf = k.sb([128, D], BF16, "runf_bf")
    t1 = k.sb([128, D], F32, "t1")
    t2 = k.sb([128, D], F32, "t2")
    hh = k.sb([128, D], F32, "hh")
    sq = t2
    ss = k.sb([128, 1], F32, "ss")
    rstd = k.sb([128, 1], F32, "rstd")
    yg = k.sb([128, D], BF16, "yg")
    ygT = k.sb([128, 8, 128], BF16, "ygT")
    m1 = hh
    mg = k.sb([128, D], BF16, "mg")
    mgT = k.sb([128, 8, 128], BF16, "mgT")
    r1 = t1
    x1h = [k.sb([128, D], F32, "x1h") for _ in range(2)]
    lst = k.sb([128, 2, 6], F32, "lst")
    lmv = k.sb([128, 2], F32, "lmv")
    lrs = k.sb([128, 1], F32, "lrs")
    lnm = k.sb([128, 1], F32, "lnm")

    k.copy("act", runf_bf.t[:], run_f.t[:], [run_f], [runf_bf])
    print("SBUF free in pass 2:", nc.sbuf_bytes_remaining)

    def h3(ap, h=H):
        return ap.rearrange("p (h q) -> p h q", h=h)

    KSTOP2 = int(os.environ.get("KSTOP2", "1000"))
    ln2cache = {}

    def front2a(c):
        ln2cache[c] = ln0.run(x_d, x_d.t[c * 128:(c + 1) * 128, :], S1X, B1X, (pss[0], pss[1]))

    def front2b(c):
        xh, hT = ln2cache[c]
        sz, g1 = sz_l[c % 2], g1_l[c % 2]
        for half in range(2):
            pb = pss[half]
            proj_tok(hT, w2, half * 512, 512, pb)
            k.act(sz.t[:, half * 512:(half + 1) * 512], pb.t[:], AF.Silu, [pb], [sz])
        for half in range(2):
            pb = pss[half]
            proj_tok(hT, w2, 1024 + half * 512, 512, pb, bg1_rows, half * 512)
            k.act(g1.t[:, half * 512:(half + 1) * 512], pb.t[:], AF.Sigmoid, [pb], [g1])

    front2a(0)
    scale_wout()
    front2b(0)
    for c in range(min(NR, KSTOP2)):
        s = c % 2
        xs, bt, bcl, hp, g2l = xs_l[s], bt_l[s], bc_l[s], hp_l[s], g2_l[s]
        k.dma("sp", xs.t[:], xs_s[c].t, r=[xs_s[c]], w=[xs])
        k.dma("sp", bt.t[:], bt_s[c].t, r=[bt_s[c]], w=[bt])
        k.dma("sp", bcl.t[:], bc_s[c].t, r=[bc_s[c]], w=[bcl])
        k.dma("sp", hp.t[:], hp_s[c].t, r=[hp_s[c]], w=[hp])
        k.dma("sp", g2l.t[:], g2_s[c].t, r=[g2_s[c]], w=[g2l])
        xh, hT = ln2cache.pop(c)
        sz, g1 = sz_l[s], g1_l[s]
        xo = x1h[s]
        k.tt("pool", xo.t[:], xh.t[:], a0row.t[:], ALU.mult, [xh, a0row], [xo])
        if c + 1 < min(NR, KSTOP2):
            front2a(c + 1)
        dtc = dt_all.t[:, c, :]
        k.tt("dve", a_sb.t[:], dtc, arow.t[:], ALU.mult, [dt_all, arow], [a_sb])
        pb = pss[6]
        for i, (mi, ao) in enumerate(((MLE, 0), (MGE, 16), (MGT, 0), (MLT, 16))):
            k.mm(pb.t[:, i * 16:(i + 1) * 16], cm.t[:, mi, :], a_sb.t[:, ao:ao + 16], True, True, [cm, a_sb], [pb], inc=False)
        k.mm(pb.t[:, 64:96], ones_f.t[:], a_sb.t[:, 0:32], True, True, [ones_f, a_sb], [pb], inc=True)
        k.act(ex.t[:], pb.t[:, 0:96], AF.Exp, [pb], [ex])
        pb = pss[6]
        for g in range(2):
            k.mm(pb.t[:, 256 + g * 128:256 + (g + 1) * 128], bcl.t[:, g, :], bcl.t[:, 2 + g, :], True, True, [bcl], [pb], inc=(g == 1))
        pv = pb.t[:, 256:512].rearrange("p (g i) -> p g i", g=2)
        k.tt("dve", cbm.t[:, 0, :, :], pv, bc(cm.t[:, MLE, :].unsqueeze(1), [128, 2, 128]), ALU.mult, [pb, cm], [cbm])
        k.tt("dve", cbm.t[:, 1, :, :], pv, bc(cm.t[:, MGE, :].unsqueeze(1), [128, 2, 128]), ALU.mult, [pb, cm], [cbm])
        for d in range(2):
            k.tt("pool", h3(xdt[d].t[:]), h3(xs.t[:]), bc(dtc[:, d * 16:(d + 1) * 16].unsqueeze(2), [128, H, P]),
                 ALU.mult, [xs, dt_all], [xdt[d]])
        for d, (R, Lm) in enumerate(((Rf, mgt_r), (Rb, mlt_r))):
            k.tt("pool", R.t[:], bc(a_sb.t[:, d * 16:(d + 1) * 16].unsqueeze(2), [128, H, 128]),
                 bc(cm.t[:, MGE if d else MLE, :].unsqueeze(1), [128, H, 128]), ALU.mult, [a_sb, cm], [R])
            R2 = R.t[:].rearrange("p h i -> p (h i)")
            E2 = Ef.t[:].rearrange("p h i -> p (h i)")
            for q in range(4):
                pb = pss[2 + q]
                k.mm(pb.t[:], Lm.t[:], R2[:, q * 512:(q + 1) * 512], True, True, [Lm, R], [pb], inc=True)
                k.act(E2[:, q * 512:(q + 1) * 512], pb.t[:], AF.Exp, [pb], [Ef])
            for g in range(2):
                k.tt("dve", wgt[d].t[:, g * 8:(g + 1) * 8, :], Ef.t[:, g * 8:(g + 1) * 8, :],
                     bc(cbm.t[:, d, g, :].unsqueeze(1), [128, 8, 128]), ALU.mult, [Ef, cbm], [wgt[d]])
        if c + 1 < min(NR, KSTOP2):
            front2b(c + 1)
        for h in range(H):
            pb = pss[h // 8]
            o = pb.t[:, (h % 8) * 64:(h % 8 + 1) * 64]
            cs = slice(h * 64, (h + 1) * 64)
            k.mm(o, wgt[0].t[:, h, :], xdt[0].t[:, cs], True, False, [wgt[0], xdt[0]], [pb], inc=False)
            k.mm(o, wgt[1].t[:, h, :], xdt[1].t[:, cs], False, False, [wgt[1], xdt[1]], [pb], inc=False)
            k.mm(o, dh.t[:, h, :], xs.t[:, cs], False, True, [dh, xs], [pb], inc=(h % 8 == 7))
        for g in range(2):
            k.mm(pss[2 + g].t[:], bcl.t[:, 2 + g, :], runf_bf.t[:, g * 512:(g + 1) * 512], True, True, [bcl, runf_bf], [pss[2 + g]], inc=True)
            k.mm(pss[4 + g].t[:], bcl.t[:, 2 + g, :], hp.t[:, g * 512:(g + 1) * 512], True, True, [bcl, hp], [pss[4 + g]], inc=True)
        for g in range(2):
            sl = slice(g * 512, (g + 1) * 512)
            k.tt("dve", h3(t1.t[:, sl], 8), h3(pss[2 + g].t[:], 8), bc(ex.t[:, g * 8:(g + 1) * 8].unsqueeze(2), [128, 8, P]),
                 ALU.mult, [pss[2 + g], ex], [t1])
            k.tt("dve", h3(t2.t[:, sl], 8), h3(pss[4 + g].t[:], 8), bc(ex.t[:, 16 + g * 8:16 + (g + 1) * 8].unsqueeze(2), [128, 8, P]),
                 ALU.mult, [pss[4 + g], ex], [t2])
            k.tt("pool", t1.t[:, sl], t1.t[:, sl], t2.t[:, sl], ALU.add, [t1, t2], [t1])
            k.tt("dve", t1.t[:, sl], t1.t[:, sl], pss[g].t[:], ALU.add, [t1, pss[g]], [t1])
        if c == 0:
            k.dump("y0", t1, t1.t[:], [128, D])
        k.tt("dve", hh.t[:], t1.t[:], sz.t[:], ALU.mult, [t1, sz], [hh])
        k.act(sq.t[:], hh.t[:], AF.Square, [hh], [sq, ss], accum=ss.t[:])
        rsqrt_act(rstd, ss.t[:], ss, 1.0 / D)
        k.stt(yg.t[:], hh.t[:], rstd.t[:], ngrow.t[:], ALU.mult, ALU.mult, [hh, rstd, ngrow], [yg])

        transpose_to(ygT, yg, pss[6], "act")
        k.tt("dve", ddv.t[:, 0:16], ex.t[:, 32:48], dtc[:, 0:16], ALU.mult, [ex, dt_all], [ddv])
        k.tt("pool", h3(xdd.t[:]), h3(xs.t[:]), bc(ddv.t[:, 0:16].unsqueeze(2), [128, H, P]), ALU.mult, [xs, ddv], [xdd])
        for g in range(2):
            k.mm(pss[2 + g].t[:], bt.t[:, g * 128:(g + 1) * 128], xdd.t[:, g * 512:(g + 1) * 512], True, True, [bt, xdd], [pss[2 + g]], inc=True)
        k.tt("pool", h3(run_f.t[:]), h3(run_f.t[:]), bc(ex.t[:, 64:80].unsqueeze(2), [128, H, P]), ALU.mult, [run_f, ex], [run_f])
        for g in range(2):
            sl = slice(g * 512, (g + 1) * 512)
            k.tt("dve", run_f.t[:, sl], run_f.t[:, sl], pss[2 + g].t[:], ALU.add, [run_f, pss[2 + g]], [run_f])
        k.copy("act", runf_bf.t[:], run_f.t[:], [run_f], [runf_bf])
        for half in range(2):
            sl = slice(half * 512, (half + 1) * 512)
            pb = pss[4 + half]
            for kk in range(8):
                k.mm(pb.t[:], ygT.t[:, kk, :], wssd.t[:, kk, sl], kk == 0, kk == 7, [ygT, wssd], [pb], inc=(kk == 7))
            k.tt("dve", m1.t[:, sl], pb.t[:], g1.t[:, sl], ALU.mult, [pb, g1], [m1])
            k.tt("dve", mg.t[:, sl], m1.t[:, sl], g2l.t[:, sl], ALU.add, [m1, g2l], [mg])

        transpose_to(mgT, mg, pss[6], "act")
        for half in range(2):
            sl = slice(half * 512, (half + 1) * 512)
            pb = pss[half]
            for kk in range(8):
                k.mm(pb.t[:], mgT.t[:, kk, :], wout.t[:, kk, sl], kk == 0, False, [mgT, wout], [pb], inc=False)
            k.mm(pb.t[:], ones_bf.t[0:2, :], b0_rows.t[0:2, sl], False, True, [ones_bf, b0_rows], [pb], inc=True)
            k.tt("dve", r1.t[:, sl], xo.t[:, sl], pb.t[:], ALU.add, [xo, pb], [r1])
        if c == 0:
            k.dump("r1", r1, r1.t[:], [128, D])
        ln_stats(r1, lst, lmv, lrs, lnm)
        k.act(xo.t[:], r1.t[:], AF.Identity, [r1, lrs, lnm], [xo], bias=lnm.t[:], scale=lrs.t[:])
        k.dma("pool", x1_s[c].t, xo.t[:], r=[xo], w=[x1_s[c]])
    k.pop()

    k.fill = None
    if KPASS < 3:
        k.barrier()
        k.es.close()
        return nc, k.dbg
    st = k.push()
    b1_rows = bias_rows(rows_d.t[6:7, :], D, ALPHA, "b1r")
    a1row = k.sb([128, D], F32, "a1row")
    k.dma("sp", a1row.t[:], rows_d.t[5:6, :].partition_broadcast(128), w=[a1row])
    k.ts("dve", a1row.t[:], a1row.t[:], float(ALPHA), None, ALU.mult, None, [a1row], [a1row])
    l2g = k.sb([128, D], F32, "l2g")
    k.dma("sp", l2g.t[:], rows_d.t[7:8, :].partition_broadcast(128), w=[l2g])
    l2b = k.sb([128, D], F32, "l2b")
    k.dma("sp", l2b.t[:], rows_d.t[8:9, :].partition_broadcast(128), w=[l2b])
    make_row_psum(40, "g2row", (pss[6], pss[7]))
    wf1 = k.sb([128, 8, DFF], BF16, "wf1")
    wf3 = k.sb([128, 8, DFF], BF16, "wf3")
    wf2 = k.sb([128, NFF, D], BF16, "wf2")
    FG = ((0, 6), (6, 12), (12, 17), (17, 22))
    wf1g = [Buf(wf1.t) for _ in FG]
    wf3g = [Buf(wf3.t) for _ in FG]
    for gi, (f0, f1) in enumerate(FG):
        for kk in range(8):
            k.dma("pool", wf1.t[:, kk, f0 * 128:f1 * 128], wff1_d.t[kk * 128:(kk + 1) * 128, f0 * 128:f1 * 128], w=[wf1g[gi]], partial=True)
        for kk in range(8):
            k.dma("pool", wf3.t[:, kk, f0 * 128:f1 * 128], wff3_d.t[kk * 128:(kk + 1) * 128, f0 * 128:f1 * 128], w=[wf3g[gi]], partial=True)
    load_w(wf2, wff2_d.t, NFF)

    def fgrp(f):
        return [gi for gi, (f0, f1) in enumerate(FG) if f0 <= f < f1][0]

    def scale_wf2():
        for kk in range(NFF):
            for half in range(2):
                sl = slice(half * 512, (half + 1) * 512)
                k.tt("dve", wf2.t[:, kk, sl], wf2.t[:, kk, sl], pss[6 + half].t[:], ALU.mult, [wf2, pss[6 + half]], [wf2])
    TB = 2
    NB = NR // TB
    x1t = [[k.sb([128, D], F32, "x1t") for _ in range(TB)] for _ in range(2)]
    xmT = [k.sb([128, 8, TB * 128], BF16, "xmT") for _ in range(2)]
    hid = k.sb([128, NFF, TB * 128], BF16, "hid")
    sl1 = [k.sb([128, TB * 128], F32, "sl1") for _ in range(2)]
    fst = k.sb([128, 2, 6], F32, "fst")
    fmv = k.sb([128, 2], F32, "fmv")
    frs = k.sb([128, 1], F32, "frs")
    fnm = k.sb([128, 1], F32, "fnm")
    NT = TB * 128
    print("SBUF free in pass 3:", nc.sbuf_bytes_remaining)

    def ffn_front(b):
        s = b % 2
        for t in range(TB):
            c = b * TB + t
            xt = x1t[s][t]
            k.dma("sp", xt.t[:], x1_s[c].t, r=[x1_s[c]], w=[xt])
            for cc in range(8):
                pb = pss[cc // 4]
                k.tr(pb.t[:, (cc % 4) * 128:(cc % 4 + 1) * 128], xt.t[:, cc * 128:(cc + 1) * 128], cm.t[:, IDN, :],
                     [xt, cm], [pb], inc=(cc % 4 == 3))
            for cc in range(8):
                pb = pss[cc // 4]
                src = pb.t[:, (cc % 4) * 128:(cc % 4 + 1) * 128]
                dst = xmT[s].t[:, cc, t * 128:(t + 1) * 128]
                if cc < 4:
                    k.ts("dve", dst, src, cols.t[:, S2, cc:cc + 1], cols.t[:, B2, cc:cc + 1], ALU.mult, ALU.add, [pb, cols], [xmT[s]])
                else:
                    k.act(dst, src, AF.Identity, [pb, cols], [xmT[s]], bias=cols.t[:, B2, cc:cc + 1], scale=cols.t[:, S2, cc:cc + 1])
            k.tt("pool", xt.t[:], xt.t[:], a1row.t[:], ALU.mult, [xt, a1row], [xt])

    def ffn_w13(b):
        s = b % 2
        for f in range(NFF):
            p1 = pss[2 + (f % 2) * 2]
            p3 = pss[3 + (f % 2) * 2]
            for kk in range(8):
                k.mm(p1.t[:, 0:NT], wf1.t[:, kk, f * 128:(f + 1) * 128], xmT[s].t[:, kk, :], kk == 0, kk == 7, [wf1g[fgrp(f)], xmT[s]], [p1], inc=(kk == 7))
            for kk in range(8):
                k.mm(p3.t[:, 0:NT], wf3.t[:, kk, f * 128:(f + 1) * 128], xmT[s].t[:, kk, :], kk == 0, kk == 7, [wf3g[fgrp(f)], xmT[s]], [p3], inc=(kk == 7))
            sv = sl1[f % 2]
            k.act(sv.t[:], p1.t[:, 0:NT], AF.Silu, [p1], [sv])
            k.tt("dve", hid.t[:, f, :], sv.t[:], p3.t[:, 0:NT], ALU.mult, [sv, p3], [hid])

    def ffn_w2(b):
        s = b % 2
        for t in range(TB):
            c = b * TB + t
            xt = x1t[s][t]
            r2 = xt
            for half in range(2):
                sl = slice(half * 512, (half + 1) * 512)
                pb = pss[6 + half]
                for f in range(NFF):
                    k.mm(pb.t[:], hid.t[:, f, t * 128:(t + 1) * 128], wf2.t[:, f, sl], f == 0, False, [hid, wf2], [pb], inc=False)
                k.mm(pb.t[:], ones_bf.t[0:2, :], b1_rows.t[0:2, sl], False, True, [ones_bf, b1_rows], [pb], inc=True)
                k.tt("dve", r2.t[:, sl], r2.t[:, sl], pb.t[:], ALU.add, [r2, pb], [r2])
            ln_stats(r2, fst, fmv, frs, fnm)
            o = r2
            k.act(o.t[:], r2.t[:], AF.Identity, [r2, frs, fnm], [o], bias=fnm.t[:], scale=frs.t[:])
            k.tt("pool", o.t[:], o.t[:], l2g.t[:], ALU.mult, [o, l2g], [o])
            k.tt("dve", o.t[:], o.t[:], l2b.t[:], ALU.add, [o, l2b], [o])
            k.dma("pool", out_d.t[c * 128:(c + 1) * 128, :], o.t[:], r=[o], w=[out_d])

    if NB > 0:
        ffn_front(0)
    for b in range(NB):
        ffn_w13(b)
        if b + 1 < NB:
            ffn_front(b + 1)
        if b == 0:
            scale_wf2()
        ffn_w2(b)
    k.pop()
    k.barrier()
    k.es.close()
    return nc, k.dbg


def _prep(inputs):
    f = lambda a: np.ascontiguousarray(np.asarray(a, dtype=np.float32))
    i = {kk: f(v) for kk, v in inputs.items()}
    kk = np.arange(128)
    cm = np.stack([np.eye(128), kk[:, None] <= kk[None, :], kk[:, None] >= kk[None, :],
                   kk[:, None] > kk[None, :], kk[:, None] < kk[None, :]], axis=1).astype(np.float32)
    col = lambda v: f(v.reshape(-1, 128).T)
    shared = dict(
        cm=f(cm), w_ada=i["w_ada"][0], b_ada_c=col(i["b_ada"][0]),
        colp=f(np.stack([col(i["ln0_g"]), col(i["ln0_b"]), col(i["ln1_g"][0]), col(i["ln1_b"][0])], axis=1)),
        convw_c=f(i["conv_w"][0].T.reshape(12, 128, 5).transpose(1, 0, 2)),
        convb_c=col(i["conv_b"][0]), w_in=i["w_in"][0],
        rows=f(np.stack([i["ln0_g"], i["ln0_b"], i["ssd_norm_g"][0], i["gm_norm_g"][0], i["gm_norm_b"][0],
                         i["ln1_g"][0], i["ln1_b"][0], i["ln2_g"][0], i["ln2_b"][0]])),
        b_gate=f(i["b_gate"][0][None, :]),
        small=f(np.concatenate([i["dt_bias"][0].reshape(-1), i["a_log"][0].reshape(-1), i["d_skip"][0].reshape(-1)])[None, :]),
        wspT=f(i["w_spatial"][0].transpose(2, 0, 1)), wsp=f(i["w_spatial"][0].transpose(1, 0, 2)),
        bsp_c=f(i["b_spatial"][0].T),
        w_ssd=i["w_ssd_proj"][0], w_gm=i["w_gm_proj"][0], w_out=i["w_out"][0],
        w_ff1=i["w_ff1"][0], w_ff3=i["w_ff3"][0], w_ff2=i["w_ff2"][0],
    )
    maps = []
    for b in range(i["x"].shape[0]):
        m = dict(shared)
        m["x"] = i["x"][b]
        m["ctx"] = i["ctx"][b]
        m["cT"] = f(np.stack([i["c"][b], i["c_ctx"]], axis=1).reshape(8, 128, 2).transpose(1, 0, 2))
        maps.append(m)
    return maps


def kernel(**inputs):
    maps = _prep(inputs)
    nc, _ = build()
    n = len(maps)
    res = run_bass_kernel_spmd(nc, maps, core_ids=list(range(n)))
    return np.stack([np.asarray(r["out"], dtype=np.float32) for r in res.results], axis=0)
```

```python
import os
from contextlib import ExitStack
import numpy as np
import concourse.bass as bass
import concourse.mybir as mybir
from concourse.bass_utils import run_bass_kernel_spmd

F32 = mybir.dt.float32
F32R = mybir.dt.float32r
BF16 = mybir.dt.bfloat16
AF = mybir.ActivationFunctionType
ALU = mybir.AluOpType

D = 1024
SEQ = 4096
CTX = 256
NCH = SEQ // 128
H = 16
P = 64
NST = 128
DFF = 2816
NFF = DFF // 128
XBC = 1536
DPROJ = 6688
O_Z, O_XBC, O_DT, O_U, O_V, O_G = 0, 1024, 2560, 2592, 3616, 4640
ALPHA = 2.0 ** 0.25
EPS = 1e-5
DEBUG = bool(int(os.environ.get("KDEBUG", "0")))
NCH_RUN = int(os.environ.get("KNCH", str(NCH)))
KPASS = int(os.environ.get("KPASS", "3"))
NFILL = int(os.environ.get("KFILL", "3"))


class Buf:
    def __init__(self, t):
        self.t = t
        self.w = None
        self.pw = []
        self.r = {}
        self.psum = False

    def __getitem__(self, idx):
        return self.t[idx]


class Eng:
    def __init__(self, e, sem, name):
        self.e, self.sem, self.name = e, sem, name
        self.count = 0
        self.seen = {}
        self.pend_r, self.pend_w = [], []


class KB:
    def __init__(self):
        self.nc = bass.Bass("TRN2", target_bir_lowering=False)
        nc = self.nc
        self.es = ExitStack()
        self.stacks = [self.es]
        self.engs = {}
        for name, e in (("pe", nc.tensor), ("act", nc.scalar), ("dve", nc.vector), ("pool", nc.gpsimd), ("sp", nc.sync)):
            sem = self.es.enter_context(nc.semaphore("s_" + name))
            self.engs[name] = Eng(e, sem, name)
        self.dpool = {}
        for q, n in (("sp", 20), ("pool", 20), ("act", 6)):
            self.dpool[q] = [[self.es.enter_context(nc.semaphore(f"d_{q}{i}")), 0] for i in range(n)]
        self.dnext = {q: 0 for q in self.dpool}
        self.uid = 0
        self.dbg = []
        self.fill = None

    def sb(self, shape, dt, name=None):
        self.uid += 1
        t = self.stacks[-1].enter_context(self.nc.sbuf_tensor(f"{name or 't'}_{self.uid}", list(shape), dt))
        return Buf(t)

    def ps(self, name):
        self.uid += 1
        t = self.stacks[-1].enter_context(self.nc.psum_tensor(f"{name}_{self.uid}", [128, 512], F32))
        b = Buf(t)
        b.psum = True
        return b

    def dram(self, name, shape, dt, kind="Internal"):
        return Buf(self.nc.dram_tensor(name, list(shape), dt, kind=kind).ap())

    def push(self):
        print("SBUF remaining at push:", self.nc.sbuf_bytes_remaining)
        st = ExitStack()
        self.stacks.append(st)
        return st

    def pop(self):
        self.barrier()
        self.stacks.pop().close()

    def _waits(self, E, r, w, partial=False):
        deps = {}
        raw = set()
        for b in r:
            if b.w is not None:
                deps[(b.w[0], b.w[1])] = b.w
                raw.add((b.w[0], b.w[1]))
            for t in b.pw:
                deps[(t[0], t[1])] = t
            if b.psum:
                for t in b.r.values():
                    if t[2] is not E:
                        deps[(t[0], t[1])] = t
        for b in w:
            if b.w is not None:
                deps[(b.w[0], b.w[1])] = b.w
            if not partial:
                for t in b.pw:
                    deps[(t[0], t[1])] = t
            for t in b.r.values():
                deps[(t[0], t[1])] = t
        for key, (sem, val, owner) in deps.items():
            if owner is E:
                if E.name == "pe":
                    continue
                if key not in raw:
                    continue
            if E.seen.get(sem, 0) >= val:
                continue
            if E.name == "pe" and self.fill is not None and owner is not None:
                for v in range(max(E.seen.get(sem, 0) + 1, val - NFILL), val):
                    E.e.wait_ge(sem, v)
                    self.nc.tensor.matmul(self.fill[0], self.fill[1], self.fill[2], start=True, stop=True)
            E.e.wait_ge(sem, val)
            E.seen[sem] = val

    def op(self, eng, fn, r=(), w=(), inc=True):
        E = self.engs[eng]
        self._waits(E, r, w)
        ins = fn()
        E.pend_r.extend(r)
        E.pend_w.extend(w)
        if inc:
            E.count += 1
            ins.then_inc(E.sem, 1)
            tok = (E.sem, E.count, E)
            for b in E.pend_r:
                b.r[E.sem] = tok
            for b in E.pend_w:
                b.w = tok
                b.pw = []
                b.r = {}
            E.pend_r, E.pend_w = [], []
        return ins

    def dma(self, q, out, in_, r=(), w=(), partial=False, **kw):
        E = self.engs[q]
        self._waits(E, r, w, partial)
        pool = self.dpool[q]
        i = self.dnext[q]
        self.dnext[q] = (i + 1) % len(pool)
        sem, val = pool[i]
        if val > 0 and E.seen.get(sem, 0) < val:
            E.e.wait_ge(sem, val)
            E.seen[sem] = val
        ins = E.e.dma_start(out=out, in_=in_, **kw)
        val += 16
        pool[i][1] = val
        ins.then_inc(sem, 16)
        tok = (sem, val, None)
        for b in r:
            b.r[sem] = tok
        for b in w:
            if partial:
                b.pw.append(tok)
            else:
                b.w = tok
                b.pw = []
            b.r = {}

    def barrier(self):
        toks = [(E.sem, E.count) for E in self.engs.values() if E.count > 0]
        for pool in self.dpool.values():
            toks += [(s, v) for s, v in pool if v > 0]
        for E in self.engs.values():
            assert not E.pend_r and not E.pend_w, E.name
            for sem, val in toks:
                if sem is E.sem or E.seen.get(sem, 0) >= val:
                    continue
                E.e.wait_ge(sem, val)
                E.seen[sem] = val

    def act(self, out, in_, func, r, w, bias=None, scale=None, accum=None):
        kw = {}
        if bias is not None:
            kw["bias"] = bias
        if scale is not None:
            kw["scale"] = scale
        if accum is not None:
            kw["accum_out"] = accum
        return self.op("act", lambda: self.nc.scalar.activation(out=out, in_=in_, func=func, **kw), r, w)

    def tt(self, eng, out, in0, in1, op, r, w):
        e = self.engs[eng].e
        return self.op(eng, lambda: e.tensor_tensor(out=out, in0=in0, in1=in1, op=op), r, w)

    def ts(self, eng, out, in0, s1, s2, op0, op1, r, w):
        e = self.engs[eng].e
        if s2 is None:
            return self.op(eng, lambda: e.tensor_scalar(out=out, in0=in0, scalar1=s1, scalar2=None, op0=op0), r, w)
        return self.op(eng, lambda: e.tensor_scalar(out=out, in0=in0, scalar1=s1, scalar2=s2, op0=op0, op1=op1), r, w)

    def stt(self, out, in0, scalar, in1, op0, op1, r, w):
        return self.op("dve", lambda: self.nc.vector.scalar_tensor_tensor(
            out=out, in0=in0, scalar=scalar, in1=in1, op0=op0, op1=op1), r, w)

    def copy(self, eng, out, in_, r, w):
        e = self.engs[eng].e
        if eng == "act":
            return self.act(out, in_, AF.Copy, r, w)
        return self.op(eng, lambda: e.tensor_copy(out=out, in_=in_), r, w)

    def memset(self, eng, ap, val, w):
        e = self.engs[eng].e
        return self.op(eng, lambda: e.memset(ap, val), (), w)

    def mm(self, out, lhsT, rhs, start, stop, r, w, inc):
        return self.op("pe", lambda: self.nc.tensor.matmul(out, lhsT, rhs, start=start, stop=stop), r, w, inc)

    def tr(self, out, in_, ident, r, w, inc):
        return self.op("pe", lambda: self.nc.tensor.transpose(out, in_, ident), r, w, inc)

    def dump(self, name, buf, ap, shape, dt=F32):
        if not DEBUG:
            return
        d = self.dram("dbg_" + name, shape, dt, kind="ExternalOutput")
        self.dma("sp", d.t, ap, r=[buf], w=[d])
        self.dbg.append("dbg_" + name)


def bc(ap, shape):
    return ap.to_broadcast(list(shape))


def build():
    k = KB()
    nc = k.nc

    def din(name, shape, dt=F32):
        return k.dram(name, shape, dt, kind="ExternalInput")

    x_d = din("x", [SEQ, D])
    ctx_d = din("ctx", [CTX, D])
    cT_d = din("cT", [128, 8, 2])
    cm_d = din("cm", [128, 5, 128])
    wada_d = din("w_ada", [D, 6 * D])
    badac_d = din("b_ada_c", [128, 48])
    colp_d = din("colp", [128, 4, 8])
    convw_d = din("convw_c", [128, 12, 5])
    convb_d = din("convb_c", [128, 12])
    win_d = din("w_in", [D, DPROJ])
    rows_d = din("rows", [9, D])
    bgate_d = din("b_gate", [1, 2 * D])
    small_d = din("small", [1, 96])
    wspT_d = din("wspT", [128, 8, 128])
    wsp_d = din("wsp", [128, 8, 128])
    bsp_d = din("bsp_c", [128, 8])
    wssd_d = din("w_ssd", [D, D])
    wgm_d = din("w_gm", [D, D])
    wout_d = din("w_out", [D, D])
    wff1_d = din("w_ff1", [D, DFF])
    wff3_d = din("w_ff3", [D, DFF])
    wff2_d = din("w_ff2", [DFF, D])
    out_d = k.dram("out", [SEQ, D], F32, kind="ExternalOutput")

    xs_s = [k.dram(f"xs_s{c}", [128, D], BF16) for c in range(NCH)]
    bt_s = [k.dram(f"bt_s{c}", [128, 256], BF16) for c in range(NCH)]
    bc_s = [k.dram(f"bc_s{c}", [128, 4, 128], BF16) for c in range(NCH)]
    hp_s = [k.dram(f"hp_s{c}", [128, D], BF16) for c in range(NCH)]
    g2_s = [k.dram(f"g2_s{c}", [128, D], BF16) for c in range(NCH)]
    x1_s = [k.dram(f"x1_s{c}", [128, D], F32) for c in range(NCH)]

    cm = k.sb([128, 5, 128], F32, "cm")
    k.dma("sp", cm.t[:], cm_d.t, w=[cm])
    IDN, MLE, MGE, MGT, MLT = range(5)
    ident_bf = k.sb([128, 128], BF16, "identbf")
    k.copy("dve", ident_bf.t[:], cm.t[:, IDN, :], [cm], [ident_bf])
    mgt_r = k.sb([128, 128], F32R, "mgtr")
    mlt_r = k.sb([128, 128], F32R, "mltr")
    k.copy("dve", mgt_r.t[:], cm.t[:, MGT, :], [cm], [mgt_r])
    k.copy("dve", mlt_r.t[:], cm.t[:, MLT, :], [cm], [mlt_r])
    ones_f = k.sb([128, 128], F32, "onesf")
    k.memset("dve", ones_f.t[:], 1.0, [ones_f])
    ones_bf = k.sb([2, 128], BF16, "onesbf")
    k.memset("dve", ones_bf.t[:], 1.0, [ones_bf])

    colp = k.sb([128, 4, 8], F32, "colp")
    k.dma("sp", colp.t[:], colp_d.t, w=[colp])
    convw = k.sb([128, 12, 5], F32, "convw")
    k.dma("sp", convw.t[:], convw_d.t, w=[convw])
    convb = k.sb([128, 12], F32, "convb")
    k.dma("sp", convb.t[:], convb_d.t, w=[convb])
    smallr = k.sb([128, 96], F32, "smallr")
    k.dma("sp", smallr.t[:], small_d.t.partition_broadcast(128), w=[smallr])
    arow = k.sb([128, 32], F32, "arow")
    k.act(arow.t[:], smallr.t[:, 32:64], AF.Exp, [smallr], [arow])
    k.ts("dve", arow.t[:], arow.t[:], -1.0, None, ALU.mult, None, [arow], [arow])
    dsum = k.sb([128, 16], F32, "dsum")
    k.tt("dve", dsum.t[:], smallr.t[:, 64:80], smallr.t[:, 80:96], ALU.add, [smallr], [dsum])
    dh = k.sb([128, H, 128], BF16, "dh")
    k.tt("dve", dh.t[:], bc(cm.t[:, IDN, :].unsqueeze(1), [128, H, 128]),
         bc(dsum.t[:].unsqueeze(2), [128, H, 128]), ALU.mult, [cm, dsum], [dh])
    dt_all = k.sb([128, NCH, 32], F32, "dtall")
    run_f = k.sb([128, D], F32, "runf")
    run_b = k.sb([128, D], F32, "runb")
    modc = k.sb([128, 48, 2], F32, "modc")
    cols = k.sb([128, 6, 8], F32, "cols")
    S1X, B1X, S1C, B1C, S2, B2 = range(6)
    lntmp = k.sb([128, 1], F32, "lntmp")
    epsc = k.sb([128, 1], F32, "epsc")
    k.memset("dve", epsc.t[:], EPS, [epsc])

    pss = [k.ps(f"ps{i}") for i in range(8)]
    fill_spec = (pss[7].t[:, 0:256], ident_bf.t[:], dh.t[:, 0:2, :].rearrange("p a i -> p (a i)"))

    st = k.push()
    w1 = k.sb([128, 8, 1568 + 2048 + 1024], BF16, "w1")
    W_XBC, W_DT, W_U, W_V, W_G2 = 0, 1536, 1568, 2592, 3616
    for kk in range(8):
        rs_ = slice(kk * 128, (kk + 1) * 128)
        k.dma("pool", w1.t[:, kk, 0:1568], win_d.t[rs_, O_XBC:O_XBC + 1568], r=[win_d], w=[w1], partial=True)
    for kk in range(8):
        rs_ = slice(kk * 128, (kk + 1) * 128)
        k.dma("pool", w1.t[:, kk, 1568:3616], win_d.t[rs_, O_U:O_U + 2048], r=[win_d], w=[w1], partial=True)
        k.dma("pool", w1.t[:, kk, 3616:4640], win_d.t[rs_, O_G + D:O_G + 2 * D], r=[win_d], w=[w1], partial=True)
    wgm = k.sb([128, 8, D], BF16, "wgm")
    for kk in range(8):
        k.dma("pool", wgm.t[:, kk, :], wgm_d.t[kk * 128:(kk + 1) * 128, :], w=[wgm], partial=True)
    wspT = k.sb([128, 8, 128], BF16, "wspT")
    k.dma("pool", wspT.t[:], wspT_d.t, w=[wspT])
    st = k.push()
    cT = k.sb([128, 8, 2], F32, "cT")
    k.dma("sp", cT.t[:], cT_d.t, w=[cT])
    scT = k.sb([128, 8, 2], F32, "scT")
    k.act(scT.t[:], cT.t[:], AF.Silu, [cT], [scT])
    badac = k.sb([128, 48], F32, "badac")
    k.dma("sp", badac.t[:], badac_d.t, w=[badac])
    wst = [k.sb([128, 8, 512], F32, f"wst{i}") for i in range(4)]
    wada_v = wada_d.t.rearrange("(k p) n -> p k n", p=128)
    modrow = k.sb([2, 6 * D], F32, "modrow")
    for cb in range(12):
        wb = wst[cb % 4]
        k.dma("sp", wb.t[:], wada_v[:, :, cb * 512:(cb + 1) * 512], r=[wada_d], w=[wb])
        pb = pss[cb % 2]
        for kk in range(8):
            k.mm(pb.t[0:2, :], scT.t[:, kk, :], wb.t[:, kk, :], kk == 0, kk == 7, [scT, wb], [pb], inc=(kk == 7))
        k.copy("act", modrow.t[:, cb * 512:(cb + 1) * 512], pb.t[0:2, :], [pb], [modrow])
    pm = pss[2]
    for j in range(48):
        k.mm(pm.t[:, 2 * j:2 * j + 2], modrow.t[0:2, j * 128:(j + 1) * 128], cm.t[0:2, IDN, 0:2], True, True,
             [modrow, cm], [pm], inc=(j == 47))
    k.tt("dve", modc.t[:], pm.t[:, 0:96].rearrange("p (j m) -> p j m", m=2),
         bc(badac.t[:].unsqueeze(2), [128, 48, 2]), ALU.add, [pm, badac], [modc])
    tmp8 = k.sb([128, 8], F32, "tmp8")
    for (si, bi, m, g_i, b_i, sc_o, sh_o) in ((S1X, B1X, 0, 0, 1, 8, 0), (S1C, B1C, 1, 0, 1, 8, 0), (S2, B2, 0, 2, 3, 32, 24)):
        k.ts("dve", tmp8.t[:], modc.t[:, sc_o:sc_o + 8, m], 1.0, None, ALU.add, None, [modc], [tmp8])
        k.tt("dve", cols.t[:, si, :], tmp8.t[:], colp.t[:, g_i, :], ALU.mult, [tmp8, colp], [cols])
        k.tt("dve", cols.t[:, bi, :], tmp8.t[:], colp.t[:, b_i, :], ALU.mult, [tmp8, colp], [cols])
        k.tt("dve", cols.t[:, bi, :], cols.t[:, bi, :], modc.t[:, sh_o:sh_o + 8, m], ALU.add, [cols, modc], [cols])
    k.pop()
    k.dump("modc", modc, modc.t[:].rearrange("p j m -> p (j m)"), [128, 96])

    def make_row_psum(off, name, banks):
        dg = k.sb([128, 128], F32, name + "dg")
        for c in range(8):
            k.ts("dve", dg.t[:], cm.t[:, IDN, :], modc.t[:, off + c, 0:1], None, ALU.mult, None, [cm, modc], [dg])
            pb = banks[c // 4]
            k.mm(pb.t[:, (c % 4) * 128:(c % 4 + 1) * 128], ones_f.t[:], dg.t[:], True, True, [ones_f, dg], [pb], True)

    def make_row(off, name):
        row = k.sb([128, D], F32, name)
        dg = k.sb([128, 128], F32, name + "dg")
        for c in range(8):
            k.ts("dve", dg.t[:], cm.t[:, IDN, :], modc.t[:, off + c, 0:1], None, ALU.mult, None, [cm, modc], [dg])
            pb = pss[1 + (c // 4)]
            k.mm(pb.t[:, (c % 4) * 128:(c % 4 + 1) * 128], ones_f.t[:], dg.t[:], True, True, [ones_f, dg], [pb], True)
        k.copy("act", row.t[:, 0:512], pss[1].t[:], [pss[1]], [row])
        k.copy("act", row.t[:, 512:1024], pss[2].t[:], [pss[2]], [row])
        return row

    def load_w(dst, src_ap, k_chunks, eng="pool"):
        for kk in range(k_chunks):
            k.dma(eng, dst.t[:, kk, :], src_ap[kk * 128:(kk + 1) * 128, :], w=[dst], partial=True)

    def bias_rows(src_row_ap, n, scale, name):
        rows = k.sb([2, n], BF16, name)
        k.push()
        b32 = k.sb([1, n], F32, name + "32")
        k.dma("sp", b32.t[:], src_row_ap, w=[b32])
        if scale != 1.0:
            k.ts("dve", b32.t[:], b32.t[:], float(scale), None, ALU.mult, None, [b32], [b32])
        k.copy("dve", rows.t[0:1, :], b32.t[:], [b32], [rows])
        lo32 = k.sb([1, n], F32, name + "lo32")
        k.tt("dve", lo32.t[:], b32.t[:], rows.t[0:1, :], ALU.subtract, [b32, rows], [lo32])
        lo = k.sb([1, n], BF16, name + "lo")
        k.copy("dve", lo.t[:], lo32.t[:], [lo32], [lo])
        k.dma("sp", rows.t[1:2, :], lo.t[:], r=[lo], w=[rows])
        k.pop()
        return rows

    def rsqrt_act(out_buf, in_ap, in_buf, scale):
        k.act(lntmp.t[:], in_ap, AF.Ln, [in_buf, epsc], [lntmp], bias=epsc.t[:], scale=float(scale))
        k.act(out_buf.t[:], lntmp.t[:], AF.Exp, [lntmp], [out_buf], scale=-0.5)

    def ln_stats(src, tmp_st, tmp_mv, rs, nm):
        for i in range(2):
            k.op("dve", lambda i=i: nc.vector.bn_stats(out=tmp_st.t[:, i, :], in_=src.t[:, i * 512:(i + 1) * 512]), [src], [tmp_st])
        k.op("dve", lambda: nc.vector.bn_aggr(out=tmp_mv.t[:], in_=tmp_st.t[:].rearrange("p a b -> p (a b)")), [tmp_st], [tmp_mv])
        rsqrt_act(rs, tmp_mv.t[:, 1:2], tmp_mv, 1.0)
        k.ts("dve", nm.t[:], tmp_mv.t[:, 0:1], rs.t[:], -1.0, ALU.mult, ALU.mult, [tmp_mv, rs], [nm])

    def proj_tok(hT, W, c0, n, pb, brow=None, b0=0):
        for kk in range(8):
            last = (kk == 7 and brow is None)
            k.mm(pb.t[:, 0:n], hT.t[:, kk, :], W.t[:, kk, c0:c0 + n], kk == 0, last, [hT, W], [pb], inc=last)
        if brow is not None:
            k.mm(pb.t[:, 0:n], ones_bf.t[0:2, :], brow.t[0:2, b0:b0 + n], False, True, [ones_bf, brow], [pb], inc=True)

    def transpose_to(dstT, src, pb, eng):
        pv = pb.t[:].bitcast(BF16)
        for c in range(8):
            k.tr(pv[:, c * 128:(c + 1) * 128], src.t[:, c * 128:(c + 1) * 128], ident_bf.t[:], [src, ident_bf], [pb], inc=(c == 7))
        k.copy(eng, dstT.t[:].rearrange("p a t -> p (a t)"), pv, [pb], [dstT])

    class LN0:
        def __init__(self, nh=2):
            self.nh = nh
            self.xt = [k.sb([128, D], F32, "xt") for _ in range(2)]
            self.xh = [k.sb([128, D], F32, "xh") for _ in range(nh)]
            self.hT = [k.sb([128, 8, 128], BF16, "hT") for _ in range(2)]
            self.st = k.sb([128, 2, 6], F32, "lnst")
            self.mv = k.sb([128, 2], F32, "lnmv")
            self.rs = k.sb([128, 1], F32, "lnrs")
            self.nm = k.sb([128, 1], F32, "lnnm")
            self.i = 0

        def run(self, src_buf, src_ap, si, bi, pbanks):
            s = self.i % 2
            self.i += 1
            xt, xh, hT = self.xt[s], self.xh[s % self.nh], self.hT[s]
            k.dma("sp", xt.t[:], src_ap, r=[src_buf], w=[xt])
            ln_stats(xt, self.st, self.mv, self.rs, self.nm)
            k.act(xh.t[:], xt.t[:], AF.Identity, [xt, self.rs, self.nm], [xh], bias=self.nm.t[:], scale=self.rs.t[:])
            for c in range(8):
                pb = pbanks[c // 4]
                k.tr(pb.t[:, (c % 4) * 128:(c % 4 + 1) * 128], xh.t[:, c * 128:(c + 1) * 128], cm.t[:, IDN, :],
                     [xh, cm], [pb], inc=(c % 4 == 3))
            for c in range(8):
                pb = pbanks[c // 4]
                src = pb.t[:, (c % 4) * 128:(c % 4 + 1) * 128]
                if c < 4:
                    k.ts("dve", hT.t[:, c, :], src, cols.t[:, si, c:c + 1], cols.t[:, bi, c:c + 1], ALU.mult, ALU.add, [pb, cols], [hT])
                else:
                    k.act(hT.t[:, c, :], src, AF.Identity, [pb, cols], [hT], bias=cols.t[:, bi, c:c + 1], scale=cols.t[:, si, c:c + 1])
            return xh, hT

    gmgrow = k.sb([128, D], F32, "gmgrow")
    k.dma("sp", gmgrow.t[:], rows_d.t[3:4, :].partition_broadcast(128), w=[gmgrow])
    biasM = k.sb([128, 8, 128], F32, "biasM")
    k.dma("sp", biasM.t[:].rearrange("p g c -> p (g c)"), rows_d.t[4:5, :].partition_broadcast(128), w=[biasM])
    gv = k.sb([128, D], F32, "gv")
    wsp_v = gv.t[:].rearrange("p (g q) -> p g q", g=8)
    k.dma("sp", wsp_v, wsp_d.t, w=[gv])
    rsum = k.sb([128, 8], F32, "rsum")
    k.op("dve", lambda: nc.vector.reduce_sum(out=rsum.t[:], in_=wsp_v, axis=mybir.AxisListType.X), [gv], [rsum])
    bspc = k.sb([128, 8], F32, "bspc")
    k.dma("sp", bspc.t[:], bsp_d.t, w=[bspc])
    k.tt("dve", biasM.t[:], biasM.t[:], bc(rsum.t[:].unsqueeze(2), [128, 8, 128]), ALU.mult, [biasM, rsum], [biasM])
    k.tt("dve", biasM.t[:], biasM.t[:], bc(bspc.t[:].unsqueeze(2), [128, 8, 128]), ALU.add, [biasM, bspc], [biasM])
    dtb_rows = bias_rows(small_d.t[0:1, 0:32], 32, 1.0, "dtb")
    bg2_rows = bias_rows(bgate_d.t[0:1, D:2 * D], D, 1.0, "bg2")

    if NFILL > 0:
        k.fill = fill_spec
    ln0 = LN0(2)
    pre = [k.sb([128, 12, 132], BF16, f"pre{i}") for i in range(3)]
    dgw = k.sb([128, 12, 5, 128], BF16, "dgw")
    for cc in range(12):
        for tap in range(5):
            k.ts("pool" if (cc + tap) % 2 else "dve", dgw.t[:, cc, tap, :], cm.t[:, IDN, :], convw.t[:, cc, tap:tap + 1], None,
                 ALU.mult, None, [cm, convw], [dgw])
    xbcT = k.sb([128, 12, 128], BF16, "xbcT")
    xs_tok = k.sb([128, D], BF16, "xs_tok")
    b_tok = k.sb([128, 256], BF16, "b_tok")
    a_sb = k.sb([128, 32], F32, "a_sb")
    ex = k.sb([128, 96], F32, "ex")
    ddv = k.sb([128, 32], F32, "ddv")
    xdd = k.sb([128, D], BF16, "xdd")
    hp_bf = k.sb([128, D], BF16, "hp_bf")
    e_t = k.sb([128, 32], F32, "e_t")
    decf1 = k.sb([128, 16], F32, "decf1")
    gu = k.sb([128, D], F32, "gu")
    sf1 = gu
    vhat = k.sb([128, D], BF16, "vhat")
    g2 = k.sb([128, D], F32, "g2")
    ygm = gv
    ygm_bf = k.sb([128, D], BF16, "ygm_bf")
    ygmT = k.sb([128, 8, 128], BF16, "ygmT")
    G2 = k.sb([128, D], BF16, "G2")
    vst = k.sb([128, 2, 6], F32, "vst")
    vmv = k.sb([128, 2], F32, "vmv")
    vrs = k.sb([128, 1], F32, "vrs")
    vnm = k.sb([128, 1], F32, "vnm")

    k.memset("dve", run_f.t[:], 0.0, [run_f])
    k.memset("dve", run_b.t[:], 0.0, [run_b])
    print("SBUF free in pass 1:", nc.sbuf_bytes_remaining)

    def small_exps(dt_ap, dt_buf):
        k.tt("dve", a_sb.t[:], dt_ap, arow.t[:], ALU.mult, [dt_buf, arow], [a_sb])
        pb = pss[4]
        specs = ((MLE, 0), (MGE, 16), (MGT, 0), (MLT, 16))
        for i, (mi, ao) in enumerate(specs):
            k.mm(pb.t[:, i * 16:(i + 1) * 16], cm.t[:, mi, :], a_sb.t[:, ao:ao + 16], True, True, [cm, a_sb], [pb], inc=False)
        k.mm(pb.t[:, 64:96], ones_f.t[:], a_sb.t[:, 0:32], True, True, [ones_f, a_sb], [pb], inc=True)
        k.act(ex.t[:], pb.t[:, 0:96], AF.Exp, [pb], [ex])

    order1 = [("c", 1), ("c", 0)] + [("x", cc_) for cc_ in range(NCH_RUN - 1, -1, -1)]
    lncache = {}

    def ln_for(i):
        if i >= len(order1) or i in lncache:
            return
        kind, cc_ = order1[i]
        if kind == "c":
            lncache[i] = ln0.run(ctx_d, ctx_d.t[cc_ * 128:(cc_ + 1) * 128, :], S1C, B1C, (pss[0], pss[1]))
        else:
            lncache[i] = ln0.run(x_d, x_d.t[cc_ * 128:(cc_ + 1) * 128, :], S1X, B1X, (pss[0], pss[1]))

    def s1(seq_buf, seq_ap, c, n, si, bi, slot_of, is_ctx):
        oi = order1.index(("c" if is_ctx else "x", c))
        ln_for(oi)
        xh, hT = lncache.pop(oi)
        for cc in range(12):
            pb = pss[2 + cc // 4]
            for kk in range(8):
                k.mm(pb.t[:, (cc % 4) * 128:(cc % 4 + 1) * 128], w1.t[:, kk, W_XBC + cc * 128:W_XBC + (cc + 1) * 128],
                     hT.t[:, kk, :], kk == 0, kk == 7, [w1, hT], [pb], inc=(kk == 7))
        me = pre[slot_of(c)]
        for q in range(3):
            pv = pss[2 + q].t[:].rearrange("p (a t) -> p a t", a=4)
            k.act(me.t[:, q * 4:(q + 1) * 4, 2:130], pv, AF.Copy, [pss[2 + q]], [me])
            KH = os.environ.get("KHALO", "")
            if c + 1 < n and KH != "skipL":
                k.copy("dve", pre[slot_of(c + 1)].t[:, q * 4:(q + 1) * 4, 0:2], pv[:, :, 126:128], [pss[2 + q]], [pre[slot_of(c + 1)]])
            if c - 1 >= 0 and KH != "skipR":
                k.copy("dve", pre[slot_of(c - 1)].t[:, q * 4:(q + 1) * 4, 130:132], pv[:, :, 0:2], [pss[2 + q]], [pre[slot_of(c - 1)]])
        if c == n - 1:
            k.memset("dve", me.t[:, :, 130:132], 0.0, [me])
        if c == 0:
            k.memset("dve", me.t[:, :, 0:2], 0.0, [me])
        yield "halo"
        pb = pss[5]
        proj_tok(hT, w1, W_DT, 32, pb, dtb_rows, 0)
        k.act(e_t.t[:], pb.t[:, 0:32], AF.Exp, [pb], [e_t])
        dts = dtc_all if is_ctx else dt_all
        k.act(dts.t[:, c, :], e_t.t[:], AF.Ln, [e_t], [dts], bias=1.0)
        if is_ctx or os.environ.get("KSKIP_GM"):
            ln_for(oi + 1)
            return
        yield
        for half in range(2):
            pb = pss[5 + half]
            proj_tok(hT, w1, W_U + half * 512, 512, pb)
            k.act(gu.t[:, half * 512:(half + 1) * 512], pb.t[:], AF.Gelu_apprx_tanh, [pb], [gu])
        yield
        for half in range(2):
            pb = pss[5 + half]
            proj_tok(hT, w1, W_V + half * 512, 512, pb)
            k.act(gv.t[:, half * 512:(half + 1) * 512], pb.t[:], AF.Gelu_apprx_tanh, [pb], [gv])
        yield
        ln_stats(gv, vst, vmv, vrs, vnm)
        k.act(vhat.t[:], gv.t[:], AF.Identity, [gv, vrs, vnm], [vhat], bias=vnm.t[:], scale=vrs.t[:])
        for half in range(2):
            pb = pss[5 + half]
            proj_tok(hT, w1, W_G2 + half * 512, 512, pb, bg2_rows, half * 512)
            k.act(g2.t[:, half * 512:(half + 1) * 512], pb.t[:], AF.Sigmoid, [pb], [g2])
        yield
        ln_for(oi + 1)
        yield
        for g in range(8):
            pb = pss[5 + g // 4]
            k.mm(pb.t[:, (g % 4) * 128:(g % 4 + 1) * 128], wspT.t[:, g, :], vhat.t[:, g * 128:(g + 1) * 128],
                 True, True, [wspT, vhat], [pb], inc=(g % 4 == 3))
        bM = biasM.t[:].rearrange("p g c -> p (g c)")
        for half in range(2):
            sl = slice(half * 512, (half + 1) * 512)
            pb = pss[5 + half]
            k.tt("dve", ygm.t[:, sl], pb.t[:], gmgrow.t[:, sl], ALU.mult, [pb, gmgrow], [ygm])
            k.tt("dve", ygm.t[:, sl], ygm.t[:, sl], bM[:, sl], ALU.add, [ygm, biasM], [ygm])
            k.tt("dve", ygm_bf.t[:, sl], ygm.t[:, sl], gu.t[:, sl], ALU.mult, [ygm, gu], [ygm_bf])
        yield
        transpose_to(ygmT, ygm_bf, pss[5], "dve")
        for half in range(2):
            pb = pss[5 + half]
            for kk in range(8):
                k.mm(pb.t[:], ygmT.t[:, kk, :], wgm.t[:, kk, half * 512:(half + 1) * 512], kk == 0, kk == 7,
                     [ygmT, wgm], [pb], inc=(kk == 7))
            k.tt("dve", G2.t[:, half * 512:(half + 1) * 512], pb.t[:], g2.t[:, half * 512:(half + 1) * 512], ALU.mult, [pb, g2], [G2])
        k.dma("pool", g2_s[c].t, G2.t[:], r=[G2], w=[g2_s[c]])

    def post(c, n, slot_of, is_ctx):
        if os.environ.get("KSKIP_POST") and not is_ctx:
            return
        me = pre[slot_of(c)]
        for cc in range(12):
            pb = pss[2 + cc // 4]
            for tap in range(5):
                k.mm(pb.t[:, (cc % 4) * 128:(cc % 4 + 1) * 128], dgw.t[:, cc, tap, :], me.t[:, cc, tap:tap + 128],
                     tap == 0, tap == 4, [dgw, me], [pb], inc=(tap == 4 and cc % 4 == 3))
        for cc in range(12):
            pb = pss[2 + cc // 4]
            k.act(xbcT.t[:, cc, :], pb.t[:, (cc % 4) * 128:(cc % 4 + 1) * 128], AF.Silu, [pb, convb], [xbcT], bias=convb.t[:, cc:cc + 1])
        yield

        pv0 = pss[0].t[:].bitcast(BF16)
        pv1 = pss[1].t[:].bitcast(BF16)
        for cix in range(8):
            k.tr(pv0[:, cix * 128:(cix + 1) * 128], xbcT.t[:, cix, :], ident_bf.t[:], [xbcT, ident_bf], [pss[0]], inc=(cix == 7))
        for cix in range(2):
            k.tr(pv1[:, cix * 128:(cix + 1) * 128], xbcT.t[:, 8 + cix, :], ident_bf.t[:], [xbcT, ident_bf], [pss[1]], inc=(cix == 1))
        k.copy("dve", xs_tok.t[:], pv0, [pss[0]], [xs_tok])
        k.copy("act", b_tok.t[:], pv1[:, 0:256], [pss[1]], [b_tok])
        if not is_ctx:
            k.dma("pool", xs_s[c].t, xs_tok.t[:], r=[xs_tok], w=[xs_s[c]])
            k.dma("pool", bt_s[c].t, b_tok.t[:], r=[b_tok], w=[bt_s[c]])
            k.dma("pool", bc_s[c].t, xbcT.t[:, 8:12, :], r=[xbcT], w=[bc_s[c]])
        yield
        dts = dtc_all if is_ctx else dt_all
        small_exps(dts.t[:, c, :], dts)
        k.tt("dve", ddv.t[:, 16:32], ex.t[:, 48:64], dts.t[:, c, 16:32], ALU.mult, [ex, dts], [ddv])
        if is_ctx:
            k.tt("dve", ddv.t[:, 0:16], ex.t[:, 32:48], dts.t[:, c, 0:16], ALU.mult, [ex, dts], [ddv])
        dirs = ((1, run_b),) + (((0, run_f),) if is_ctx else ())
        for d, run in dirs:
            yield
            k.tt("dve", xdd.t[:].rearrange("p (h q) -> p h q", h=H), xs_tok.t[:].rearrange("p (h q) -> p h q", h=H),
                 bc(ddv.t[:, d * 16:(d + 1) * 16].unsqueeze(2), [128, H, P]), ALU.mult, [xs_tok, ddv], [xdd])
            for g in range(2):
                pb = pss[2 + g]
                k.mm(pb.t[:], b_tok.t[:, g * 128:(g + 1) * 128], xdd.t[:, g * 512:(g + 1) * 512], True, True,
                     [b_tok, xdd], [pb], inc=True)
            dec = ex.t[:, 64 + d * 16:80 + d * 16]
            if is_ctx and d == 0:
                if c == 1:
                    for g in range(2):
                        k.copy("act", sf1.t[:, g * 512:(g + 1) * 512], pss[2 + g].t[:], [pss[2 + g]], [sf1])
                    k.copy("dve", decf1.t[:], dec, [ex], [decf1])
                else:
                    for g in range(2):
                        sl = slice(g * 512, (g + 1) * 512)
                        k.tt("dve", run_f.t[:, sl].rearrange("p (h q) -> p h q", h=8), pss[2 + g].t[:].rearrange("p (h q) -> p h q", h=8),
                             bc(decf1.t[:, g * 8:(g + 1) * 8].unsqueeze(2), [128, 8, P]), ALU.mult, [pss[2 + g], decf1], [run_f])
                        k.tt("dve", run_f.t[:, sl], run_f.t[:, sl], sf1.t[:, sl], ALU.add, [run_f, sf1], [run_f])
                continue
            if not is_ctx:
                k.copy("dve", hp_bf.t[:], run.t[:], [run], [hp_bf])
                k.dma("pool", hp_s[c].t, hp_bf.t[:], r=[hp_bf], w=[hp_s[c]])
            k.tt("pool", run.t[:].rearrange("p (h q) -> p h q", h=H), run.t[:].rearrange("p (h q) -> p h q", h=H),
                 bc(dec.unsqueeze(2), [128, H, P]), ALU.mult, [run, ex], [run])
            for g in range(2):
                sl = slice(g * 512, (g + 1) * 512)
                k.tt("dve", run.t[:, sl], run.t[:, sl], pss[2 + g].t[:], ALU.add, [run, pss[2 + g]], [run])

    def run_all(g):
        for _ in g:
            pass

    def interleave(ga, gb):
        da = db = False
        while not (da and db):
            if not da:
                try:
                    next(ga)
                except StopIteration:
                    da = True
            if not db:
                try:
                    next(gb)
                except StopIteration:
                    db = True

    dtc_all = k.sb([128, 2, 32], F32, "dtcall")
    for c in (1, 0):
        run_all(s1(ctx_d, ctx_d.t, c, 2, S1C, B1C, lambda cc: cc % 3, True))
        if c + 1 < 2:
            run_all(post(c + 1, 2, lambda cc: cc % 3, True))
    run_all(post(0, 2, lambda cc: cc % 3, True))
    k.dump("s_f", run_f, run_f.t[:], [128, D])
    k.dump("s_b", run_b, run_b.t[:], [128, D])
    NR = NCH_RUN
    KSTOP = int(os.environ.get("KSTOP", "1000"))
    steps = 0
    for c in range(NR - 1, -1, -1):
        if steps >= KSTOP:
            break
        ga = s1(x_d, x_d.t, c, NR, S1X, B1X, lambda cc: cc % 3, False)
        next(ga)
        steps += 1
        if c + 1 < NR:
            interleave(ga, post(c + 1, NR, lambda cc: cc % 3, False))
            steps += 1
        else:
            run_all(ga)
    if steps < KSTOP:
        run_all(post(0, NR, lambda cc: cc % 3, False))
    k.dump("dt_all", dt_all, dt_all.t[:].rearrange("p c d -> p (c d)"), [128, NCH * 32])
    k.pop()

    if KPASS < 2:
        k.barrier()
        k.es.close()
        return nc, k.dbg
    st = k.push()
    bg1_rows = bias_rows(bgate_d.t[0:1, 0:D], D, 1.0, "bg1")
    b0_rows = bias_rows(rows_d.t[1:2, :], D, ALPHA, "b0r")
    a0row = k.sb([128, D], F32, "a0row")
    k.dma("sp", a0row.t[:], rows_d.t[0:1, :].partition_broadcast(128), w=[a0row])
    k.ts("dve", a0row.t[:], a0row.t[:], float(ALPHA), None, ALU.mult, None, [a0row], [a0row])
    ngrow = k.sb([128, D], F32, "ngrow")
    k.dma("sp", ngrow.t[:], rows_d.t[2:3, :].partition_broadcast(128), w=[ngrow])
    make_row_psum(16, "g1row", (pss[4], pss[5]))
    wout = k.sb([128, 8, D], BF16, "wout")
    load_w(wout, wout_d.t, 8)
    w2 = k.sb([128, 8, 2048], BF16, "w2")
    for kk in range(8):
        rs_ = slice(kk * 128, (kk + 1) * 128)
        k.dma("pool", w2.t[:, kk, 0:1024], win_d.t[rs_, O_Z:O_Z + D], r=[win_d], w=[w2], partial=True)
        k.dma("pool", w2.t[:, kk, 1024:2048], win_d.t[rs_, O_G:O_G + D], r=[win_d], w=[w2], partial=True)
    wssd = k.sb([128, 8, D], BF16, "wssd")
    load_w(wssd, wssd_d.t, 8)

    def scale_wout():
        for kk in range(8):
            for half in range(2):
                sl = slice(half * 512, (half + 1) * 512)
                k.tt("dve", wout.t[:, kk, sl], wout.t[:, kk, sl], pss[4 + half].t[:], ALU.mult, [wout, pss[4 + half]], [wout])

    ln0 = LN0()
    xs_l = [k.sb([128, D], BF16, "xs_l") for _ in range(2)]
    bt_l = [k.sb([128, 256], BF16, "bt_l") for _ in range(2)]
    bc_l = [k.sb([128, 4, 128], BF16, "bc_l") for _ in range(2)]
    hp_l = [k.sb([128, D], BF16, "hp_l") for _ in range(2)]
    g2_l = [k.sb([128, D], BF16, "g2_l") for _ in range(2)]
    a_sb = k.sb([128, 32], F32, "a_sb2")
    ex = k.sb([128, 96], F32, "ex2")
    ddv = k.sb([128, 32], F32, "ddv2")
    sz_l = [k.sb([128, D], BF16, "sz") for _ in range(2)]
    g1_l = [k.sb([128, D], BF16, "g1") for _ in range(2)]
    Rf = k.sb([128, H, 128], F32R, "Rf")
    Rb = Rf
    Ef = k.sb([128, H, 128], F32, "Ef")
    cbm = k.sb([128, 2, 2, 128], F32, "cbm")
    wgt = [k.sb([128, H, 128], BF16, f"wgt{d}") for d in range(2)]
    xdt = [k.sb([128, D], BF16, f"xdt{d}") for d in range(2)]
    xdd = k.sb([128, D], BF16, "xdd2")
    runf_bf = k.sb([128, D], BF16, "runf_bf")
    t1 = k.sb([128, D], F32, "t1")
    t2 = k.sb([128, D], F32, "t2")
    hh = k.sb([128, D], F32, "hh")
    sq = t2
    ss = k.sb([128, 1], F32, "ss")
    rstd = k.sb([128, 1], F32, "rstd")
    yg = k.sb([128, D], BF16, "yg")
    ygT = k.sb([128, 8, 128], BF16, "ygT")
    m1 = hh
    mg = k.sb([128, D], BF16, "mg")
    mgT = k.sb([128, 8, 128], BF16, "mgT")
    r1 = t1
    x1h = [k.sb([128, D], F32, "x1h") for _ in range(2)]
    lst = k.sb([128, 2, 6], F32, "lst")
    lmv = k.sb([128, 2], F32, "lmv")
    lrs = k.sb([128, 1], F32, "lrs")
    lnm = k.sb([128, 1], F32, "lnm")

    k.copy("act", runf_bf.t[:], run_f.t[:], [run_f], [runf_bf])
    print("SBUF free in pass 2:", nc.sbuf_bytes_remaining)

    def h3(ap, h=H):
        return ap.rearrange("p (h q) -> p h q", h=h)

    KSTOP2 = int(os.environ.get("KSTOP2", "1000"))
    ln2cache = {}

    def front2a(c):
        ln2cache[c] = ln0.run(x_d, x_d.t[c * 128:(c + 1) * 128, :], S1X, B1X, (pss[0], pss[1]))

    def front2b(c):
        xh, hT = ln2cache[c]
        sz, g1 = sz_l[c % 2], g1_l[c % 2]
        for half in range(2):
            pb = pss[half]
            proj_tok(hT, w2, half * 512, 512, pb)
            k.act(sz.t[:, half * 512:(half + 1) * 512], pb.t[:], AF.Silu, [pb], [sz])
        for half in range(2):
            pb = pss[half]
            proj_tok(hT, w2, 1024 + half * 512, 512, pb, bg1_rows, half * 512)
            k.act(g1.t[:, half * 512:(half + 1) * 512], pb.t[:], AF.Sigmoid, [pb], [g1])

    front2a(0)
    scale_wout()
    front2b(0)
    for c in range(min(NR, KSTOP2)):
        s = c % 2
        xs, bt, bcl, hp, g2l = xs_l[s], bt_l[s], bc_l[s], hp_l[s], g2_l[s]
        k.dma("sp", xs.t[:], xs_s[c].t, r=[xs_s[c]], w=[xs])
        k.dma("sp", bt.t[:], bt_s[c].t, r=[bt_s[c]], w=[bt])
        k.dma("sp", bcl.t[:], bc_s[c].t, r=[bc_s[c]], w=[bcl])
        k.dma("sp", hp.t[:], hp_s[c].t, r=[hp_s[c]], w=[hp])
        k.dma("sp", g2l.t[:], g2_s[c].t, r=[g2_s[c]], w=[g2l])
        xh, hT = ln2cache.pop(c)
        sz, g1 = sz_l[s], g1_l[s]
        xo = x1h[s]
        k.tt("pool", xo.t[:], xh.t[:], a0row.t[:], ALU.mult, [xh, a0row], [xo])
        if c + 1 < min(NR, KSTOP2):
            front2a(c + 1)
        dtc = dt_all.t[:, c, :]
        k.tt("dve", a_sb.t[:], dtc, arow.t[:], ALU.mult, [dt_all, arow], [a_sb])
        pb = pss[6]
        for i, (mi, ao) in enumerate(((MLE, 0), (MGE, 16), (MGT, 0), (MLT, 16))):
            k.mm(pb.t[:, i * 16:(i + 1) * 16], cm.t[:, mi, :], a_sb.t[:, ao:ao + 16], True, True, [cm, a_sb], [pb], inc=False)
        k.mm(pb.t[:, 64:96], ones_f.t[:], a_sb.t[:, 0:32], True, True, [ones_f, a_sb], [pb], inc=True)
        k.act(ex.t[:], pb.t[:, 0:96], AF.Exp, [pb], [ex])
        pb = pss[6]
        for g in range(2):
            k.mm(pb.t[:, 256 + g * 128:256 + (g + 1) * 128], bcl.t[:, g, :], bcl.t[:, 2 + g, :], True, True, [bcl], [pb], inc=(g == 1))
        pv = pb.t[:, 256:512].rearrange("p (g i) -> p g i", g=2)
        k.tt("dve", cbm.t[:, 0, :, :], pv, bc(cm.t[:, MLE, :].unsqueeze(1), [128, 2, 128]), ALU.mult, [pb, cm], [cbm])
        k.tt("dve", cbm.t[:, 1, :, :], pv, bc(cm.t[:, MGE, :].unsqueeze(1), [128, 2, 128]), ALU.mult, [pb, cm], [cbm])
        for d in range(2):
            k.tt("pool", h3(xdt[d].t[:]), h3(xs.t[:]), bc(dtc[:, d * 16:(d + 1) * 16].unsqueeze(2), [128, H, P]),
                 ALU.mult, [xs, dt_all], [xdt[d]])
        for d, (R, Lm) in enumerate(((Rf, mgt_r), (Rb, mlt_r))):
            k.tt("pool", R.t[:], bc(a_sb.t[:, d * 16:(d + 1) * 16].unsqueeze(2), [128, H, 128]),
                 bc(cm.t[:, MGE if d else MLE, :].unsqueeze(1), [128, H, 128]), ALU.mult, [a_sb, cm], [R])
            R2 = R.t[:].rearrange("p h i -> p (h i)")
            E2 = Ef.t[:].rearrange("p h i -> p (h i)")
            for q in range(4):
                pb = pss[2 + q]
                k.mm(pb.t[:], Lm.t[:], R2[:, q * 512:(q + 1) * 512], True, True, [Lm, R], [pb], inc=True)
                k.act(E2[:, q * 512:(q + 1) * 512], pb.t[:], AF.Exp, [pb], [Ef])
            for g in range(2):
                k.tt("dve", wgt[d].t[:, g * 8:(g + 1) * 8, :], Ef.t[:, g * 8:(g + 1) * 8, :],
                     bc(cbm.t[:, d, g, :].unsqueeze(1), [128, 8, 128]), ALU.mult, [Ef, cbm], [wgt[d]])
        if c + 1 < min(NR, KSTOP2):
            front2b(c + 1)
        for h in range(H):
            pb = pss[h // 8]
            o = pb.t[:, (h % 8) * 64:(h % 8 + 1) * 64]
            cs = slice(h * 64, (h + 1) * 64)
            k.mm(o, wgt[0].t[:, h, :], xdt[0].t[:, cs], True, False, [wgt[0], xdt[0]], [pb], inc=False)
            k.mm(o, wgt[1].t[:, h, :], xdt[1].t[:, cs], False, False, [wgt[1], xdt[1]], [pb], inc=False)
            k.mm(o, dh.t[:, h, :], xs.t[:, cs], False, True, [dh, xs], [pb], inc=(h % 8 == 7))
        for g in range(2):
            k.mm(pss[2 + g].t[:], bcl.t[:, 2 + g, :], runf_bf.t[:, g * 512:(g + 1) * 512], True, True, [bcl, runf_bf], [pss[2 + g]], inc=True)
            k.mm(pss[4 + g].t[:], bcl.t[:, 2 + g, :], hp.t[:, g * 512:(g + 1) * 512], True, True, [bcl, hp], [pss[4 + g]], inc=True)
        for g in range(2):
            sl = slice(g * 512, (g + 1) * 512)
            k.tt("dve", h3(t1.t[:, sl], 8), h3(pss[2 + g].t[:], 8), bc(ex.t[:, g * 8:(g + 1) * 8].unsqueeze(2), [128, 8, P]),
                 ALU.mult, [pss[2 + g], ex], [t1])
            k.tt("dve", h3(t2.t[:, sl], 8), h3(pss[4 + g].t[:], 8), bc(ex.t[:, 16 + g * 8:16 + (g + 1) * 8].unsqueeze(2), [128, 8, P]),
                 ALU.mult, [pss[4 + g], ex], [t2])
            k.tt("pool", t1.t[:, sl], t1.t[:, sl], t2.t[:, sl], ALU.add, [t1, t2], [t1])
            k.tt("dve", t1.t[:, sl], t1.t[:, sl], pss[g].t[:], ALU.add, [t1, pss[g]], [t1])
        if c == 0:
            k.dump("y0", t1, t1.t[:], [128, D])
        k.tt("dve", hh.t[:], t1.t[:], sz.t[:], ALU.mult, [t1, sz], [hh])
        k.act(sq.t[:], hh.t[:], AF.Square, [hh], [sq, ss], accum=ss.t[:])
        rsqrt_act(rstd, ss.t[:], ss, 1.0 / D)
        k.stt(yg.t[:], hh.t[:], rstd.t[:], ngrow.t[:], ALU.mult, ALU.mult, [hh, rstd, ngrow], [yg])

        transpose_to(ygT, yg, pss[6], "act")
        k.tt("dve", ddv.t[:, 0:16], ex.t[:, 32:48], dtc[:, 0:16], ALU.mult, [ex, dt_all], [ddv])
        k.tt("pool", h3(xdd.t[:]), h3(xs.t[:]), bc(ddv.t[:, 0:16].unsqueeze(2), [128, H, P]), ALU.mult, [xs, ddv], [xdd])
        for g in range(2):
            k.mm(pss[2 + g].t[:], bt.t[:, g * 128:(g + 1) * 128], xdd.t[:, g * 512:(g + 1) * 512], True, True, [bt, xdd], [pss[2 + g]], inc=True)
        k.tt("pool", h3(run_f.t[:]), h3(run_f.t[:]), bc(ex.t[:, 64:80].unsqueeze(2), [128, H, P]), ALU.mult, [run_f, ex], [run_f])
        for g in range(2):
            sl = slice(g * 512, (g + 1) * 512)
            k.tt("dve", run_f.t[:, sl], run_f.t[:, sl], pss[2 + g].t[:], ALU.add, [run_f, pss[2 + g]], [run_f])
        k.copy("act", runf_bf.t[:], run_f.t[:], [run_f], [runf_bf])
        for half in range(2):
            sl = slice(half * 512, (half + 1) * 512)
            pb = pss[4 + half]
            for kk in range(8):
                k.mm(pb.t[:], ygT.t[:, kk, :], wssd.t[:, kk, sl], kk == 0, kk == 7, [ygT, wssd], [pb], inc=(kk == 7))
            k.tt("dve", m1.t[:, sl], pb.t[:], g1.t[:, sl], ALU.mult, [pb, g1], [m1])
            k.tt("dve", mg.t[:, sl], m1.t[:, sl], g2l.t[:, sl], ALU.add, [m1, g2l], [mg])

        transpose_to(mgT, mg, pss[6], "dve")
        for half in range(2):
            sl = slice(half * 512, (half + 1) * 512)
            pb = pss[half]
            for kk in range(8):
                k.mm(pb.t[:], mgT.t[:, kk, :], wout.t[:, kk, sl], kk == 0, False, [mgT, wout], [pb], inc=False)
            k.mm(pb.t[:], ones_bf.t[0:2, :], b0_rows.t[0:2, sl], False, True, [ones_bf, b0_rows], [pb], inc=True)
            k.tt("dve", r1.t[:, sl], xo.t[:, sl], pb.t[:], ALU.add, [xo, pb], [r1])
        if c == 0:
            k.dump("r1", r1, r1.t[:], [128, D])
        ln_stats(r1, lst, lmv, lrs, lnm)
        k.act(xo.t[:], r1.t[:], AF.Identity, [r1, lrs, lnm], [xo], bias=lnm.t[:], scale=lrs.t[:])
        k.dma("pool", x1_s[c].t, xo.t[:], r=[xo], w=[x1_s[c]])
    k.pop()

    k.fill = None
    if KPASS < 3:
        k.barrier()
        k.es.close()
        return nc, k.dbg
    st = k.push()
    b1_rows = bias_rows(rows_d.t[6:7, :], D, ALPHA, "b1r")
    a1row = k.sb([128, D], F32, "a1row")
    k.dma("sp", a1row.t[:], rows_d.t[5:6, :].partition_broadcast(128), w=[a1row])
    k.ts("dve", a1row.t[:], a1row.t[:], float(ALPHA), None, ALU.mult, None, [a1row], [a1row])
    l2g = k.sb([128, D], F32, "l2g")
    k.dma("sp", l2g.t[:], rows_d.t[7:8, :].partition_broadcast(128), w=[l2g])
    l2b = k.sb([128, D], F32, "l2b")
    k.dma("sp", l2b.t[:], rows_d.t[8:9, :].partition_broadcast(128), w=[l2b])
    make_row_psum(40, "g2row", (pss[6], pss[7]))
    wf1 = k.sb([128, 8, DFF], BF16, "wf1")
    wf3 = k.sb([128, 8, DFF], BF16, "wf3")
    wf2 = k.sb([128, NFF, D], BF16, "wf2")
    FG = ((0, 6), (6, 12), (12, 17), (17, 22))
    wf1g = [Buf(wf1.t) for _ in FG]
    wf3g = [Buf(wf3.t) for _ in FG]
    for gi, (f0, f1) in enumerate(FG):
        for kk in range(8):
            k.dma("pool", wf1.t[:, kk, f0 * 128:f1 * 128], wff1_d.t[kk * 128:(kk + 1) * 128, f0 * 128:f1 * 128], w=[wf1g[gi]], partial=True)
        for kk in range(8):
            k.dma("pool", wf3.t[:, kk, f0 * 128:f1 * 128], wff3_d.t[kk * 128:(kk + 1) * 128, f0 * 128:f1 * 128], w=[wf3g[gi]], partial=True)
    load_w(wf2, wff2_d.t, NFF)

    def fgrp(f):
        return [gi for gi, (f0, f1) in enumerate(FG) if f0 <= f < f1][0]

    def scale_wf2():
        for kk in range(NFF):
            for half in range(2):
                sl = slice(half * 512, (half + 1) * 512)
                k.tt("dve", wf2.t[:, kk, sl], wf2.t[:, kk, sl], pss[6 + half].t[:], ALU.mult, [wf2, pss[6 + half]], [wf2])
    TB = 2
    NB = NR // TB
    x1t = [[k.sb([128, D], F32, "x1t") for _ in range(TB)] for _ in range(2)]
    xmT = [k.sb([128, 8, TB * 128], BF16, "xmT") for _ in range(2)]
    hid = k.sb([128, NFF, TB * 128], BF16, "hid")
    sl1 = [k.sb([128, TB * 128], F32, "sl1") for _ in range(2)]
    fst = k.sb([128, 2, 6], F32, "fst")
    fmv = k.sb([128, 2], F32, "fmv")
    frs = k.sb([128, 1], F32, "frs")
    fnm = k.sb([128, 1], F32, "fnm")
    NT = TB * 128
    print("SBUF free in pass 3:", nc.sbuf_bytes_remaining)

    def ffn_front(b):
        s = b % 2
        for t in range(TB):
            c = b * TB + t
            xt = x1t[s][t]
            k.dma("sp", xt.t[:], x1_s[c].t, r=[x1_s[c]], w=[xt])
            for cc in range(8):
                pb = pss[cc // 4]
                k.tr(pb.t[:, (cc % 4) * 128:(cc % 4 + 1) * 128], xt.t[:, cc * 128:(cc + 1) * 128], cm.t[:, IDN, :],
                     [xt, cm], [pb], inc=(cc % 4 == 3))
            for cc in range(8):
                pb = pss[cc // 4]
                src = pb.t[:, (cc % 4) * 128:(cc % 4 + 1) * 128]
                dst = xmT[s].t[:, cc, t * 128:(t + 1) * 128]
                if cc < 4:
                    k.ts("dve", dst, src, cols.t[:, S2, cc:cc + 1], cols.t[:, B2, cc:cc + 1], ALU.mult, ALU.add, [pb, cols], [xmT[s]])
                else:
                    k.act(dst, src, AF.Identity, [pb, cols], [xmT[s]], bias=cols.t[:, B2, cc:cc + 1], scale=cols.t[:, S2, cc:cc + 1])
            k.tt("pool", xt.t[:], xt.t[:], a1row.t[:], ALU.mult, [xt, a1row], [xt])

    def ffn_w13(b):
        s = b % 2
        for f in range(NFF):
            p1 = pss[2 + (f % 2) * 2]
            p3 = pss[3 + (f % 2) * 2]
            for kk in range(8):
                k.mm(p1.t[:, 0:NT], wf1.t[:, kk, f * 128:(f + 1) * 128], xmT[s].t[:, kk, :], kk == 0, kk == 7, [wf1g[fgrp(f)], xmT[s]], [p1], inc=(kk == 7))
            for kk in range(8):
                k.mm(p3.t[:, 0:NT], wf3.t[:, kk, f * 128:(f + 1) * 128], xmT[s].t[:, kk, :], kk == 0, kk == 7, [wf3g[fgrp(f)], xmT[s]], [p3], inc=(kk == 7))
            sv = sl1[f % 2]
            k.act(sv.t[:], p1.t[:, 0:NT], AF.Silu, [p1], [sv])
            k.tt("dve", hid.t[:, f, :], sv.t[:], p3.t[:, 0:NT], ALU.mult, [sv, p3], [hid])

    def ffn_w2(b):
        s = b % 2
        for t in range(TB):
            c = b * TB + t
            xt = x1t[s][t]
            r2 = xt
            for half in range(2):
                sl = slice(half * 512, (half + 1) * 512)
                pb = pss[6 + half]
                for f in range(NFF):
                    k.mm(pb.t[:], hid.t[:, f, t * 128:(t + 1) * 128], wf2.t[:, f, sl], f == 0, False, [hid, wf2], [pb], inc=False)
                k.mm(pb.t[:], ones_bf.t[0:2, :], b1_rows.t[0:2, sl], False, True, [ones_bf, b1_rows], [pb], inc=True)
                k.tt("dve", r2.t[:, sl], r2.t[:, sl], pb.t[:], ALU.add, [r2, pb], [r2])
            ln_stats(r2, fst, fmv, frs, fnm)
            o = r2
            k.act(o.t[:], r2.t[:], AF.Identity, [r2, frs, fnm], [o], bias=fnm.t[:], scale=frs.t[:])
            k.tt("pool", o.t[:], o.t[:], l2g.t[:], ALU.mult, [o, l2g], [o])
            k.tt("dve", o.t[:], o.t[:], l2b.t[:], ALU.add, [o, l2b], [o])
            k.dma("pool", out_d.t[c * 128:(c + 1) * 128, :], o.t[:], r=[o], w=[out_d])

    if NB > 0:
        ffn_front(0)
    for b in range(NB):
        ffn_w13(b)
        if b + 1 < NB:
            ffn_front(b + 1)
        if b == 0:
            scale_wf2()
        ffn_w2(b)
    k.pop()
    k.barrier()
    k.es.close()
    return nc, k.dbg


def _prep(inputs):
    f = lambda a: np.ascontiguousarray(np.asarray(a, dtype=np.float32))
    i = {kk: f(v) for kk, v in inputs.items()}
    kk = np.arange(128)
    cm = np.stack([np.eye(128), kk[:, None] <= kk[None, :], kk[:, None] >= kk[None, :],
                   kk[:, None] > kk[None, :], kk[:, None] < kk[None, :]], axis=1).astype(np.float32)
    col = lambda v: f(v.reshape(-1, 128).T)
    shared = dict(
        cm=f(cm), w_ada=i["w_ada"][0], b_ada_c=col(i["b_ada"][0]),
        colp=f(np.stack([col(i["ln0_g"]), col(i["ln0_b"]), col(i["ln1_g"][0]), col(i["ln1_b"][0])], axis=1)),
        convw_c=f(i["conv_w"][0].T.reshape(12, 128, 5).transpose(1, 0, 2)),
        convb_c=col(i["conv_b"][0]), w_in=i["w_in"][0],
        rows=f(np.stack([i["ln0_g"], i["ln0_b"], i["ssd_norm_g"][0], i["gm_norm_g"][0], i["gm_norm_b"][0],
                         i["ln1_g"][0], i["ln1_b"][0], i["ln2_g"][0], i["ln2_b"][0]])),
        b_gate=f(i["b_gate"][0][None, :]),
        small=f(np.concatenate([i["dt_bias"][0].reshape(-1), i["a_log"][0].reshape(-1), i["d_skip"][0].reshape(-1)])[None, :]),
        wspT=f(i["w_spatial"][0].transpose(2, 0, 1)), wsp=f(i["w_spatial"][0].transpose(1, 0, 2)),
        bsp_c=f(i["b_spatial"][0].T),
        w_ssd=i["w_ssd_proj"][0], w_gm=i["w_gm_proj"][0], w_out=i["w_out"][0],
        w_ff1=i["w_ff1"][0], w_ff3=i["w_ff3"][0], w_ff2=i["w_ff2"][0],
    )
    maps = []
    for b in range(i["x"].shape[0]):
        m = dict(shared)
        m["x"] = i["x"][b]
        m["ctx"] = i["ctx"][b]
        m["cT"] = f(np.stack([i["c"][b], i["c_ctx"]], axis=1).reshape(8, 128, 2).transpose(1, 0, 2))
        maps.append(m)
    return maps


def kernel(**inputs):
    maps = _prep(inputs)
    nc, _ = build()
    n = len(maps)
    res = run_bass_kernel_spmd(nc, maps, core_ids=list(range(n)))
    return np.stack([np.asarray(r["out"], dtype=np.float32) for r in res.results], axis=0)
```

```python
import os
from contextlib import ExitStack
import numpy as np
import concourse.bass as bass
import concourse.mybir as mybir
from concourse.bass_utils import run_bass_kernel_spmd

F32 = mybir.dt.float32
F32R = mybir.dt.float32r
BF16 = mybir.dt.bfloat16
AF = mybir.ActivationFunctionType
ALU = mybir.AluOpType

D = 1024
SEQ = 4096
CTX = 256
NCH = SEQ // 128
H = 16
P = 64
NST = 128
DFF = 2816
NFF = DFF // 128
XBC = 1536
DPROJ = 6688
O_Z, O_XBC, O_DT, O_U, O_V, O_G = 0, 1024, 2560, 2592, 3616, 4640
ALPHA = 2.0 ** 0.25
EPS = 1e-5
DEBUG = bool(int(os.environ.get("KDEBUG", "0")))
NCH_RUN = int(os.environ.get("KNCH", str(NCH)))
KPASS = int(os.environ.get("KPASS", "3"))
NFILL = int(os.environ.get("KFILL", "3"))


class Buf:
    def __init__(self, t):
        self.t = t
        self.w = None
        self.pw = []
        self.r = {}
        self.psum = False

    def __getitem__(self, idx):
        return self.t[idx]


class Eng:
    def __init__(self, e, sem, name):
        self.e, self.sem, self.name = e, sem, name
        self.count = 0
        self.seen = {}
        self.pend_r, self.pend_w = [], []


class KB:
    def __init__(self):
        self.nc = bass.Bass("TRN2", target_bir_lowering=False)
        nc = self.nc
        self.es = ExitStack()
        self.stacks = [self.es]
        self.engs = {}
        for name, e in (("pe", nc.tensor), ("act", nc.scalar), ("dve", nc.vector), ("pool", nc.gpsimd), ("sp", nc.sync)):
            sem = self.es.enter_context(nc.semaphore("s_" + name))
            self.engs[name] = Eng(e, sem, name)
        self.dpool = {}
        for q, n in (("sp", 20), ("pool", 20), ("act", 6)):
            self.dpool[q] = [[self.es.enter_context(nc.semaphore(f"d_{q}{i}")), 0] for i in range(n)]
        self.dnext = {q: 0 for q in self.dpool}
        self.uid = 0
        self.dbg = []
        self.fill = None

    def sb(self, shape, dt, name=None):
        self.uid += 1
        t = self.stacks[-1].enter_context(self.nc.sbuf_tensor(f"{name or 't'}_{self.uid}", list(shape), dt))
        return Buf(t)

    def ps(self, name):
        self.uid += 1
        t = self.stacks[-1].enter_context(self.nc.psum_tensor(f"{name}_{self.uid}", [128, 512], F32))
        b = Buf(t)
        b.psum = True
        return b

    def dram(self, name, shape, dt, kind="Internal"):
        return Buf(self.nc.dram_tensor(name, list(shape), dt, kind=kind).ap())

    def push(self):
        print("SBUF remaining at push:", self.nc.sbuf_bytes_remaining)
        st = ExitStack()
        self.stacks.append(st)
        return st

    def pop(self):
        self.barrier()
        self.stacks.pop().close()

    def _waits(self, E, r, w, partial=False):
        deps = {}
        raw = set()
        for b in r:
            if b.w is not None:
                deps[(b.w[0], b.w[1])] = b.w
                raw.add((b.w[0], b.w[1]))
            for t in b.pw:
                deps[(t[0], t[1])] = t
            if b.psum:
                for t in b.r.values():
                    if t[2] is not E:
                        deps[(t[0], t[1])] = t
        for b in w:
            if b.w is not None:
                deps[(b.w[0], b.w[1])] = b.w
            if not partial:
                for t in b.pw:
                    deps[(t[0], t[1])] = t
            for t in b.r.values():
                deps[(t[0], t[1])] = t
        for key, (sem, val, owner) in deps.items():
            if owner is E:
                if E.name == "pe":
                    continue
                if key not in raw:
                    continue
            if E.seen.get(sem, 0) >= val:
                continue
            if E.name == "pe" and self.fill is not None and owner is not None:
                for v in range(max(E.seen.get(sem, 0) + 1, val - NFILL), val):
                    E.e.wait_ge(sem, v)
                    self.nc.tensor.matmul(self.fill[0], self.fill[1], self.fill[2], start=True, stop=True)
            E.e.wait_ge(sem, val)
            E.seen[sem] = val

    def op(self, eng, fn, r=(), w=(), inc=True):
        E = self.engs[eng]
        self._waits(E, r, w)
        ins = fn()
        E.pend_r.extend(r)
        E.pend_w.extend(w)
        if inc:
            E.count += 1
            ins.then_inc(E.sem, 1)
            tok = (E.sem, E.count, E)
            for b in E.pend_r:
                b.r[E.sem] = tok
            for b in E.pend_w:
                b.w = tok
                b.pw = []
                b.r = {}
            E.pend_r, E.pend_w = [], []
        return ins

    def dma(self, q, out, in_, r=(), w=(), partial=False, **kw):
        E = self.engs[q]
        self._waits(E, r, w, partial)
        pool = self.dpool[q]
        i = self.dnext[q]
        self.dnext[q] = (i + 1) % len(pool)
        sem, val = pool[i]
        if val > 0 and E.seen.get(sem, 0) < val:
            E.e.wait_ge(sem, val)
            E.seen[sem] = val
        ins = E.e.dma_start(out=out, in_=in_, **kw)
        val += 16
        pool[i][1] = val
        ins.then_inc(sem, 16)
        tok = (sem, val, None)
        for b in r:
            b.r[sem] = tok
        for b in w:
            if partial:
                b.pw.append(tok)
            else:
                b.w = tok
                b.pw = []
            b.r = {}

    def barrier(self):
        toks = [(E.sem, E.count) for E in self.engs.values() if E.count > 0]
        for pool in self.dpool.values():
            toks += [(s, v) for s, v in pool if v > 0]
        for E in self.engs.values():
            assert not E.pend_r and not E.pend_w, E.name
            for sem, val in toks:
                if sem is E.sem or E.seen.get(sem, 0) >= val:
                    continue
                E.e.wait_ge(sem, val)
                E.seen[sem] = val

    def act(self, out, in_, func, r, w, bias=None, scale=None, accum=None):
        kw = {}
        if bias is not None:
            kw["bias"] = bias
        if scale is not None:
            kw["scale"] = scale
        if accum is not None:
            kw["accum_out"] = accum
        return self.op("act", lambda: self.nc.scalar.activation(out=out, in_=in_, func=func, **kw), r, w)

    def tt(self, eng, out, in0, in1, op, r, w):
        e = self.engs[eng].e
        return self.op(eng, lambda: e.tensor_tensor(out=out, in0=in0, in1=in1, op=op), r, w)

    def ts(self, eng, out, in0, s1, s2, op0, op1, r, w):
        e = self.engs[eng].e
        if s2 is None:
            return self.op(eng, lambda: e.tensor_scalar(out=out, in0=in0, scalar1=s1, scalar2=None, op0=op0), r, w)
        return self.op(eng, lambda: e.tensor_scalar(out=out, in0=in0, scalar1=s1, scalar2=s2, op0=op0, op1=op1), r, w)

    def stt(self, out, in0, scalar, in1, op0, op1, r, w):
        return self.op("dve", lambda: self.nc.vector.scalar_tensor_tensor(
            out=out, in0=in0, scalar=scalar, in1=in1, op0=op0, op1=op1), r, w)

    def copy(self, eng, out, in_, r, w):
        e = self.engs[eng].e
        if eng == "act":
            return self.act(out, in_, AF.Copy, r, w)
        return self.op(eng, lambda: e.tensor_copy(out=out, in_=in_), r, w)

    def memset(self, eng, ap, val, w):
        e = self.engs[eng].e
        return self.op(eng, lambda: e.memset(ap, val), (), w)

    def mm(self, out, lhsT, rhs, start, stop, r, w, inc):
        return self.op("pe", lambda: self.nc.tensor.matmul(out, lhsT, rhs, start=start, stop=stop), r, w, inc)

    def tr(self, out, in_, ident, r, w, inc):
        return self.op("pe", lambda: self.nc.tensor.transpose(out, in_, ident), r, w, inc)

    def dump(self, name, buf, ap, shape, dt=F32):
        if not DEBUG:
            return
        d = self.dram("dbg_" + name, shape, dt, kind="ExternalOutput")
        self.dma("sp", d.t, ap, r=[buf], w=[d])
        self.dbg.append("dbg_" + name)


def bc(ap, shape):
    return ap.to_broadcast(list(shape))


def build():
    k = KB()
    nc = k.nc

    def din(name, shape, dt=F32):
        return k.dram(name, shape, dt, kind="ExternalInput")

    x_d = din("x", [SEQ, D])
    ctx_d = din("ctx", [CTX, D])
    cT_d = din("cT", [128, 8, 2])
    cm_d = din("cm", [128, 5, 128])
    wada_d = din("w_ada", [D, 6 * D])
    badac_d = din("b_ada_c", [128, 48])
    colp_d = din("colp", [128, 4, 8])
    convw_d = din("convw_c", [128, 12, 5])
    convb_d = din("convb_c", [128, 12])
    win_d = din("w_in", [D, DPROJ])
    rows_d = din("rows", [9, D])
    bgate_d = din("b_gate", [1, 2 * D])
    small_d = din("small", [1, 96])
    wspT_d = din("wspT", [128, 8, 128])
    wsp_d = din("wsp", [128, 8, 128])
    bsp_d = din("bsp_c", [128, 8])
    wssd_d = din("w_ssd", [D, D])
    wgm_d = din("w_gm", [D, D])
    wout_d = din("w_out", [D, D])
    wff1_d = din("w_ff1", [D, DFF])
    wff3_d = din("w_ff3", [D, DFF])
    wff2_d = din("w_ff2", [DFF, D])
    out_d = k.dram("out", [SEQ, D], F32, kind="ExternalOutput")

    xs_s = [k.dram(f"xs_s{c}", [128, D], BF16) for c in range(NCH)]
    bt_s = [k.dram(f"bt_s{c}", [128, 256], BF16) for c in range(NCH)]
    bc_s = [k.dram(f"bc_s{c}", [128, 4, 128], BF16) for c in range(NCH)]
    hp_s = [k.dram(f"hp_s{c}", [128, D], BF16) for c in range(NCH)]
    g2_s = [k.dram(f"g2_s{c}", [128, D], BF16) for c in range(NCH)]
    x1_s = [k.dram(f"x1_s{c}", [128, D], F32) for c in range(NCH)]

    cm = k.sb([128, 5, 128], F32, "cm")
    k.dma("sp", cm.t[:], cm_d.t, w=[cm])
    IDN, MLE, MGE, MGT, MLT = range(5)
    ident_bf = k.sb([128, 128], BF16, "identbf")
    k.copy("dve", ident_bf.t[:], cm.t[:, IDN, :], [cm], [ident_bf])
    mgt_r = k.sb([128, 128], F32R, "mgtr")
    mlt_r = k.sb([128, 128], F32R, "mltr")
    k.copy("dve", mgt_r.t[:], cm.t[:, MGT, :], [cm], [mgt_r])
    k.copy("dve", mlt_r.t[:], cm.t[:, MLT, :], [cm], [mlt_r])
    ones_f = k.sb([128, 128], F32, "onesf")
    k.memset("dve", ones_f.t[:], 1.0, [ones_f])
    ones_bf = k.sb([2, 128], BF16, "onesbf")
    k.memset("dve", ones_bf.t[:], 1.0, [ones_bf])

    colp = k.sb([128, 4, 8], F32, "colp")
    k.dma("sp", colp.t[:], colp_d.t, w=[colp])
    convw = k.sb([128, 12, 5], F32, "convw")
    k.dma("sp", convw.t[:], convw_d.t, w=[convw])
    convb = k.sb([128, 12], F32, "convb")
    k.dma("sp", convb.t[:], convb_d.t, w=[convb])
    smallr = k.sb([128, 96], F32, "smallr")
    k.dma("sp", smallr.t[:], small_d.t.partition_broadcast(128), w=[smallr])
    arow = k.sb([128, 32], F32, "arow")
    k.act(arow.t[:], smallr.t[:, 32:64], AF.Exp, [smallr], [arow])
    k.ts("dve", arow.t[:], arow.t[:], -1.0, None, ALU.mult, None, [arow], [arow])
    dsum = k.sb([128, 16], F32, "dsum")
    k.tt("dve", dsum.t[:], smallr.t[:, 64:80], smallr.t[:, 80:96], ALU.add, [smallr], [dsum])
    dh = k.sb([128, H, 128], BF16, "dh")
    k.tt("dve", dh.t[:], bc(cm.t[:, IDN, :].unsqueeze(1), [128, H, 128]),
         bc(dsum.t[:].unsqueeze(2), [128, H, 128]), ALU.mult, [cm, dsum], [dh])
    dt_all = k.sb([128, NCH, 32], F32, "dtall")
    run_f = k.sb([128, D], F32, "runf")
    run_b = k.sb([128, D], F32, "runb")
    modc = k.sb([128, 48, 2], F32, "modc")
    cols = k.sb([128, 6, 8], F32, "cols")
    S1X, B1X, S1C, B1C, S2, B2 = range(6)
    lntmp = k.sb([128, 1], F32, "lntmp")
    epsc = k.sb([128, 1], F32, "epsc")
    k.memset("dve", epsc.t[:], EPS, [epsc])

    pss = [k.ps(f"ps{i}") for i in range(8)]
    fill_spec = (pss[7].t[:, 0:256], ident_bf.t[:], dh.t[:, 0:2, :].rearrange("p a i -> p (a i)"))

    st = k.push()
    w1 = k.sb([128, 8, 1568 + 2048 + 1024], BF16, "w1")
    W_XBC, W_DT, W_U, W_V, W_G2 = 0, 1536, 1568, 2592, 3616
    for kk in range(8):
        rs_ = slice(kk * 128, (kk + 1) * 128)
        k.dma("pool", w1.t[:, kk, 0:1568], win_d.t[rs_, O_XBC:O_XBC + 1568], r=[win_d], w=[w1], partial=True)
    for kk in range(8):
        rs_ = slice(kk * 128, (kk + 1) * 128)
        k.dma("pool", w1.t[:, kk, 1568:3616], win_d.t[rs_, O_U:O_U + 2048], r=[win_d], w=[w1], partial=True)
        k.dma("pool", w1.t[:, kk, 3616:4640], win_d.t[rs_, O_G + D:O_G + 2 * D], r=[win_d], w=[w1], partial=True)
    wgm = k.sb([128, 8, D], BF16, "wgm")
    for kk in range(8):
        k.dma("pool", wgm.t[:, kk, :], wgm_d.t[kk * 128:(kk + 1) * 128, :], w=[wgm], partial=True)
    wspT = k.sb([128, 8, 128], BF16, "wspT")
    k.dma("pool", wspT.t[:], wspT_d.t, w=[wspT])
    st = k.push()
    cT = k.sb([128, 8, 2], F32, "cT")
    k.dma("sp", cT.t[:], cT_d.t, w=[cT])
    scT = k.sb([128, 8, 2], F32, "scT")
    k.act(scT.t[:], cT.t[:], AF.Silu, [cT], [scT])
    badac = k.sb([128, 48], F32, "badac")
    k.dma("sp", badac.t[:], badac_d.t, w=[badac])
    wst = [k.sb([128, 8, 512], F32, f"wst{i}") for i in range(4)]
    wada_v = wada_d.t.rearrange("(k p) n -> p k n", p=128)
    modrow = k.sb([2, 6 * D], F32, "modrow")
    for cb in range(12):
        wb = wst[cb % 4]
        k.dma("sp", wb.t[:], wada_v[:, :, cb * 512:(cb + 1) * 512], r=[wada_d], w=[wb])
        pb = pss[cb % 2]
        for kk in range(8):
            k.mm(pb.t[0:2, :], scT.t[:, kk, :], wb.t[:, kk, :], kk == 0, kk == 7, [scT, wb], [pb], inc=(kk == 7))
        k.copy("act", modrow.t[:, cb * 512:(cb + 1) * 512], pb.t[0:2, :], [pb], [modrow])
    pm = pss[2]
    for j in range(48):
        k.mm(pm.t[:, 2 * j:2 * j + 2], modrow.t[0:2, j * 128:(j + 1) * 128], cm.t[0:2, IDN, 0:2], True, True,
             [modrow, cm], [pm], inc=(j == 47))
    k.tt("dve", modc.t[:], pm.t[:, 0:96].rearrange("p (j m) -> p j m", m=2),
         bc(badac.t[:].unsqueeze(2), [128, 48, 2]), ALU.add, [pm, badac], [modc])
    tmp8 = k.sb([128, 8], F32, "tmp8")
    for (si, bi, m, g_i, b_i, sc_o, sh_o) in ((S1X, B1X, 0, 0, 1, 8, 0), (S1C, B1C, 1, 0, 1, 8, 0), (S2, B2, 0, 2, 3, 32, 24)):
        k.ts("dve", tmp8.t[:], modc.t[:, sc_o:sc_o + 8, m], 1.0, None, ALU.add, None, [modc], [tmp8])
        k.tt("dve", cols.t[:, si, :], tmp8.t[:], colp.t[:, g_i, :], ALU.mult, [tmp8, colp], [cols])
        k.tt("dve", cols.t[:, bi, :], tmp8.t[:], colp.t[:, b_i, :], ALU.mult, [tmp8, colp], [cols])
        k.tt("dve", cols.t[:, bi, :], cols.t[:, bi, :], modc.t[:, sh_o:sh_o + 8, m], ALU.add, [cols, modc], [cols])
    k.pop()
    k.dump("modc", modc, modc.t[:].rearrange("p j m -> p (j m)"), [128, 96])

    def make_row_psum(off, name, banks):
        dg = k.sb([128, 128], F32, name + "dg")
        for c in range(8):
            k.ts("dve", dg.t[:], cm.t[:, IDN, :], modc.t[:, off + c, 0:1], None, ALU.mult, None, [cm, modc], [dg])
            pb = banks[c // 4]
            k.mm(pb.t[:, (c % 4) * 128:(c % 4 + 1) * 128], ones_f.t[:], dg.t[:], True, True, [ones_f, dg], [pb], True)

    def make_row(off, name):
        row = k.sb([128, D], F32, name)
        dg = k.sb([128, 128], F32, name + "dg")
        for c in range(8):
            k.ts("dve", dg.t[:], cm.t[:, IDN, :], modc.t[:, off + c, 0:1], None, ALU.mult, None, [cm, modc], [dg])
            pb = pss[1 + (c // 4)]
            k.mm(pb.t[:, (c % 4) * 128:(c % 4 + 1) * 128], ones_f.t[:], dg.t[:], True, True, [ones_f, dg], [pb], True)
        k.copy("act", row.t[:, 0:512], pss[1].t[:], [pss[1]], [row])
        k.copy("act", row.t[:, 512:1024], pss[2].t[:], [pss[2]], [row])
        return row

    def load_w(dst, src_ap, k_chunks, eng="pool"):
        for kk in range(k_chunks):
            k.dma(eng, dst.t[:, kk, :], src_ap[kk * 128:(kk + 1) * 128, :], w=[dst], partial=True)

    def bias_rows(src_row_ap, n, scale, name):
        rows = k.sb([2, n], BF16, name)
        k.push()
        b32 = k.sb([1, n], F32, name + "32")
        k.dma("sp", b32.t[:], src_row_ap, w=[b32])
        if scale != 1.0:
            k.ts("dve", b32.t[:], b32.t[:], float(scale), None, ALU.mult, None, [b32], [b32])
        k.copy("dve", rows.t[0:1, :], b32.t[:], [b32], [rows])
        lo32 = k.sb([1, n], F32, name + "lo32")
        k.tt("dve", lo32.t[:], b32.t[:], rows.t[0:1, :], ALU.subtract, [b32, rows], [lo32])
        lo = k.sb([1, n], BF16, name + "lo")
        k.copy("dve", lo.t[:], lo32.t[:], [lo32], [lo])
        k.dma("sp", rows.t[1:2, :], lo.t[:], r=[lo], w=[rows])
        k.pop()
        return rows

    def rsqrt_act(out_buf, in_ap, in_buf, scale):
        k.act(lntmp.t[:], in_ap, AF.Ln, [in_buf, epsc], [lntmp], bias=epsc.t[:], scale=float(scale))
        k.act(out_buf.t[:], lntmp.t[:], AF.Exp, [lntmp], [out_buf], scale=-0.5)

    def ln_stats(src, tmp_st, tmp_mv, rs, nm):
        for i in range(2):
            k.op("dve", lambda i=i: nc.vector.bn_stats(out=tmp_st.t[:, i, :], in_=src.t[:, i * 512:(i + 1) * 512]), [src], [tmp_st])
        k.op("dve", lambda: nc.vector.bn_aggr(out=tmp_mv.t[:], in_=tmp_st.t[:].rearrange("p a b -> p (a b)")), [tmp_st], [tmp_mv])
        rsqrt_act(rs, tmp_mv.t[:, 1:2], tmp_mv, 1.0)
        k.ts("dve", nm.t[:], tmp_mv.t[:, 0:1], rs.t[:], -1.0, ALU.mult, ALU.mult, [tmp_mv, rs], [nm])

    def proj_tok(hT, W, c0, n, pb, brow=None, b0=0):
        for kk in range(8):
            last = (kk == 7 and brow is None)
            k.mm(pb.t[:, 0:n], hT.t[:, kk, :], W.t[:, kk, c0:c0 + n], kk == 0, last, [hT, W], [pb], inc=last)
        if brow is not None:
            k.mm(pb.t[:, 0:n], ones_bf.t[0:2, :], brow.t[0:2, b0:b0 + n], False, True, [ones_bf, brow], [pb], inc=True)

    def transpose_to(dstT, src, pb, eng):
        pv = pb.t[:].bitcast(BF16)
        for c in range(8):
            k.tr(pv[:, c * 128:(c + 1) * 128], src.t[:, c * 128:(c + 1) * 128], ident_bf.t[:], [src, ident_bf], [pb], inc=(c == 7))
        k.copy(eng, dstT.t[:].rearrange("p a t -> p (a t)"), pv, [pb], [dstT])

    class LN0:
        def __init__(self, nh=2):
            self.nh = nh
            self.xt = [k.sb([128, D], F32, "xt") for _ in range(2)]
            self.xh = [k.sb([128, D], F32, "xh") for _ in range(nh)]
            self.hT = [k.sb([128, 8, 128], BF16, "hT") for _ in range(2)]
            self.st = k.sb([128, 2, 6], F32, "lnst")
            self.mv = k.sb([128, 2], F32, "lnmv")
            self.rs = k.sb([128, 1], F32, "lnrs")
            self.nm = k.sb([128, 1], F32, "lnnm")
            self.i = 0

        def run(self, src_buf, src_ap, si, bi, pbanks):
            s = self.i % 2
            self.i += 1
            xt, xh, hT = self.xt[s], self.xh[s % self.nh], self.hT[s]
            k.dma("sp", xt.t[:], src_ap, r=[src_buf], w=[xt])
            ln_stats(xt, self.st, self.mv, self.rs, self.nm)
            k.act(xh.t[:], xt.t[:], AF.Identity, [xt, self.rs, self.nm], [xh], bias=self.nm.t[:], scale=self.rs.t[:])
            for c in range(8):
                pb = pbanks[c // 4]
                k.tr(pb.t[:, (c % 4) * 128:(c % 4 + 1) * 128], xh.t[:, c * 128:(c + 1) * 128], cm.t[:, IDN, :],
                     [xh, cm], [pb], inc=(c % 4 == 3))
            for c in range(8):
                pb = pbanks[c // 4]
                src = pb.t[:, (c % 4) * 128:(c % 4 + 1) * 128]
                if c < 4:
                    k.ts("dve", hT.t[:, c, :], src, cols.t[:, si, c:c + 1], cols.t[:, bi, c:c + 1], ALU.mult, ALU.add, [pb, cols], [hT])
                else:
                    k.act(hT.t[:, c, :], src, AF.Identity, [pb, cols], [hT], bias=cols.t[:, bi, c:c + 1], scale=cols.t[:, si, c:c + 1])
            return xh, hT

    gmgrow = k.sb([128, D], F32, "gmgrow")
    k.dma("sp", gmgrow.t[:], rows_d.t[3:4, :].partition_broadcast(128), w=[gmgrow])
    biasM = k.sb([128, 8, 128], F32, "biasM")
    k.dma("sp", biasM.t[:].rearrange("p g c -> p (g c)"), rows_d.t[4:5, :].partition_broadcast(128), w=[biasM])
    gv = k.sb([128, D], F32, "gv")
    wsp_v = gv.t[:].rearrange("p (g q) -> p g q", g=8)
    k.dma("sp", wsp_v, wsp_d.t, w=[gv])
    rsum = k.sb([128, 8], F32, "rsum")
    k.op("dve", lambda: nc.vector.reduce_sum(out=rsum.t[:], in_=wsp_v, axis=mybir.AxisListType.X), [gv], [rsum])
    bspc = k.sb([128, 8], F32, "bspc")
    k.dma("sp", bspc.t[:], bsp_d.t, w=[bspc])
    k.tt("dve", biasM.t[:], biasM.t[:], bc(rsum.t[:].unsqueeze(2), [128, 8, 128]), ALU.mult, [biasM, rsum], [biasM])
    k.tt("dve", biasM.t[:], biasM.t[:], bc(bspc.t[:].unsqueeze(2), [128, 8, 128]), ALU.add, [biasM, bspc], [biasM])
    dtb_rows = bias_rows(small_d.t[0:1, 0:32], 32, 1.0, "dtb")
    bg2_rows = bias_rows(bgate_d.t[0:1, D:2 * D], D, 1.0, "bg2")

    if NFILL > 0:
        k.fill = fill_spec
    ln0 = LN0(2)
    pre = [k.sb([128, 12, 132], BF16, f"pre{i}") for i in range(3)]
    dgw = k.sb([128, 12, 5, 128], BF16, "dgw")
    for cc in range(12):
        for tap in range(5):
            k.ts("pool" if (cc + tap) % 2 else "dve", dgw.t[:, cc, tap, :], cm.t[:, IDN, :], convw.t[:, cc, tap:tap + 1], None,
                 ALU.mult, None, [cm, convw], [dgw])
    xbcT = k.sb([128, 12, 128], BF16, "xbcT")
    xs_tok = k.sb([128, D], BF16, "xs_tok")
    b_tok = k.sb([128, 256], BF16, "b_tok")
    a_sb = k.sb([128, 32], F32, "a_sb")
    ex = k.sb([128, 96], F32, "ex")
    ddv = k.sb([128, 32], F32, "ddv")
    xdd = k.sb([128, D], BF16, "xdd")
    hp_bf = k.sb([128, D], BF16, "hp_bf")
    e_t = k.sb([128, 32], F32, "e_t")
    decf1 = k.sb([128, 16], F32, "decf1")
    gu = k.sb([128, D], F32, "gu")
    sf1 = gu
    vhat = k.sb([128, D], BF16, "vhat")
    g2 = k.sb([128, D], F32, "g2")
    ygm = gv
    ygm_bf = k.sb([128, D], BF16, "ygm_bf")
    ygmT = k.sb([128, 8, 128], BF16, "ygmT")
    G2 = k.sb([128, D], BF16, "G2")
    vst = k.sb([128, 2, 6], F32, "vst")
    vmv = k.sb([128, 2], F32, "vmv")
    vrs = k.sb([128, 1], F32, "vrs")
    vnm = k.sb([128, 1], F32, "vnm")

    k.memset("dve", run_f.t[:], 0.0, [run_f])
    k.memset("dve", run_b.t[:], 0.0, [run_b])
    print("SBUF free in pass 1:", nc.sbuf_bytes_remaining)

    def small_exps(dt_ap, dt_buf):
        k.tt("dve", a_sb.t[:], dt_ap, arow.t[:], ALU.mult, [dt_buf, arow], [a_sb])
        pb = pss[4]
        specs = ((MLE, 0), (MGE, 16), (MGT, 0), (MLT, 16))
        for i, (mi, ao) in enumerate(specs):
            k.mm(pb.t[:, i * 16:(i + 1) * 16], cm.t[:, mi, :], a_sb.t[:, ao:ao + 16], True, True, [cm, a_sb], [pb], inc=False)
        k.mm(pb.t[:, 64:96], ones_f.t[:], a_sb.t[:, 0:32], True, True, [ones_f, a_sb], [pb], inc=True)
        k.act(ex.t[:], pb.t[:, 0:96], AF.Exp, [pb], [ex])

    order1 = [("c", 1), ("c", 0)] + [("x", cc_) for cc_ in range(NCH_RUN - 1, -1, -1)]
    lncache = {}

    def ln_for(i):
        if i >= len(order1) or i in lncache:
            return
        kind, cc_ = order1[i]
        if kind == "c":
            lncache[i] = ln0.run(ctx_d, ctx_d.t[cc_ * 128:(cc_ + 1) * 128, :], S1C, B1C, (pss[0], pss[1]))
        else:
            lncache[i] = ln0.run(x_d, x_d.t[cc_ * 128:(cc_ + 1) * 128, :], S1X, B1X, (pss[0], pss[1]))

    def s1(seq_buf, seq_ap, c, n, si, bi, slot_of, is_ctx):
        oi = order1.index(("c" if is_ctx else "x", c))
        ln_for(oi)
        xh, hT = lncache.pop(oi)
        for cc in range(12):
            pb = pss[2 + cc // 4]
            for kk in range(8):
                k.mm(pb.t[:, (cc % 4) * 128:(cc % 4 + 1) * 128], w1.t[:, kk, W_XBC + cc * 128:W_XBC + (cc + 1) * 128],
                     hT.t[:, kk, :], kk == 0, kk == 7, [w1, hT], [pb], inc=(kk == 7))
        me = pre[slot_of(c)]
        for q in range(3):
            pv = pss[2 + q].t[:].rearrange("p (a t) -> p a t", a=4)
            k.act(me.t[:, q * 4:(q + 1) * 4, 2:130], pv, AF.Copy, [pss[2 + q]], [me])
            KH = os.environ.get("KHALO", "")
            if c + 1 < n and KH != "skipL":
                k.copy("dve", pre[slot_of(c + 1)].t[:, q * 4:(q + 1) * 4, 0:2], pv[:, :, 126:128], [pss[2 + q]], [pre[slot_of(c + 1)]])
            if c - 1 >= 0 and KH != "skipR":
                k.copy("dve", pre[slot_of(c - 1)].t[:, q * 4:(q + 1) * 4, 130:132], pv[:, :, 0:2], [pss[2 + q]], [pre[slot_of(c - 1)]])
        if c == n - 1:
            k.memset("dve", me.t[:, :, 130:132], 0.0, [me])
        if c == 0:
            k.memset("dve", me.t[:, :, 0:2], 0.0, [me])
        yield "halo"
        pb = pss[5]
        proj_tok(hT, w1, W_DT, 32, pb, dtb_rows, 0)
        k.act(e_t.t[:], pb.t[:, 0:32], AF.Exp, [pb], [e_t])
        dts = dtc_all if is_ctx else dt_all
        k.act(dts.t[:, c, :], e_t.t[:], AF.Ln, [e_t], [dts], bias=1.0)
        if is_ctx or os.environ.get("KSKIP_GM"):
            ln_for(oi + 1)
            return
        yield
        for half in range(2):
            pb = pss[5 + half]
            proj_tok(hT, w1, W_U + half * 512, 512, pb)
            k.act(gu.t[:, half * 512:(half + 1) * 512], pb.t[:], AF.Gelu_apprx_tanh, [pb], [gu])
        yield
        for half in range(2):
            pb = pss[5 + half]
            proj_tok(hT, w1, W_V + half * 512, 512, pb)
            k.act(gv.t[:, half * 512:(half + 1) * 512], pb.t[:], AF.Gelu_apprx_tanh, [pb], [gv])
        yield
        ln_stats(gv, vst, vmv, vrs, vnm)
        k.act(vhat.t[:], gv.t[:], AF.Identity, [gv, vrs, vnm], [vhat], bias=vnm.t[:], scale=vrs.t[:])
        for half in range(2):
            pb = pss[5 + half]
            proj_tok(hT, w1, W_G2 + half * 512, 512, pb, bg2_rows, half * 512)
            k.act(g2.t[:, half * 512:(half + 1) * 512], pb.t[:], AF.Sigmoid, [pb], [g2])
        yield
        ln_for(oi + 1)
        yield
        for g in range(8):
            pb = pss[5 + g // 4]
            k.mm(pb.t[:, (g % 4) * 128:(g % 4 + 1) * 128], wspT.t[:, g, :], vhat.t[:, g * 128:(g + 1) * 128],
                 True, True, [wspT, vhat], [pb], inc=(g % 4 == 3))
        bM = biasM.t[:].rearrange("p g c -> p (g c)")
        for half in range(2):
            sl = slice(half * 512, (half + 1) * 512)
            pb = pss[5 + half]
            k.tt("dve", ygm.t[:, sl], pb.t[:], gmgrow.t[:, sl], ALU.mult, [pb, gmgrow], [ygm])
            k.tt("dve", ygm.t[:, sl], ygm.t[:, sl], bM[:, sl], ALU.add, [ygm, biasM], [ygm])
            k.tt("dve", ygm_bf.t[:, sl], ygm.t[:, sl], gu.t[:, sl], ALU.mult, [ygm, gu], [ygm_bf])
        yield
        transpose_to(ygmT, ygm_bf, pss[5], "dve")
        for half in range(2):
            pb = pss[5 + half]
            for kk in range(8):
                k.mm(pb.t[:], ygmT.t[:, kk, :], wgm.t[:, kk, half * 512:(half + 1) * 512], kk == 0, kk == 7,
                     [ygmT, wgm], [pb], inc=(kk == 7))
            k.tt("dve", G2.t[:, half * 512:(half + 1) * 512], pb.t[:], g2.t[:, half * 512:(half + 1) * 512], ALU.mult, [pb, g2], [G2])
        k.dma("pool", g2_s[c].t, G2.t[:], r=[G2], w=[g2_s[c]])

    def post(c, n, slot_of, is_ctx):
        if os.environ.get("KSKIP_POST") and not is_ctx:
            return
        me = pre[slot_of(c)]
        for cc in range(12):
            pb = pss[2 + cc // 4]
            for tap in range(5):
                k.mm(pb.t[:, (cc % 4) * 128:(cc % 4 + 1) * 128], dgw.t[:, cc, tap, :], me.t[:, cc, tap:tap + 128],
                     tap == 0, tap == 4, [dgw, me], [pb], inc=(tap == 4 and cc % 4 == 3))
        for cc in range(12):
            pb = pss[2 + cc // 4]
            k.act(xbcT.t[:, cc, :], pb.t[:, (cc % 4) * 128:(cc % 4 + 1) * 128], AF.Silu, [pb, convb], [xbcT], bias=convb.t[:, cc:cc + 1])
        yield

        pv0 = pss[0].t[:].bitcast(BF16)
        pv1 = pss[1].t[:].bitcast(BF16)
        for cix in range(8):
            k.tr(pv0[:, cix * 128:(cix + 1) * 128], xbcT.t[:, cix, :], ident_bf.t[:], [xbcT, ident_bf], [pss[0]], inc=(cix == 7))
        for cix in range(2):
            k.tr(pv1[:, cix * 128:(cix + 1) * 128], xbcT.t[:, 8 + cix, :], ident_bf.t[:], [xbcT, ident_bf], [pss[1]], inc=(cix == 1))
        k.copy("dve", xs_tok.t[:], pv0, [pss[0]], [xs_tok])
        k.copy("act", b_tok.t[:], pv1[:, 0:256], [pss[1]], [b_tok])
        if not is_ctx:
            k.dma("pool", xs_s[c].t, xs_tok.t[:], r=[xs_tok], w=[xs_s[c]])
            k.dma("pool", bt_s[c].t, b_tok.t[:], r=[b_tok], w=[bt_s[c]])
            k.dma("pool", bc_s[c].t, xbcT.t[:, 8:12, :], r=[xbcT], w=[bc_s[c]])
        yield
        dts = dtc_all if is_ctx else dt_all
        small_exps(dts.t[:, c, :], dts)
        k.tt("dve", ddv.t[:, 16:32], ex.t[:, 48:64], dts.t[:, c, 16:32], ALU.mult, [ex, dts], [ddv])
        if is_ctx:
            k.tt("dve", ddv.t[:, 0:16], ex.t[:, 32:48], dts.t[:, c, 0:16], ALU.mult, [ex, dts], [ddv])
        dirs = ((1, run_b),) + (((0, run_f),) if is_ctx else ())
        for d, run in dirs:
            yield
            k.tt("dve", xdd.t[:].rearrange("p (h q) -> p h q", h=H), xs_tok.t[:].rearrange("p (h q) -> p h q", h=H),
                 bc(ddv.t[:, d * 16:(d + 1) * 16].unsqueeze(2), [128, H, P]), ALU.mult, [xs_tok, ddv], [xdd])
            for g in range(2):
                pb = pss[2 + g]
                k.mm(pb.t[:], b_tok.t[:, g * 128:(g + 1) * 128], xdd.t[:, g * 512:(g + 1) * 512], True, True,
                     [b_tok, xdd], [pb], inc=True)
            dec = ex.t[:, 64 + d * 16:80 + d * 16]
            if is_ctx and d == 0:
                if c == 1:
                    for g in range(2):
                        k.copy("act", sf1.t[:, g * 512:(g + 1) * 512], pss[2 + g].t[:], [pss[2 + g]], [sf1])
                    k.copy("dve", decf1.t[:], dec, [ex], [decf1])
                else:
                    for g in range(2):
                        sl = slice(g * 512, (g + 1) * 512)
                        k.tt("dve", run_f.t[:, sl].rearrange("p (h q) -> p h q", h=8), pss[2 + g].t[:].rearrange("p (h q) -> p h q", h=8),
                             bc(decf1.t[:, g * 8:(g + 1) * 8].unsqueeze(2), [128, 8, P]), ALU.mult, [pss[2 + g], decf1], [run_f])
                        k.tt("dve", run_f.t[:, sl], run_f.t[:, sl], sf1.t[:, sl], ALU.add, [run_f, sf1], [run_f])
                continue
            if not is_ctx:
                k.copy("dve", hp_bf.t[:], run.t[:], [run], [hp_bf])
                k.dma("pool", hp_s[c].t, hp_bf.t[:], r=[hp_bf], w=[hp_s[c]])
            k.tt("dve", run.t[:].rearrange("p (h q) -> p h q", h=H), run.t[:].rearrange("p (h q) -> p h q", h=H),
                 bc(dec.unsqueeze(2), [128, H, P]), ALU.mult, [run, ex], [run])
            for g in range(2):
                sl = slice(g * 512, (g + 1) * 512)
                k.tt("dve", run.t[:, sl], run.t[:, sl], pss[2 + g].t[:], ALU.add, [run, pss[2 + g]], [run])

    def run_all(g):
        for _ in g:
            pass

    def interleave(ga, gb):
        da = db = False
        while not (da and db):
            if not da:
                try:
                    next(ga)
                except StopIteration:
                    da = True
            if not db:
                try:
                    next(gb)
                except StopIteration:
                    db = True

    dtc_all = k.sb([128, 2, 32], F32, "dtcall")
    for c in (1, 0):
        run_all(s1(ctx_d, ctx_d.t, c, 2, S1C, B1C, lambda cc: cc % 3, True))
        if c + 1 < 2:
            run_all(post(c + 1, 2, lambda cc: cc % 3, True))
    run_all(post(0, 2, lambda cc: cc % 3, True))
    k.dump("s_f", run_f, run_f.t[:], [128, D])
    k.dump("s_b", run_b, run_b.t[:], [128, D])
    NR = NCH_RUN
    KSTOP = int(os.environ.get("KSTOP", "1000"))
    steps = 0
    for c in range(NR - 1, -1, -1):
        if steps >= KSTOP:
            break
        ga = s1(x_d, x_d.t, c, NR, S1X, B1X, lambda cc: cc % 3, False)
        next(ga)
        steps += 1
        if c + 1 < NR:
            interleave(ga, post(c + 1, NR, lambda cc: cc % 3, False))
            steps += 1
        else:
            run_all(ga)
    if steps < KSTOP:
        run_all(post(0, NR, lambda cc: cc % 3, False))
    k.dump("dt_all", dt_all, dt_all.t[:].rearrange("p c d -> p (c d)"), [128, NCH * 32])
    k.pop()

    if KPASS < 2:
        k.barrier()
        k.es.close()
        return nc, k.dbg
    st = k.push()
    bg1_rows = bias_rows(bgate_d.t[0:1, 0:D], D, 1.0, "bg1")
    b0_rows = bias_rows(rows_d.t[1:2, :], D, ALPHA, "b0r")
    a0row = k.sb([128, D], F32, "a0row")
    k.dma("sp", a0row.t[:], rows_d.t[0:1, :].partition_broadcast(128), w=[a0row])
    k.ts("dve", a0row.t[:], a0row.t[:], float(ALPHA), None, ALU.mult, None, [a0row], [a0row])
    ngrow = k.sb([128, D], F32, "ngrow")
    k.dma("sp", ngrow.t[:], rows_d.t[2:3, :].partition_broadcast(128), w=[ngrow])
    make_row_psum(16, "g1row", (pss[4], pss[5]))
    wout = k.sb([128, 8, D], BF16, "wout")
    load_w(wout, wout_d.t, 8)
    w2 = k.sb([128, 8, 2048], BF16, "w2")
    for kk in range(8):
        rs_ = slice(kk * 128, (kk + 1) * 128)
        k.dma("pool", w2.t[:, kk, 0:1024], win_d.t[rs_, O_Z:O_Z + D], r=[win_d], w=[w2], partial=True)
        k.dma("pool", w2.t[:, kk, 1024:2048], win_d.t[rs_, O_G:O_G + D], r=[win_d], w=[w2], partial=True)
    wssd = k.sb([128, 8, D], BF16, "wssd")
    load_w(wssd, wssd_d.t, 8)

    def scale_wout():
        for kk in range(8):
            for half in range(2):
                sl = slice(half * 512, (half + 1) * 512)
                k.tt("dve", wout.t[:, kk, sl], wout.t[:, kk, sl], pss[4 + half].t[:], ALU.mult, [wout, pss[4 + half]], [wout])

    ln0 = LN0()
    xs_l = [k.sb([128, D], BF16, "xs_l") for _ in range(2)]
    bt_l = [k.sb([128, 256], BF16, "bt_l") for _ in range(2)]
    bc_l = [k.sb([128, 4, 128], BF16, "bc_l") for _ in range(2)]
    hp_l = [k.sb([128, D], BF16, "hp_l") for _ in range(2)]
    g2_l = [k.sb([128, D], BF16, "g2_l") for _ in range(2)]
    a_sb = k.sb([128, 32], F32, "a_sb2")
    ex = k.sb([128, 96], F32, "ex2")
    ddv = k.sb([128, 32], F32, "ddv2")
    sz_l = [k.sb([128, D], BF16, "sz") for _ in range(2)]
    g1_l = [k.sb([128, D], BF16, "g1") for _ in range(2)]
    Rf = k.sb([128, H, 128], F32R, "Rf")
    Rb = Rf
    Ef = k.sb([128, H, 128], F32, "Ef")
    cbm = k.sb([128, 2, 2, 128], F32, "cbm")
    wgt = [k.sb([128, H, 128], BF16, f"wgt{d}") for d in range(2)]
    xdt = [k.sb([128, D], BF16, f"xdt{d}") for d in range(2)]
    xdd = k.sb([128, D], BF16, "xdd2")
    runf_bf = k.sb([128, D], BF16, "runf_bf")
    t1 = k.sb([128, D], F32, "t1")
    t2 = k.sb([128, D], F32, "t2")
    hh = k.sb([128, D], F32, "hh")
    sq = t2
    ss = k.sb([128, 1], F32, "ss")
    rstd = k.sb([128, 1], F32, "rstd")
    yg = k.sb([128, D], BF16, "yg")
    ygT = k.sb([128, 8, 128], BF16, "ygT")
    m1 = hh
    mg = k.sb([128, D], BF16, "mg")
    mgT = k.sb([128, 8, 128], BF16, "mgT")
    r1 = t1
    x1h = [k.sb([128, D], F32, "x1h") for _ in range(2)]
    lst = k.sb([128, 2, 6], F32, "lst")
    lmv = k.sb([128, 2], F32, "lmv")
    lrs = k.sb([128, 1], F32, "lrs")
    lnm = k.sb([128, 1], F32, "lnm")

    k.copy("act", runf_bf.t[:], run_f.t[:], [run_f], [runf_bf])
    print("SBUF free in pass 2:", nc.sbuf_bytes_remaining)

    def h3(ap, h=H):
        return ap.rearrange("p (h q) -> p h q", h=h)

    KSTOP2 = int(os.environ.get("KSTOP2", "1000"))
    ln2cache = {}

    def front2a(c):
        ln2cache[c] = ln0.run(x_d, x_d.t[c * 128:(c + 1) * 128, :], S1X, B1X, (pss[0], pss[1]))

    def front2b(c):
        xh, hT = ln2cache[c]
        sz, g1 = sz_l[c % 2], g1_l[c % 2]
        for half in range(2):
            pb = pss[half]
            proj_tok(hT, w2, half * 512, 512, pb)
            k.act(sz.t[:, half * 512:(half + 1) * 512], pb.t[:], AF.Silu, [pb], [sz])
        for half in range(2):
            pb = pss[half]
            proj_tok(hT, w2, 1024 + half * 512, 512, pb, bg1_rows, half * 512)
            k.act(g1.t[:, half * 512:(half + 1) * 512], pb.t[:], AF.Sigmoid, [pb], [g1])

    front2a(0)
    scale_wout()
    front2b(0)
    for c in range(min(NR, KSTOP2)):
        s = c % 2
        xs, bt, bcl, hp, g2l = xs_l[s], bt_l[s], bc_l[s], hp_l[s], g2_l[s]
        k.dma("sp", xs.t[:], xs_s[c].t, r=[xs_s[c]], w=[xs])
        k.dma("sp", bt.t[:], bt_s[c].t, r=[bt_s[c]], w=[bt])
        k.dma("sp", bcl.t[:], bc_s[c].t, r=[bc_s[c]], w=[bcl])
        k.dma("sp", hp.t[:], hp_s[c].t, r=[hp_s[c]], w=[hp])
        k.dma("sp", g2l.t[:], g2_s[c].t, r=[g2_s[c]], w=[g2l])
        xh, hT = ln2cache.pop(c)
        sz, g1 = sz_l[s], g1_l[s]
        xo = x1h[s]
        k.tt("pool", xo.t[:], xh.t[:], a0row.t[:], ALU.mult, [xh, a0row], [xo])
        if c + 1 < min(NR, KSTOP2):
            front2a(c + 1)
        dtc = dt_all.t[:, c, :]
        k.tt("dve", a_sb.t[:], dtc, arow.t[:], ALU.mult, [dt_all, arow], [a_sb])
        pb = pss[6]
        for i, (mi, ao) in enumerate(((MLE, 0), (MGE, 16), (MGT, 0), (MLT, 16))):
            k.mm(pb.t[:, i * 16:(i + 1) * 16], cm.t[:, mi, :], a_sb.t[:, ao:ao + 16], True, True, [cm, a_sb], [pb], inc=False)
        k.mm(pb.t[:, 64:96], ones_f.t[:], a_sb.t[:, 0:32], True, True, [ones_f, a_sb], [pb], inc=True)
        k.act(ex.t[:], pb.t[:, 0:96], AF.Exp, [pb], [ex])
        pb = pss[6]
        for g in range(2):
            k.mm(pb.t[:, 256 + g * 128:256 + (g + 1) * 128], bcl.t[:, g, :], bcl.t[:, 2 + g, :], True, True, [bcl], [pb], inc=(g == 1))
        pv = pb.t[:, 256:512].rearrange("p (g i) -> p g i", g=2)
        k.tt("dve", cbm.t[:, 0, :, :], pv, bc(cm.t[:, MLE, :].unsqueeze(1), [128, 2, 128]), ALU.mult, [pb, cm], [cbm])
        k.tt("dve", cbm.t[:, 1, :, :], pv, bc(cm.t[:, MGE, :].unsqueeze(1), [128, 2, 128]), ALU.mult, [pb, cm], [cbm])
        for d in range(2):
            k.tt("pool", h3(xdt[d].t[:]), h3(xs.t[:]), bc(dtc[:, d * 16:(d + 1) * 16].unsqueeze(2), [128, H, P]),
                 ALU.mult, [xs, dt_all], [xdt[d]])
        for d, (R, Lm) in enumerate(((Rf, mgt_r), (Rb, mlt_r))):
            k.tt("pool", R.t[:], bc(a_sb.t[:, d * 16:(d + 1) * 16].unsqueeze(2), [128, H, 128]),
                 bc(cm.t[:, MGE if d else MLE, :].unsqueeze(1), [128, H, 128]), ALU.mult, [a_sb, cm], [R])
            R2 = R.t[:].rearrange("p h i -> p (h i)")
            E2 = Ef.t[:].rearrange("p h i -> p (h i)")
            for q in range(4):
                pb = pss[2 + q]
                k.mm(pb.t[:], Lm.t[:], R2[:, q * 512:(q + 1) * 512], True, True, [Lm, R], [pb], inc=True)
                k.act(E2[:, q * 512:(q + 1) * 512], pb.t[:], AF.Exp, [pb], [Ef])
            for g in range(2):
                k.tt("dve", wgt[d].t[:, g * 8:(g + 1) * 8, :], Ef.t[:, g * 8:(g + 1) * 8, :],
                     bc(cbm.t[:, d, g, :].unsqueeze(1), [128, 8, 128]), ALU.mult, [Ef, cbm], [wgt[d]])
        if c + 1 < min(NR, KSTOP2):
            front2b(c + 1)
        for h in range(H):
            pb = pss[h // 8]
            o = pb.t[:, (h % 8) * 64:(h % 8 + 1) * 64]
            cs = slice(h * 64, (h + 1) * 64)
            k.mm(o, wgt[0].t[:, h, :], xdt[0].t[:, cs], True, False, [wgt[0], xdt[0]], [pb], inc=False)
            k.mm(o, wgt[1].t[:, h, :], xdt[1].t[:, cs], False, False, [wgt[1], xdt[1]], [pb], inc=False)
            k.mm(o, dh.t[:, h, :], xs.t[:, cs], False, True, [dh, xs], [pb], inc=(h % 8 == 7))
        for g in range(2):
            k.mm(pss[2 + g].t[:], bcl.t[:, 2 + g, :], runf_bf.t[:, g * 512:(g + 1) * 512], True, True, [bcl, runf_bf], [pss[2 + g]], inc=True)
            k.mm(pss[4 + g].t[:], bcl.t[:, 2 + g, :], hp.t[:, g * 512:(g + 1) * 512], True, True, [bcl, hp], [pss[4 + g]], inc=True)
        for g in range(2):
            sl = slice(g * 512, (g + 1) * 512)
            k.tt("dve", h3(t1.t[:, sl], 8), h3(pss[2 + g].t[:], 8), bc(ex.t[:, g * 8:(g + 1) * 8].unsqueeze(2), [128, 8, P]),
                 ALU.mult, [pss[2 + g], ex], [t1])
            k.tt("dve", h3(t2.t[:, sl], 8), h3(pss[4 + g].t[:], 8), bc(ex.t[:, 16 + g * 8:16 + (g + 1) * 8].unsqueeze(2), [128, 8, P]),
                 ALU.mult, [pss[4 + g], ex], [t2])
            k.tt("pool", t1.t[:, sl], t1.t[:, sl], t2.t[:, sl], ALU.add, [t1, t2], [t1])
            k.tt("dve", t1.t[:, sl], t1.t[:, sl], pss[g].t[:], ALU.add, [t1, pss[g]], [t1])
        if c == 0:
            k.dump("y0", t1, t1.t[:], [128, D])
        k.tt("dve", hh.t[:], t1.t[:], sz.t[:], ALU.mult, [t1, sz], [hh])
        k.act(sq.t[:], hh.t[:], AF.Square, [hh], [sq, ss], accum=ss.t[:])
        rsqrt_act(rstd, ss.t[:], ss, 1.0 / D)
        k.stt(yg.t[:], hh.t[:], rstd.t[:], ngrow.t[:], ALU.mult, ALU.mult, [hh, rstd, ngrow], [yg])

        transpose_to(ygT, yg, pss[6], "act")
        k.tt("dve", ddv.t[:, 0:16], ex.t[:, 32:48], dtc[:, 0:16], ALU.mult, [ex, dt_all], [ddv])
        k.tt("pool", h3(xdd.t[:]), h3(xs.t[:]), bc(ddv.t[:, 0:16].unsqueeze(2), [128, H, P]), ALU.mult, [xs, ddv], [xdd])
        for g in range(2):
            k.mm(pss[2 + g].t[:], bt.t[:, g * 128:(g + 1) * 128], xdd.t[:, g * 512:(g + 1) * 512], True, True, [bt, xdd], [pss[2 + g]], inc=True)
        k.tt("pool", h3(run_f.t[:]), h3(run_f.t[:]), bc(ex.t[:, 64:80].unsqueeze(2), [128, H, P]), ALU.mult, [run_f, ex], [run_f])
        for g in range(2):
            sl = slice(g * 512, (g + 1) * 512)
            k.tt("dve", run_f.t[:, sl], run_f.t[:, sl], pss[2 + g].t[:], ALU.add, [run_f, pss[2 + g]], [run_f])
        k.copy("act", runf_bf.t[:], run_f.t[:], [run_f], [runf_bf])
        for half in range(2):
            sl = slice(half * 512, (half + 1) * 512)
            pb = pss[4 + half]
            for kk in range(8):
                k.mm(pb.t[:], ygT.t[:, kk, :], wssd.t[:, kk, sl], kk == 0, kk == 7, [ygT, wssd], [pb], inc=(kk == 7))
            k.tt("dve", m1.t[:, sl], pb.t[:], g1.t[:, sl], ALU.mult, [pb, g1], [m1])
            k.tt("dve", mg.t[:, sl], m1.t[:, sl], g2l.t[:, sl], ALU.add, [m1, g2l], [mg])

        transpose_to(mgT, mg, pss[6], "dve")
        for half in range(2):
            sl = slice(half * 512, (half + 1) * 512)
            pb = pss[half]
            for kk in range(8):
                k.mm(pb.t[:], mgT.t[:, kk, :], wout.t[:, kk, sl], kk == 0, False, [mgT, wout], [pb], inc=False)
            k.mm(pb.t[:], ones_bf.t[0:2, :], b0_rows.t[0:2, sl], False, True, [ones_bf, b0_rows], [pb], inc=True)
            k.tt("dve", r1.t[:, sl], xo.t[:, sl], pb.t[:], ALU.add, [xo, pb], [r1])
        if c == 0:
            k.dump("r1", r1, r1.t[:], [128, D])
        ln_stats(r1, lst, lmv, lrs, lnm)
        k.act(xo.t[:], r1.t[:], AF.Identity, [r1, lrs, lnm], [xo], bias=lnm.t[:], scale=lrs.t[:])
        k.dma("pool", x1_s[c].t, xo.t[:], r=[xo], w=[x1_s[c]])
    k.pop()

    k.fill = None
    if KPASS < 3:
        k.barrier()
        k.es.close()
        return nc, k.dbg
    st = k.push()
    b1_rows = bias_rows(rows_d.t[6:7, :], D, ALPHA, "b1r")
    a1row = k.sb([128, D], F32, "a1row")
    k.dma("sp", a1row.t[:], rows_d.t[5:6, :].partition_broadcast(128), w=[a1row])
    k.ts("dve", a1row.t[:], a1row.t[:], float(ALPHA), None, ALU.mult, None, [a1row], [a1row])
    l2g = k.sb([128, D], F32, "l2g")
    k.dma("sp", l2g.t[:], rows_d.t[7:8, :].partition_broadcast(128), w=[l2g])
    l2b = k.sb([128, D], F32, "l2b")
    k.dma("sp", l2b.t[:], rows_d.t[8:9, :].partition_broadcast(128), w=[l2b])
    make_row_psum(40, "g2row", (pss[6], pss[7]))
    wf1 = k.sb([128, 8, DFF], BF16, "wf1")
    wf3 = k.sb([128, 8, DFF], BF16, "wf3")
    wf2 = k.sb([128, NFF, D], BF16, "wf2")
    FG = ((0, 6), (6, 12), (12, 17), (17, 22))
    wf1g = [Buf(wf1.t) for _ in FG]
    wf3g = [Buf(wf3.t) for _ in FG]
    for gi, (f0, f1) in enumerate(FG):
        for kk in range(8):
            k.dma("pool", wf1.t[:, kk, f0 * 128:f1 * 128], wff1_d.t[kk * 128:(kk + 1) * 128, f0 * 128:f1 * 128], w=[wf1g[gi]], partial=True)
        for kk in range(8):
            k.dma("pool", wf3.t[:, kk, f0 * 128:f1 * 128], wff3_d.t[kk * 128:(kk + 1) * 128, f0 * 128:f1 * 128], w=[wf3g[gi]], partial=True)
    load_w(wf2, wff2_d.t, NFF)

    def fgrp(f):
        return [gi for gi, (f0, f1) in enumerate(FG) if f0 <= f < f1][0]

    def scale_wf2():
        for kk in range(NFF):
            for half in range(2):
                sl = slice(half * 512, (half + 1) * 512)
                k.tt("dve", wf2.t[:, kk, sl], wf2.t[:, kk, sl], pss[6 + half].t[:], ALU.mult, [wf2, pss[6 + half]], [wf2])
    TB = 2
    NB = NR // TB
    x1t = [[k.sb([128, D], F32, "x1t") for _ in range(TB)] for _ in range(2)]
    xmT = [k.sb([128, 8, TB * 128], BF16, "xmT") for _ in range(2)]
    hid = k.sb([128, NFF, TB * 128], BF16, "hid")
    sl1 = [k.sb([128, TB * 128], F32, "sl1") for _ in range(2)]
    fst = k.sb([128, 2, 6], F32, "fst")
    fmv = k.sb([128, 2], F32, "fmv")
    frs = k.sb([128, 1], F32, "frs")
    fnm = k.sb([128, 1], F32, "fnm")
    NT = TB * 128
    print("SBUF free in pass 3:", nc.sbuf_bytes_remaining)

    def ffn_front(b):
        s = b % 2
        for t in range(TB):
            c = b * TB + t
            xt = x1t[s][t]
            k.dma("sp", xt.t[:], x1_s[c].t, r=[x1_s[c]], w=[xt])
            for cc in range(8):
                pb = pss[cc // 4]
                k.tr(pb.t[:, (cc % 4) * 128:(cc % 4 + 1) * 128], xt.t[:, cc * 128:(cc + 1) * 128], cm.t[:, IDN, :],
                     [xt, cm], [pb], inc=(cc % 4 == 3))
            for cc in range(8):
                pb = pss[cc // 4]
                src = pb.t[:, (cc % 4) * 128:(cc % 4 + 1) * 128]
                dst = xmT[s].t[:, cc, t * 128:(t + 1) * 128]
                if cc < 4:
                    k.ts("dve", dst, src, cols.t[:, S2, cc:cc + 1], cols.t[:, B2, cc:cc + 1], ALU.mult, ALU.add, [pb, cols], [xmT[s]])
                else:
                    k.act(dst, src, AF.Identity, [pb, cols], [xmT[s]], bias=cols.t[:, B2, cc:cc + 1], scale=cols.t[:, S2, cc:cc + 1])
            k.tt("pool", xt.t[:], xt.t[:], a1row.t[:], ALU.mult, [xt, a1row], [xt])

    def ffn_w13(b):
        s = b % 2
        for f in range(NFF):
            p1 = pss[2 + (f % 2) * 2]
            p3 = pss[3 + (f % 2) * 2]
            for kk in range(8):
                k.mm(p1.t[:, 0:NT], wf1.t[:, kk, f * 128:(f + 1) * 128], xmT[s].t[:, kk, :], kk == 0, kk == 7, [wf1g[fgrp(f)], xmT[s]], [p1], inc=(kk == 7))
            for kk in range(8):
                k.mm(p3.t[:, 0:NT], wf3.t[:, kk, f * 128:(f + 1) * 128], xmT[s].t[:, kk, :], kk == 0, kk == 7, [wf3g[fgrp(f)], xmT[s]], [p3], inc=(kk == 7))
            sv = sl1[f % 2]
            k.act(sv.t[:], p1.t[:, 0:NT], AF.Silu, [p1], [sv])
            k.tt("dve", hid.t[:, f, :], sv.t[:], p3.t[:, 0:NT], ALU.mult, [sv, p3], [hid])

    def ffn_w2(b):
        s = b % 2
        for t in range(TB):
            c = b * TB + t
            xt = x1t[s][t]
            r2 = xt
            for half in range(2):
                sl = slice(half * 512, (half + 1) * 512)
                pb = pss[6 + half]
                for f in range(NFF):
                    k.mm(pb.t[:], hid.t[:, f, t * 128:(t + 1) * 128], wf2.t[:, f, sl], f == 0, False, [hid, wf2], [pb], inc=False)
                k.mm(pb.t[:], ones_bf.t[0:2, :], b1_rows.t[0:2, sl], False, True, [ones_bf, b1_rows], [pb], inc=True)
                k.tt("dve", r2.t[:, sl], r2.t[:, sl], pb.t[:], ALU.add, [r2, pb], [r2])
            ln_stats(r2, fst, fmv, frs, fnm)
            o = r2
            k.act(o.t[:], r2.t[:], AF.Identity, [r2, frs, fnm], [o], bias=fnm.t[:], scale=frs.t[:])
            k.tt("pool", o.t[:], o.t[:], l2g.t[:], ALU.mult, [o, l2g], [o])
            k.tt("dve", o.t[:], o.t[:], l2b.t[:], ALU.add, [o, l2b], [o])
            k.dma("pool", out_d.t[c * 128:(c + 1) * 128, :], o.t[:], r=[o], w=[out_d])

    if NB > 0:
        ffn_front(0)
    for b in range(NB):
        ffn_w13(b)
        if b + 1 < NB:
            ffn_front(b + 1)
        if b == 0:
            scale_wf2()
        ffn_w2(b)
    k.pop()
    k.barrier()
    k.es.close()
    return nc, k.dbg


def _prep(inputs):
    f = lambda a: np.ascontiguousarray(np.asarray(a, dtype=np.float32))
    i = {kk: f(v) for kk, v in inputs.items()}
    kk = np.arange(128)
    cm = np.stack([np.eye(128), kk[:, None] <= kk[None, :], kk[:, None] >= kk[None, :],
                   kk[:, None] > kk[None, :], kk[:, None] < kk[None, :]], axis=1).astype(np.float32)
    col = lambda v: f(v.reshape(-1, 128).T)
    shared = dict(
        cm=f(cm), w_ada=i["w_ada"][0], b_ada_c=col(i["b_ada"][0]),
        colp=f(np.stack([col(i["ln0_g"]), col(i["ln0_b"]), col(i["ln1_g"][0]), col(i["ln1_b"][0])], axis=1)),
        convw_c=f(i["conv_w"][0].T.reshape(12, 128, 5).transpose(1, 0, 2)),
        convb_c=col(i["conv_b"][0]), w_in=i["w_in"][0],
        rows=f(np.stack([i["ln0_g"], i["ln0_b"], i["ssd_norm_g"][0], i["gm_norm_g"][0], i["gm_norm_b"][0],
                         i["ln1_g"][0], i["ln1_b"][0], i["ln2_g"][0], i["ln2_b"][0]])),
        b_gate=f(i["b_gate"][0][None, :]),
        small=f(np.concatenate([i["dt_bias"][0].reshape(-1), i["a_log"][0].reshape(-1), i["d_skip"][0].reshape(-1)])[None, :]),
        wspT=f(i["w_spatial"][0].transpose(2, 0, 1)), wsp=f(i["w_spatial"][0].transpose(1, 0, 2)),
        bsp_c=f(i["b_spatial"][0].T),
        w_ssd=i["w_ssd_proj"][0], w_gm=i["w_gm_proj"][0], w_out=i["w_out"][0],
        w_ff1=i["w_ff1"][0], w_ff3=i["w_ff3"][0], w_ff2=i["w_ff2"][0],
    )
    maps = []
    for b in range(i["x"].shape[0]):
        m = dict(shared)
        m["x"] = i["x"][b]
        m["ctx"] = i["ctx"][b]
        m["cT"] = f(np.stack([i["c"][b], i["c_ctx"]], axis=1).reshape(8, 128, 2).transpose(1, 0, 2))
        maps.append(m)
    return maps


def kernel(**inputs):
    maps = _prep(inputs)
    nc, _ = build()
    n = len(maps)
    res = run_bass_kernel_spmd(nc, maps, core_ids=list(range(n)))
    return np.stack([np.asarray(r["out"], dtype=np.float32) for r in res.results], axis=0)
```

```python
import os
from contextlib import ExitStack
import numpy as np
import concourse.bass as bass
import concourse.mybir as mybir
from concourse.bass_utils import run_bass_kernel_spmd

F32 = mybir.dt.float32
F32R = mybir.dt.float32r
BF16 = mybir.dt.bfloat16
AF = mybir.ActivationFunctionType
ALU = mybir.AluOpType

D = 1024
SEQ = 4096
CTX = 256
NCH = SEQ // 128
H = 16
P = 64
NST = 128
DFF = 2816
NFF = DFF // 128
XBC = 1536
DPROJ = 6688
O_Z, O_XBC, O_DT, O_U, O_V, O_G = 0, 1024, 2560, 2592, 3616, 4640
ALPHA = 2.0 ** 0.25
EPS = 1e-5
DEBUG = bool(int(os.environ.get("KDEBUG", "0")))
NCH_RUN = int(os.environ.get("KNCH", str(NCH)))
KPASS = int(os.environ.get("KPASS", "3"))
NFILL = int(os.environ.get("KFILL", "3"))


class Buf:
    def __init__(self, t):
        self.t = t
        self.w = None
        self.pw = []
        self.r = {}
        self.psum = False

    def __getitem__(self, idx):
        return self.t[idx]


class Eng:
    def __init__(self, e, sem, name):
        self.e, self.sem, self.name = e, sem, name
        self.count = 0
        self.seen = {}
        self.pend_r, self.pend_w = [], []


class KB:
    def __init__(self):
        self.nc = bass.Bass("TRN2", target_bir_lowering=False)
        nc = self.nc
        self.es = ExitStack()
        self.stacks = [self.es]
        self.engs = {}
        for name, e in (("pe", nc.tensor), ("act", nc.scalar), ("dve", nc.vector), ("pool", nc.gpsimd), ("sp", nc.sync)):
            sem = self.es.enter_context(nc.semaphore("s_" + name))
            self.engs[name] = Eng(e, sem, name)
        self.dpool = {}
        for q, n in (("sp", 20), ("pool", 20), ("act", 6)):
            self.dpool[q] = [[self.es.enter_context(nc.semaphore(f"d_{q}{i}")), 0] for i in range(n)]
        self.dnext = {q: 0 for q in self.dpool}
        self.uid = 0
        self.dbg = []
        self.fill = None

    def sb(self, shape, dt, name=None):
        self.uid += 1
        t = self.stacks[-1].enter_context(self.nc.sbuf_tensor(f"{name or 't'}_{self.uid}", list(shape), dt))
        return Buf(t)

    def ps(self, name):
        self.uid += 1
        t = self.stacks[-1].enter_context(self.nc.psum_tensor(f"{name}_{self.uid}", [128, 512], F32))
        b = Buf(t)
        b.psum = True
        return b

    def dram(self, name, shape, dt, kind="Internal"):
        return Buf(self.nc.dram_tensor(name, list(shape), dt, kind=kind).ap())

    def push(self):
        print("SBUF remaining at push:", self.nc.sbuf_bytes_remaining)
        st = ExitStack()
        self.stacks.append(st)
        return st

    def pop(self):
        self.barrier()
        self.stacks.pop().close()

    def _waits(self, E, r, w, partial=False):
        deps = {}
        raw = set()
        for b in r:
            if b.w is not None:
                deps[(b.w[0], b.w[1])] = b.w
                raw.add((b.w[0], b.w[1]))
            for t in b.pw:
                deps[(t[0], t[1])] = t
            if b.psum:
                for t in b.r.values():
                    if t[2] is not E:
                        deps[(t[0], t[1])] = t
        for b in w:
            if b.w is not None:
                deps[(b.w[0], b.w[1])] = b.w
            if not partial:
                for t in b.pw:
                    deps[(t[0], t[1])] = t
            for t in b.r.values():
                deps[(t[0], t[1])] = t
        for key, (sem, val, owner) in deps.items():
            if owner is E:
                if E.name == "pe":
                    continue
                if key not in raw:
                    continue
            if E.seen.get(sem, 0) >= val:
                continue
            if E.name == "pe" and self.fill is not None and owner is not None:
                for v in range(max(E.seen.get(sem, 0) + 1, val - NFILL), val):
                    E.e.wait_ge(sem, v)
                    self.nc.tensor.matmul(self.fill[0], self.fill[1], self.fill[2], start=True, stop=True)
            E.e.wait_ge(sem, val)
            E.seen[sem] = val

    def op(self, eng, fn, r=(), w=(), inc=True):
        E = self.engs[eng]
        self._waits(E, r, w)
        ins = fn()
        E.pend_r.extend(r)
        E.pend_w.extend(w)
        if inc:
            E.count += 1
            ins.then_inc(E.sem, 1)
            tok = (E.sem, E.count, E)
            for b in E.pend_r:
                b.r[E.sem] = tok
            for b in E.pend_w:
                b.w = tok
                b.pw = []
                b.r = {}
            E.pend_r, E.pend_w = [], []
        return ins

    def dma(self, q, out, in_, r=(), w=(), partial=False, **kw):
        E = self.engs[q]
        self._waits(E, r, w, partial)
        pool = self.dpool[q]
        i = self.dnext[q]
        self.dnext[q] = (i + 1) % len(pool)
        sem, val = pool[i]
        if val > 0 and E.seen.get(sem, 0) < val:
            E.e.wait_ge(sem, val)
            E.seen[sem] = val
        ins = E.e.dma_start(out=out, in_=in_, **kw)
        val += 16
        pool[i][1] = val
        ins.then_inc(sem, 16)
        tok = (sem, val, None)
        for b in r:
            b.r[sem] = tok
        for b in w:
            if partial:
                b.pw.append(tok)
            else:
                b.w = tok
                b.pw = []
            b.r = {}

    def barrier(self):
        toks = [(E.sem, E.count) for E in self.engs.values() if E.count > 0]
        for pool in self.dpool.values():
            toks += [(s, v) for s, v in pool if v > 0]
        for E in self.engs.values():
            assert not E.pend_r and not E.pend_w, E.name
            for sem, val in toks:
                if sem is E.sem or E.seen.get(sem, 0) >= val:
                    continue
                E.e.wait_ge(sem, val)
                E.seen[sem] = val

    def act(self, out, in_, func, r, w, bias=None, scale=None, accum=None):
        kw = {}
        if bias is not None:
            kw["bias"] = bias
        if scale is not None:
            kw["scale"] = scale
        if accum is not None:
            kw["accum_out"] = accum
        return self.op("act", lambda: self.nc.scalar.activation(out=out, in_=in_, func=func, **kw), r, w)

    def tt(self, eng, out, in0, in1, op, r, w):
        e = self.engs[eng].e
        return self.op(eng, lambda: e.tensor_tensor(out=out, in0=in0, in1=in1, op=op), r, w)

    def ts(self, eng, out, in0, s1, s2, op0, op1, r, w):
        e = self.engs[eng].e
        if s2 is None:
            return self.op(eng, lambda: e.tensor_scalar(out=out, in0=in0, scalar1=s1, scalar2=None, op0=op0), r, w)
        return self.op(eng, lambda: e.tensor_scalar(out=out, in0=in0, scalar1=s1, scalar2=s2, op0=op0, op1=op1), r, w)

    def stt(self, out, in0, scalar, in1, op0, op1, r, w):
        return self.op("dve", lambda: self.nc.vector.scalar_tensor_tensor(
            out=out, in0=in0, scalar=scalar, in1=in1, op0=op0, op1=op1), r, w)

    def copy(self, eng, out, in_, r, w):
        e = self.engs[eng].e
        if eng == "act":
            return self.act(out, in_, AF.Copy, r, w)
        return self.op(eng, lambda: e.tensor_copy(out=out, in_=in_), r, w)

    def memset(self, eng, ap, val, w):
        e = self.engs[eng].e
        return self.op(eng, lambda: e.memset(ap, val), (), w)

    def mm(self, out, lhsT, rhs, start, stop, r, w, inc):
        return self.op("pe", lambda: self.nc.tensor.matmul(out, lhsT, rhs, start=start, stop=stop), r, w, inc)

    def tr(self, out, in_, ident, r, w, inc):
        return self.op("pe", lambda: self.nc.tensor.transpose(out, in_, ident), r, w, inc)

    def dump(self, name, buf, ap, shape, dt=F32):
        if not DEBUG:
            return
        d = self.dram("dbg_" + name, shape, dt, kind="ExternalOutput")
        self.dma("sp", d.t, ap, r=[buf], w=[d])
        self.dbg.append("dbg_" + name)


def bc(ap, shape):
    return ap.to_broadcast(list(shape))


def build():
    k = KB()
    nc = k.nc

    def din(name, shape, dt=F32):
        return k.dram(name, shape, dt, kind="ExternalInput")

    x_d = din("x", [SEQ, D])
    ctx_d = din("ctx", [CTX, D])
    cT_d = din("cT", [128, 8, 2])
    cm_d = din("cm", [128, 5, 128])
    wada_d = din("w_ada", [D, 6 * D])
    badac_d = din("b_ada_c", [128, 48])
    colp_d = din("colp", [128, 4, 8])
    convw_d = din("convw_c", [128, 12, 5])
    convb_d = din("convb_c", [128, 12])
    win_d = din("w_in", [D, DPROJ])
    rows_d = din("rows", [9, D])
    bgate_d = din("b_gate", [1, 2 * D])
    small_d = din("small", [1, 96])
    wspT_d = din("wspT", [128, 8, 128])
    wsp_d = din("wsp", [128, 8, 128])
    bsp_d = din("bsp_c", [128, 8])
    wssd_d = din("w_ssd", [D, D])
    wgm_d = din("w_gm", [D, D])
    wout_d = din("w_out", [D, D])
    wff1_d = din("w_ff1", [D, DFF])
    wff3_d = din("w_ff3", [D, DFF])
    wff2_d = din("w_ff2", [DFF, D])
    out_d = k.dram("out", [SEQ, D], F32, kind="ExternalOutput")

    xs_s = [k.dram(f"xs_s{c}", [128, D], BF16) for c in range(NCH)]
    bt_s = [k.dram(f"bt_s{c}", [128, 256], BF16) for c in range(NCH)]
    bc_s = [k.dram(f"bc_s{c}", [128, 4, 128], BF16) for c in range(NCH)]
    hp_s = [k.dram(f"hp_s{c}", [128, D], BF16) for c in range(NCH)]
    g2_s = [k.dram(f"g2_s{c}", [128, D], BF16) for c in range(NCH)]
    x1_s = [k.dram(f"x1_s{c}", [128, D], F32) for c in range(NCH)]

    cm = k.sb([128, 5, 128], F32, "cm")
    k.dma("sp", cm.t[:], cm_d.t, w=[cm])
    IDN, MLE, MGE, MGT, MLT = range(5)
    ident_bf = k.sb([128, 128], BF16, "identbf")
    k.copy("dve", ident_bf.t[:], cm.t[:, IDN, :], [cm], [ident_bf])
    mgt_r = k.sb([128, 128], F32R, "mgtr")
    mlt_r = k.sb([128, 128], F32R, "mltr")
    k.copy("dve", mgt_r.t[:], cm.t[:, MGT, :], [cm], [mgt_r])
    k.copy("dve", mlt_r.t[:], cm.t[:, MLT, :], [cm], [mlt_r])
    ones_f = k.sb([128, 128], F32, "onesf")
    k.memset("dve", ones_f.t[:], 1.0, [ones_f])
    ones_bf = k.sb([2, 128], BF16, "onesbf")
    k.memset("dve", ones_bf.t[:], 1.0, [ones_bf])

    colp = k.sb([128, 4, 8], F32, "colp")
    k.dma("sp", colp.t[:], colp_d.t, w=[colp])
    convw = k.sb([128, 12, 5], F32, "convw")
    k.dma("sp", convw.t[:], convw_d.t, w=[convw])
    convb = k.sb([128, 12], F32, "convb")
    k.dma("sp", convb.t[:], convb_d.t, w=[convb])
    smallr = k.sb([128, 96], F32, "smallr")
    k.dma("sp", smallr.t[:], small_d.t.partition_broadcast(128), w=[smallr])
    arow = k.sb([128, 32], F32, "arow")
    k.act(arow.t[:], smallr.t[:, 32:64], AF.Exp, [smallr], [arow])
    k.ts("dve", arow.t[:], arow.t[:], -1.0, None, ALU.mult, None, [arow], [arow])
    dsum = k.sb([128, 16], F32, "dsum")
    k.tt("dve", dsum.t[:], smallr.t[:, 64:80], smallr.t[:, 80:96], ALU.add, [smallr], [dsum])
    dh = k.sb([128, H, 128], BF16, "dh")
    k.tt("dve", dh.t[:], bc(cm.t[:, IDN, :].unsqueeze(1), [128, H, 128]),
         bc(dsum.t[:].unsqueeze(2), [128, H, 128]), ALU.mult, [cm, dsum], [dh])
    dt_all = k.sb([128, NCH, 32], F32, "dtall")
    run_f = k.sb([128, D], F32, "runf")
    run_b = k.sb([128, D], F32, "runb")
    modc = k.sb([128, 48, 2], F32, "modc")
    cols = k.sb([128, 6, 8], F32, "cols")
    S1X, B1X, S1C, B1C, S2, B2 = range(6)
    lntmp = k.sb([128, 1], F32, "lntmp")
    epsc = k.sb([128, 1], F32, "epsc")
    k.memset("dve", epsc.t[:], EPS, [epsc])

    pss = [k.ps(f"ps{i}") for i in range(8)]
    fill_spec = (pss[7].t[:, 0:256], ident_bf.t[:], dh.t[:, 0:2, :].rearrange("p a i -> p (a i)"))

    st = k.push()
    w1 = k.sb([128, 8, 1568 + 2048 + 1024], BF16, "w1")
    W_XBC, W_DT, W_U, W_V, W_G2 = 0, 1536, 1568, 2592, 3616
    for kk in range(8):
        rs_ = slice(kk * 128, (kk + 1) * 128)
        k.dma("pool", w1.t[:, kk, 0:1568], win_d.t[rs_, O_XBC:O_XBC + 1568], r=[win_d], w=[w1], partial=True)
    for kk in range(8):
        rs_ = slice(kk * 128, (kk + 1) * 128)
        k.dma("pool", w1.t[:, kk, 1568:3616], win_d.t[rs_, O_U:O_U + 2048], r=[win_d], w=[w1], partial=True)
        k.dma("pool", w1.t[:, kk, 3616:4640], win_d.t[rs_, O_G + D:O_G + 2 * D], r=[win_d], w=[w1], partial=True)
    wgm = k.sb([128, 8, D], BF16, "wgm")
    for kk in range(8):
        k.dma("pool", wgm.t[:, kk, :], wgm_d.t[kk * 128:(kk + 1) * 128, :], w=[wgm], partial=True)
    wspT = k.sb([128, 8, 128], BF16, "wspT")
    k.dma("pool", wspT.t[:], wspT_d.t, w=[wspT])
    st = k.push()
    cT = k.sb([128, 8, 2], F32, "cT")
    k.dma("sp", cT.t[:], cT_d.t, w=[cT])
    scT = k.sb([128, 8, 2], F32, "scT")
    k.act(scT.t[:], cT.t[:], AF.Silu, [cT], [scT])
    badac = k.sb([128, 48], F32, "badac")
    k.dma("sp", badac.t[:], badac_d.t, w=[badac])
    wst = [k.sb([128, 8, 512], F32, f"wst{i}") for i in range(4)]
    wada_v = wada_d.t.rearrange("(k p) n -> p k n", p=128)
    modrow = k.sb([2, 6 * D], F32, "modrow")
    for cb in range(12):
        wb = wst[cb % 4]
        k.dma("sp", wb.t[:], wada_v[:, :, cb * 512:(cb + 1) * 512], r=[wada_d], w=[wb])
        pb = pss[cb % 2]
        for kk in range(8):
            k.mm(pb.t[0:2, :], scT.t[:, kk, :], wb.t[:, kk, :], kk == 0, kk == 7, [scT, wb], [pb], inc=(kk == 7))
        k.copy("act", modrow.t[:, cb * 512:(cb + 1) * 512], pb.t[0:2, :], [pb], [modrow])
    pm = pss[2]
    for j in range(48):
        k.mm(pm.t[:, 2 * j:2 * j + 2], modrow.t[0:2, j * 128:(j + 1) * 128], cm.t[0:2, IDN, 0:2], True, True,
             [modrow, cm], [pm], inc=(j == 47))
    k.tt("dve", modc.t[:], pm.t[:, 0:96].rearrange("p (j m) -> p j m", m=2),
         bc(badac.t[:].unsqueeze(2), [128, 48, 2]), ALU.add, [pm, badac], [modc])
    tmp8 = k.sb([128, 8], F32, "tmp8")
    for (si, bi, m, g_i, b_i, sc_o, sh_o) in ((S1X, B1X, 0, 0, 1, 8, 0), (S1C, B1C, 1, 0, 1, 8, 0), (S2, B2, 0, 2, 3, 32, 24)):
        k.ts("dve", tmp8.t[:], modc.t[:, sc_o:sc_o + 8, m], 1.0, None, ALU.add, None, [modc], [tmp8])
        k.tt("dve", cols.t[:, si, :], tmp8.t[:], colp.t[:, g_i, :], ALU.mult, [tmp8, colp], [cols])
        k.tt("dve", cols.t[:, bi, :], tmp8.t[:], colp.t[:, b_i, :], ALU.mult, [tmp8, colp], [cols])
        k.tt("dve", cols.t[:, bi, :], cols.t[:, bi, :], modc.t[:, sh_o:sh_o + 8, m], ALU.add, [cols, modc], [cols])
    k.pop()
    k.dump("modc", modc, modc.t[:].rearrange("p j m -> p (j m)"), [128, 96])

    def make_row_psum(off, name, banks):
        dg = k.sb([128, 128], F32, name + "dg")
        for c in range(8):
            k.ts("dve", dg.t[:], cm.t[:, IDN, :], modc.t[:, off + c, 0:1], None, ALU.mult, None, [cm, modc], [dg])
            pb = banks[c // 4]
            k.mm(pb.t[:, (c % 4) * 128:(c % 4 + 1) * 128], ones_f.t[:], dg.t[:], True, True, [ones_f, dg], [pb], True)

    def make_row(off, name):
        row = k.sb([128, D], F32, name)
        dg = k.sb([128, 128], F32, name + "dg")
        for c in range(8):
            k.ts("dve", dg.t[:], cm.t[:, IDN, :], modc.t[:, off + c, 0:1], None, ALU.mult, None, [cm, modc], [dg])
            pb = pss[1 + (c // 4)]
            k.mm(pb.t[:, (c % 4) * 128:(c % 4 + 1) * 128], ones_f.t[:], dg.t[:], True, True, [ones_f, dg], [pb], True)
        k.copy("act", row.t[:, 0:512], pss[1].t[:], [pss[1]], [row])
        k.copy("act", row.t[:, 512:1024], pss[2].t[:], [pss[2]], [row])
        return row

    def load_w(dst, src_ap, k_chunks, eng="pool"):
        for kk in range(k_chunks):
            k.dma(eng, dst.t[:, kk, :], src_ap[kk * 128:(kk + 1) * 128, :], w=[dst], partial=True)

    def bias_rows(src_row_ap, n, scale, name):
        rows = k.sb([2, n], BF16, name)
        k.push()
        b32 = k.sb([1, n], F32, name + "32")
        k.dma("sp", b32.t[:], src_row_ap, w=[b32])
        if scale != 1.0:
            k.ts("dve", b32.t[:], b32.t[:], float(scale), None, ALU.mult, None, [b32], [b32])
        k.copy("dve", rows.t[0:1, :], b32.t[:], [b32], [rows])
        lo32 = k.sb([1, n], F32, name + "lo32")
        k.tt("dve", lo32.t[:], b32.t[:], rows.t[0:1, :], ALU.subtract, [b32, rows], [lo32])
        lo = k.sb([1, n], BF16, name + "lo")
        k.copy("dve", lo.t[:], lo32.t[:], [lo32], [lo])
        k.dma("sp", rows.t[1:2, :], lo.t[:], r=[lo], w=[rows])
        k.pop()
        return rows

    def rsqrt_act(out_buf, in_ap, in_buf, scale):
        k.act(lntmp.t[:], in_ap, AF.Ln, [in_buf, epsc], [lntmp], bias=epsc.t[:], scale=float(scale))
        k.act(out_buf.t[:], lntmp.t[:], AF.Exp, [lntmp], [out_buf], scale=-0.5)

    def ln_stats(src, tmp_st, tmp_mv, rs, nm):
        for i in range(2):
            k.op("dve", lambda i=i: nc.vector.bn_stats(out=tmp_st.t[:, i, :], in_=src.t[:, i * 512:(i + 1) * 512]), [src], [tmp_st])
        k.op("dve", lambda: nc.vector.bn_aggr(out=tmp_mv.t[:], in_=tmp_st.t[:].rearrange("p a b -> p (a b)")), [tmp_st], [tmp_mv])
        rsqrt_act(rs, tmp_mv.t[:, 1:2], tmp_mv, 1.0)
        k.ts("dve", nm.t[:], tmp_mv.t[:, 0:1], rs.t[:], -1.0, ALU.mult, ALU.mult, [tmp_mv, rs], [nm])

    def proj_tok(hT, W, c0, n, pb, brow=None, b0=0):
        for kk in range(8):
            last = (kk == 7 and brow is None)
            k.mm(pb.t[:, 0:n], hT.t[:, kk, :], W.t[:, kk, c0:c0 + n], kk == 0, last, [hT, W], [pb], inc=last)
        if brow is not None:
            k.mm(pb.t[:, 0:n], ones_bf.t[0:2, :], brow.t[0:2, b0:b0 + n], False, True, [ones_bf, brow], [pb], inc=True)

    def transpose_to(dstT, src, pb, eng):
        pv = pb.t[:].bitcast(BF16)
        for c in range(8):
            k.tr(pv[:, c * 128:(c + 1) * 128], src.t[:, c * 128:(c + 1) * 128], ident_bf.t[:], [src, ident_bf], [pb], inc=(c == 7))
        k.copy(eng, dstT.t[:].rearrange("p a t -> p (a t)"), pv, [pb], [dstT])

    class LN0:
        def __init__(self, nh=2):
            self.nh = nh
            self.xt = [k.sb([128, D], F32, "xt") for _ in range(2)]
            self.xh = [k.sb([128, D], F32, "xh") for _ in range(nh)]
            self.hT = [k.sb([128, 8, 128], BF16, "hT") for _ in range(2)]
            self.st = k.sb([128, 2, 6], F32, "lnst")
            self.mv = k.sb([128, 2], F32, "lnmv")
            self.rs = k.sb([128, 1], F32, "lnrs")
            self.nm = k.sb([128, 1], F32, "lnnm")
            self.i = 0

        def run(self, src_buf, src_ap, si, bi, pbanks):
            s = self.i % 2
            self.i += 1
            xt, xh, hT = self.xt[s], self.xh[s % self.nh], self.hT[s]
            k.dma("sp", xt.t[:], src_ap, r=[src_buf], w=[xt])
            ln_stats(xt, self.st, self.mv, self.rs, self.nm)
            k.act(xh.t[:], xt.t[:], AF.Identity, [xt, self.rs, self.nm], [xh], bias=self.nm.t[:], scale=self.rs.t[:])
            for c in range(8):
                pb = pbanks[c // 4]
                k.tr(pb.t[:, (c % 4) * 128:(c % 4 + 1) * 128], xh.t[:, c * 128:(c + 1) * 128], cm.t[:, IDN, :],
                     [xh, cm], [pb], inc=(c % 4 == 3))
            for c in range(8):
                pb = pbanks[c // 4]
                src = pb.t[:, (c % 4) * 128:(c % 4 + 1) * 128]
                if c < 4:
                    k.ts("dve", hT.t[:, c, :], src, cols.t[:, si, c:c + 1], cols.t[:, bi, c:c + 1], ALU.mult, ALU.add, [pb, cols], [hT])
                else:
                    k.act(hT.t[:, c, :], src, AF.Identity, [pb, cols], [hT], bias=cols.t[:, bi, c:c + 1], scale=cols.t[:, si, c:c + 1])
            return xh, hT

    gmgrow = k.sb([128, D], F32, "gmgrow")
    k.dma("sp", gmgrow.t[:], rows_d.t[3:4, :].partition_broadcast(128), w=[gmgrow])
    biasM = k.sb([128, 8, 128], F32, "biasM")
    k.dma("sp", biasM.t[:].rearrange("p g c -> p (g c)"), rows_d.t[4:5, :].partition_broadcast(128), w=[biasM])
    gv = k.sb([128, D], F32, "gv")
    wsp_v = gv.t[:].rearrange("p (g q) -> p g q", g=8)
    k.dma("sp", wsp_v, wsp_d.t, w=[gv])
    rsum = k.sb([128, 8], F32, "rsum")
    k.op("dve", lambda: nc.vector.reduce_sum(out=rsum.t[:], in_=wsp_v, axis=mybir.AxisListType.X), [gv], [rsum])
    bspc = k.sb([128, 8], F32, "bspc")
    k.dma("sp", bspc.t[:], bsp_d.t, w=[bspc])
    k.tt("dve", biasM.t[:], biasM.t[:], bc(rsum.t[:].unsqueeze(2), [128, 8, 128]), ALU.mult, [biasM, rsum], [biasM])
    k.tt("dve", biasM.t[:], biasM.t[:], bc(bspc.t[:].unsqueeze(2), [128, 8, 128]), ALU.add, [biasM, bspc], [biasM])
    dtb_rows = bias_rows(small_d.t[0:1, 0:32], 32, 1.0, "dtb")
    bg2_rows = bias_rows(bgate_d.t[0:1, D:2 * D], D, 1.0, "bg2")

    if NFILL > 0:
        k.fill = fill_spec
    ln0 = LN0(2)
    pre = [k.sb([128, 12, 132], BF16, f"pre{i}") for i in range(3)]
    dgw = k.sb([128, 12, 5, 128], BF16, "dgw")
    for cc in range(12):
        for tap in range(5):
            k.ts("pool" if (cc + tap) % 2 else "dve", dgw.t[:, cc, tap, :], cm.t[:, IDN, :], convw.t[:, cc, tap:tap + 1], None,
                 ALU.mult, None, [cm, convw], [dgw])
    xbcT = k.sb([128, 12, 128], BF16, "xbcT")
    xs_tok = k.sb([128, D], BF16, "xs_tok")
    b_tok = k.sb([128, 256], BF16, "b_tok")
    a_sb = k.sb([128, 32], F32, "a_sb")
    ex = k.sb([128, 96], F32, "ex")
    ddv = k.sb([128, 32], F32, "ddv")
    xdd = k.sb([128, D], BF16, "xdd")
    hp_bf = k.sb([128, D], BF16, "hp_bf")
    e_t = k.sb([128, 32], F32, "e_t")
    decf1 = k.sb([128, 16], F32, "decf1")
    gu = k.sb([128, D], F32, "gu")
    sf1 = gu
    vhat = k.sb([128, D], BF16, "vhat")
    g2 = k.sb([128, D], F32, "g2")
    ygm = gv
    ygm_bf = k.sb([128, D], BF16, "ygm_bf")
    ygmT = k.sb([128, 8, 128], BF16, "ygmT")
    G2 = k.sb([128, D], BF16, "G2")
    vst = k.sb([128, 2, 6], F32, "vst")
    vmv = k.sb([128, 2], F32, "vmv")
    vrs = k.sb([128, 1], F32, "vrs")
    vnm = k.sb([128, 1], F32, "vnm")

    k.memset("dve", run_f.t[:], 0.0, [run_f])
    k.memset("dve", run_b.t[:], 0.0, [run_b])
    print("SBUF free in pass 1:", nc.sbuf_bytes_remaining)

    def small_exps(dt_ap, dt_buf):
        k.tt("dve", a_sb.t[:], dt_ap, arow.t[:], ALU.mult, [dt_buf, arow], [a_sb])
        pb = pss[4]
        specs = ((MLE, 0), (MGE, 16), (MGT, 0), (MLT, 16))
        for i, (mi, ao) in enumerate(specs):
            k.mm(pb.t[:, i * 16:(i + 1) * 16], cm.t[:, mi, :], a_sb.t[:, ao:ao + 16], True, True, [cm, a_sb], [pb], inc=False)
        k.mm(pb.t[:, 64:96], ones_f.t[:], a_sb.t[:, 0:32], True, True, [ones_f, a_sb], [pb], inc=True)
        k.act(ex.t[:], pb.t[:, 0:96], AF.Exp, [pb], [ex])

    order1 = [("c", 1), ("c", 0)] + [("x", cc_) for cc_ in range(NCH_RUN - 1, -1, -1)]
    lncache = {}

    def ln_for(i):
        if i >= len(order1) or i in lncache:
            return
        kind, cc_ = order1[i]
        if kind == "c":
            lncache[i] = ln0.run(ctx_d, ctx_d.t[cc_ * 128:(cc_ + 1) * 128, :], S1C, B1C, (pss[0], pss[1]))
        else:
            lncache[i] = ln0.run(x_d, x_d.t[cc_ * 128:(cc_ + 1) * 128, :], S1X, B1X, (pss[0], pss[1]))

    def s1(seq_buf, seq_ap, c, n, si, bi, slot_of, is_ctx):
        oi = order1.index(("c" if is_ctx else "x", c))
        ln_for(oi)
        xh, hT = lncache.pop(oi)
        for cc in range(12):
            pb = pss[2 + cc // 4]
            for kk in range(8):
                k.mm(pb.t[:, (cc % 4) * 128:(cc % 4 + 1) * 128], w1.t[:, kk, W_XBC + cc * 128:W_XBC + (cc + 1) * 128],
                     hT.t[:, kk, :], kk == 0, kk == 7, [w1, hT], [pb], inc=(kk == 7))
        me = pre[slot_of(c)]
        for q in range(3):
            pv = pss[2 + q].t[:].rearrange("p (a t) -> p a t", a=4)
            k.act(me.t[:, q * 4:(q + 1) * 4, 2:130], pv, AF.Copy, [pss[2 + q]], [me])
            KH = os.environ.get("KHALO", "")
            if c + 1 < n and KH != "skipL":
                k.copy("dve", pre[slot_of(c + 1)].t[:, q * 4:(q + 1) * 4, 0:2], pv[:, :, 126:128], [pss[2 + q]], [pre[slot_of(c + 1)]])
            if c - 1 >= 0 and KH != "skipR":
                k.copy("dve", pre[slot_of(c - 1)].t[:, q * 4:(q + 1) * 4, 130:132], pv[:, :, 0:2], [pss[2 + q]], [pre[slot_of(c - 1)]])
        if c == n - 1:
            k.memset("dve", me.t[:, :, 130:132], 0.0, [me])
        if c == 0:
            k.memset("dve", me.t[:, :, 0:2], 0.0, [me])
        yield "halo"
        pb = pss[5]
        proj_tok(hT, w1, W_DT, 32, pb, dtb_rows, 0)
        k.act(e_t.t[:], pb.t[:, 0:32], AF.Exp, [pb], [e_t])
        dts = dtc_all if is_ctx else dt_all
        k.act(dts.t[:, c, :], e_t.t[:], AF.Ln, [e_t], [dts], bias=1.0)
        if is_ctx or os.environ.get("KSKIP_GM"):
            ln_for(oi + 1)
            return
        yield
        for half in range(2):
            pb = pss[5 + half]
            proj_tok(hT, w1, W_U + half * 512, 512, pb)
            k.act(gu.t[:, half * 512:(half + 1) * 512], pb.t[:], AF.Gelu_apprx_tanh, [pb], [gu])
        yield
        for half in range(2):
            pb = pss[5 + half]
            proj_tok(hT, w1, W_V + half * 512, 512, pb)
            k.act(gv.t[:, half * 512:(half + 1) * 512], pb.t[:], AF.Gelu_apprx_tanh, [pb], [gv])
        yield
        ln_stats(gv, vst, vmv, vrs, vnm)
        k.act(vhat.t[:], gv.t[:], AF.Identity, [gv, vrs, vnm], [vhat], bias=vnm.t[:], scale=vrs.t[:])
        for half in range(2):
            pb = pss[5 + half]
            proj_tok(hT, w1, W_G2 + half * 512, 512, pb, bg2_rows, half * 512)
            k.act(g2.t[:, half * 512:(half + 1) * 512], pb.t[:], AF.Sigmoid, [pb], [g2])
        yield
        ln_for(oi + 1)
        yield
        for g in range(8):
            pb = pss[5 + g // 4]
            k.mm(pb.t[:, (g % 4) * 128:(g % 4 + 1) * 128], wspT.t[:, g, :], vhat.t[:, g * 128:(g + 1) * 128],
                 True, True, [wspT, vhat], [pb], inc=(g % 4 == 3))
        bM = biasM.t[:].rearrange("p g c -> p (g c)")
        for half in range(2):
            sl = slice(half * 512, (half + 1) * 512)
            pb = pss[5 + half]
            k.tt("dve", ygm.t[:, sl], pb.t[:], gmgrow.t[:, sl], ALU.mult, [pb, gmgrow], [ygm])
            k.tt("dve", ygm.t[:, sl], ygm.t[:, sl], bM[:, sl], ALU.add, [ygm, biasM], [ygm])
            k.tt("dve", ygm_bf.t[:, sl], ygm.t[:, sl], gu.t[:, sl], ALU.mult, [ygm, gu], [ygm_bf])
        yield
        transpose_to(ygmT, ygm_bf, pss[5], "dve")
        for half in range(2):
            pb = pss[5 + half]
            for kk in range(8):
                k.mm(pb.t[:], ygmT.t[:, kk, :], wgm.t[:, kk, half * 512:(half + 1) * 512], kk == 0, kk == 7,
                     [ygmT, wgm], [pb], inc=(kk == 7))
            k.tt("dve", G2.t[:, half * 512:(half + 1) * 512], pb.t[:], g2.t[:, half * 512:(half + 1) * 512], ALU.mult, [pb, g2], [G2])
        k.dma("pool", g2_s[c].t, G2.t[:], r=[G2], w=[g2_s[c]])

    def post(c, n, slot_of, is_ctx):
        if os.environ.get("KSKIP_POST") and not is_ctx:
            return
        me = pre[slot_of(c)]
        for cc in range(12):
            pb = pss[2 + cc // 4]
            for tap in range(5):
                k.mm(pb.t[:, (cc % 4) * 128:(cc % 4 + 1) * 128], dgw.t[:, cc, tap, :], me.t[:, cc, tap:tap + 128],
                     tap == 0, tap == 4, [dgw, me], [pb], inc=(tap == 4 and cc % 4 == 3))
        for cc in range(12):
            pb = pss[2 + cc // 4]
            k.act(xbcT.t[:, cc, :], pb.t[:, (cc % 4) * 128:(cc % 4 + 1) * 128], AF.Silu, [pb, convb], [xbcT], bias=convb.t[:, cc:cc + 1])
        yield

        pv0 = pss[0].t[:].bitcast(BF16)
        pv1 = pss[1].t[:].bitcast(BF16)
        for cix in range(8):
            k.tr(pv0[:, cix * 128:(cix + 1) * 128], xbcT.t[:, cix, :], ident_bf.t[:], [xbcT, ident_bf], [pss[0]], inc=(cix == 7))
        for cix in range(2):
            k.tr(pv1[:, cix * 128:(cix + 1) * 128], xbcT.t[:, 8 + cix, :], ident_bf.t[:], [xbcT, ident_bf], [pss[1]], inc=(cix == 1))
        k.copy("dve", xs_tok.t[:], pv0, [pss[0]], [xs_tok])
        k.copy("act", b_tok.t[:], pv1[:, 0:256], [pss[1]], [b_tok])
        if not is_ctx:
            k.dma("pool", xs_s[c].t, xs_tok.t[:], r=[xs_tok], w=[xs_s[c]])
            k.dma("pool", bt_s[c].t, b_tok.t[:], r=[b_tok], w=[bt_s[c]])
            k.dma("pool", bc_s[c].t, xbcT.t[:, 8:12, :], r=[xbcT], w=[bc_s[c]])
        yield
        dts = dtc_all if is_ctx else dt_all
        small_exps(dts.t[:, c, :], dts)
        k.tt("dve", ddv.t[:, 16:32], ex.t[:, 48:64], dts.t[:, c, 16:32], ALU.mult, [ex, dts], [ddv])
        if is_ctx:
            k.tt("dve", ddv.t[:, 0:16], ex.t[:, 32:48], dts.t[:, c, 0:16], ALU.mult, [ex, dts], [ddv])
        dirs = ((1, run_b),) + (((0, run_f),) if is_ctx else ())
        for d, run in dirs:
            yield
            k.tt("dve", xdd.t[:].rearrange("p (h q) -> p h q", h=H), xs_tok.t[:].rearrange("p (h q) -> p h q", h=H),
                 bc(ddv.t[:, d * 16:(d + 1) * 16].unsqueeze(2), [128, H, P]), ALU.mult, [xs_tok, ddv], [xdd])
            for g in range(2):
                pb = pss[2 + g]
                k.mm(pb.t[:], b_tok.t[:, g * 128:(g + 1) * 128], xdd.t[:, g * 512:(g + 1) * 512], True, True,
                     [b_tok, xdd], [pb], inc=True)
            dec = ex.t[:, 64 + d * 16:80 + d * 16]
            if is_ctx and d == 0:
                if c == 1:
                    for g in range(2):
                        k.copy("act", sf1.t[:, g * 512:(g + 1) * 512], pss[2 + g].t[:], [pss[2 + g]], [sf1])
                    k.copy("dve", decf1.t[:], dec, [ex], [decf1])
                else:
                    for g in range(2):
                        sl = slice(g * 512, (g + 1) * 512)
                        k.tt("dve", run_f.t[:, sl].rearrange("p (h q) -> p h q", h=8), pss[2 + g].t[:].rearrange("p (h q) -> p h q", h=8),
                             bc(decf1.t[:, g * 8:(g + 1) * 8].unsqueeze(2), [128, 8, P]), ALU.mult, [pss[2 + g], decf1], [run_f])
                        k.tt("dve", run_f.t[:, sl], run_f.t[:, sl], sf1.t[:, sl], ALU.add, [run_f, sf1], [run_f])
                continue
            if not is_ctx:
                k.copy("dve", hp_bf.t[:], run.t[:], [run], [hp_bf])
                k.dma("pool", hp_s[c].t, hp_bf.t[:], r=[hp_bf], w=[hp_s[c]])
            k.tt("dve", run.t[:].rearrange("p (h q) -> p h q", h=H), run.t[:].rearrange("p (h q) -> p h q", h=H),
                 bc(dec.unsqueeze(2), [128, H, P]), ALU.mult, [run, ex], [run])
            for g in range(2):
                sl = slice(g * 512, (g + 1) * 512)
                k.tt("dve", run.t[:, sl], run.t[:, sl], pss[2 + g].t[:], ALU.add, [run, pss[2 + g]], [run])

    def run_all(g):
        for _ in g:
            pass

    def interleave(ga, gb):
        da = db = False
        while not (da and db):
            if not da:
                try:
                    next(ga)
                except StopIteration:
                    da = True
            if not db:
                try:
                    next(gb)
                except StopIteration:
                    db = True

    dtc_all = k.sb([128, 2, 32], F32, "dtcall")
    for c in (1, 0):
        run_all(s1(ctx_d, ctx_d.t, c, 2, S1C, B1C, lambda cc: cc % 3, True))
        if c + 1 < 2:
            run_all(post(c + 1, 2, lambda cc: cc % 3, True))
    run_all(post(0, 2, lambda cc: cc % 3, True))
    k.dump("s_f", run_f, run_f.t[:], [128, D])
    k.dump("s_b", run_b, run_b.t[:], [128, D])
    NR = NCH_RUN
    KSTOP = int(os.environ.get("KSTOP", "1000"))
    steps = 0
    for c in range(NR - 1, -1, -1):
        if steps >= KSTOP:
            break
        ga = s1(x_d, x_d.t, c, NR, S1X, B1X, lambda cc: cc % 3, False)
        next(ga)
        steps += 1
        if c + 1 < NR:
            interleave(ga, post(c + 1, NR, lambda cc: cc % 3, False))
            steps += 1
        else:
            run_all(ga)
    if steps < KSTOP:
        run_all(post(0, NR, lambda cc: cc % 3, False))
    k.dump("dt_all", dt_all, dt_all.t[:].rearrange("p c d -> p (c d)"), [128, NCH * 32])
    k.pop()

    if KPASS < 2:
        k.barrier()
        k.es.close()
        return nc, k.dbg
    st = k.push()
    bg1_rows = bias_rows(bgate_d.t[0:1, 0:D], D, 1.0, "bg1")
    b0_rows = bias_rows(rows_d.t[1:2, :], D, ALPHA, "b0r")
    a0row = k.sb([128, D], F32, "a0row")
    k.dma("sp", a0row.t[:], rows_d.t[0:1, :].partition_broadcast(128), w=[a0row])
    k.ts("dve", a0row.t[:], a0row.t[:], float(ALPHA), None, ALU.mult, None, [a0row], [a0row])
    ngrow = k.sb([128, D], F32, "ngrow")
    k.dma("sp", ngrow.t[:], rows_d.t[2:3, :].partition_broadcast(128), w=[ngrow])
    make_row_psum(16, "g1row", (pss[4], pss[5]))
    wout = k.sb([128, 8, D], BF16, "wout")
    load_w(wout, wout_d.t, 8)
    w2 = k.sb([128, 8, 2048], BF16, "w2")
    for kk in range(8):
        rs_ = slice(kk * 128, (kk + 1) * 128)
        k.dma("pool", w2.t[:, kk, 0:1024], win_d.t[rs_, O_Z:O_Z + D], r=[win_d], w=[w2], partial=True)
        k.dma("pool", w2.t[:, kk, 1024:2048], win_d.t[rs_, O_G:O_G + D], r=[win_d], w=[w2], partial=True)
    wssd = k.sb([128, 8, D], BF16, "wssd")
    load_w(wssd, wssd_d.t, 8)

    def scale_wout():
        for kk in range(8):
            for half in range(2):
                sl = slice(half * 512, (half + 1) * 512)
                k.tt("dve", wout.t[:, kk, sl], wout.t[:, kk, sl], pss[4 + half].t[:], ALU.mult, [wout, pss[4 + half]], [wout])

    ln0 = LN0()
    xs_l = [k.sb([128, D], BF16, "xs_l") for _ in range(2)]
    bt_l = [k.sb([128, 256], BF16, "bt_l") for _ in range(2)]
    bc_l = [k.sb([128, 4, 128], BF16, "bc_l") for _ in range(2)]
    hp_l = [k.sb([128, D], BF16, "hp_l") for _ in range(2)]
    g2_l = [k.sb([128, D], BF16, "g2_l") for _ in range(2)]
    a_sb = k.sb([128, 32], F32, "a_sb2")
    ex = k.sb([128, 96], F32, "ex2")
    ddv = k.sb([128, 32], F32, "ddv2")
    sz_l = [k.sb([128, D], BF16, "sz") for _ in range(2)]
    g1_l = [k.sb([128, D], BF16, "g1") for _ in range(2)]
    Rf = k.sb([128, H, 128], F32R, "Rf")
    Rb = Rf
    Ef = k.sb([128, H, 128], F32, "Ef")
    cbm = k.sb([128, 2, 2, 128], F32, "cbm")
    wgt = [k.sb([128, H, 128], BF16, f"wgt{d}") for d in range(2)]
    xdt = [k.sb([128, D], BF16, f"xdt{d}") for d in range(2)]
    xdd = k.sb([128, D], BF16, "xdd2")
    runf_bf = k.sb([128, D], BF16, "runf_bf")
    t1 = k.sb([128, D], F32, "t1")
    t2 = k.sb([128, D], F32, "t2")
    hh = k.sb([128, D], F32, "hh")
    sq = t2
    ss = k.sb([128, 1], F32, "ss")
    rstd = k.sb([128, 1], F32, "rstd")
    yg = k.sb([128, D], BF16, "yg")
    ygT = k.sb([128, 8, 128], BF16, "ygT")
    m1 = hh
    mg = k.sb([128, D], BF16, "mg")
    mgT = k.sb([128, 8, 128], BF16, "mgT")
    r1 = t1
    x1h = [k.sb([128, D], F32, "x1h") for _ in range(2)]
    lst = k.sb([128, 2, 6], F32, "lst")
    lmv = k.sb([128, 2], F32, "lmv")
    lrs = k.sb([128, 1], F32, "lrs")
    lnm = k.sb([128, 1], F32, "lnm")

    k.copy("act", runf_bf.t[:], run_f.t[:], [run_f], [runf_bf])
    print("SBUF free in pass 2:", nc.sbuf_bytes_remaining)

    def h3(ap, h=H):
        return ap.rearrange("p (h q) -> p h q", h=h)

    KSTOP2 = int(os.environ.get("KSTOP2", "1000"))
    ln2cache = {}

    def front2a(c):
        ln2cache[c] = ln0.run(x_d, x_d.t[c * 128:(c + 1) * 128, :], S1X, B1X, (pss[0], pss[1]))

    def front2b(c):
        xh, hT = ln2cache[c]
        sz, g1 = sz_l[c % 2], g1_l[c % 2]
        for half in range(2):
            pb = pss[half]
            proj_tok(hT, w2, half * 512, 512, pb)
            k.act(sz.t[:, half * 512:(half + 1) * 512], pb.t[:], AF.Silu, [pb], [sz])
        for half in range(2):
            pb = pss[half]
            proj_tok(hT, w2, 1024 + half * 512, 512, pb, bg1_rows, half * 512)
            k.act(g1.t[:, half * 512:(half + 1) * 512], pb.t[:], AF.Sigmoid, [pb], [g1])

    front2a(0)
    scale_wout()
    front2b(0)
    for c in range(min(NR, KSTOP2)):
        s = c % 2
        xs, bt, bcl, hp, g2l = xs_l[s], bt_l[s], bc_l[s], hp_l[s], g2_l[s]
        k.dma("sp", xs.t[:], xs_s[c].t, r=[xs_s[c]], w=[xs])
        k.dma("sp", bt.t[:], bt_s[c].t, r=[bt_s[c]], w=[bt])
        k.dma("sp", bcl.t[:], bc_s[c].t, r=[bc_s[c]], w=[bcl])
        k.dma("sp", hp.t[:], hp_s[c].t, r=[hp_s[c]], w=[hp])
        k.dma("sp", g2l.t[:], g2_s[c].t, r=[g2_s[c]], w=[g2l])
        xh, hT = ln2cache.pop(c)
        sz, g1 = sz_l[s], g1_l[s]
        xo = x1h[s]
        k.tt("pool", xo.t[:], xh.t[:], a0row.t[:], ALU.mult, [xh, a0row], [xo])
        if c + 1 < min(NR, KSTOP2):
            front2a(c + 1)
        dtc = dt_all.t[:, c, :]
        k.tt("dve", a_sb.t[:], dtc, arow.t[:], ALU.mult, [dt_all, arow], [a_sb])
        pb = pss[6]
        for i, (mi, ao) in enumerate(((MLE, 0), (MGE, 16), (MGT, 0), (MLT, 16))):
            k.mm(pb.t[:, i * 16:(i + 1) * 16], cm.t[:, mi, :], a_sb.t[:, ao:ao + 16], True, True, [cm, a_sb], [pb], inc=False)
        k.mm(pb.t[:, 64:96], ones_f.t[:], a_sb.t[:, 0:32], True, True, [ones_f, a_sb], [pb], inc=True)
        k.act(ex.t[:], pb.t[:, 0:96], AF.Exp, [pb], [ex])
        pb = pss[6]
        for g in range(2):
            k.mm(pb.t[:, 256 + g * 128:256 + (g + 1) * 128], bcl.t[:, g, :], bcl.t[:, 2 + g, :], True, True, [bcl], [pb], inc=(g == 1))
        pv = pb.t[:, 256:512].rearrange("p (g i) -> p g i", g=2)
        k.tt("dve", cbm.t[:, 0, :, :], pv, bc(cm.t[:, MLE, :].unsqueeze(1), [128, 2, 128]), ALU.mult, [pb, cm], [cbm])
        k.tt("dve", cbm.t[:, 1, :, :], pv, bc(cm.t[:, MGE, :].unsqueeze(1), [128, 2, 128]), ALU.mult, [pb, cm], [cbm])
        for d in range(2):
            k.tt("pool", h3(xdt[d].t[:]), h3(xs.t[:]), bc(dtc[:, d * 16:(d + 1) * 16].unsqueeze(2), [128, H, P]),
                 ALU.mult, [xs, dt_all], [xdt[d]])
        for d, (R, Lm) in enumerate(((Rf, mgt_r), (Rb, mlt_r))):
            k.tt("pool", R.t[:], bc(a_sb.t[:, d * 16:(d + 1) * 16].unsqueeze(2), [128, H, 128]),
                 bc(cm.t[:, MGE if d else MLE, :].unsqueeze(1), [128, H, 128]), ALU.mult, [a_sb, cm], [R])
            R2 = R.t[:].rearrange("p h i -> p (h i)")
            E2 = Ef.t[:].rearrange("p h i -> p (h i)")
            for q in range(4):
                pb = pss[2 + q]
                k.mm(pb.t[:], Lm.t[:], R2[:, q * 512:(q + 1) * 512], True, True, [Lm, R], [pb], inc=True)
                k.act(E2[:, q * 512:(q + 1) * 512], pb.t[:], AF.Exp, [pb], [Ef])
            for g in range(2):
                k.tt("dve", wgt[d].t[:, g * 8:(g + 1) * 8, :], Ef.t[:, g * 8:(g + 1) * 8, :],
                     bc(cbm.t[:, d, g, :].unsqueeze(1), [128, 8, 128]), ALU.mult, [Ef, cbm], [wgt[d]])
        if c + 1 < min(NR, KSTOP2):
            front2b(c + 1)
        for h in range(H):
            pb = pss[h // 8]
            o = pb.t[:, (h % 8) * 64:(h % 8 + 1) * 64]
            cs = slice(h * 64, (h + 1) * 64)
            k.mm(o, wgt[0].t[:, h, :], xdt[0].t[:, cs], True, False, [wgt[0], xdt[0]], [pb], inc=False)
            k.mm(o, wgt[1].t[:, h, :], xdt[1].t[:, cs], False, False, [wgt[1], xdt[1]], [pb], inc=False)
            k.mm(o, dh.t[:, h, :], xs.t[:, cs], False, True, [dh, xs], [pb], inc=(h % 8 == 7))
        for g in range(2):
            k.mm(pss[2 + g].t[:], bcl.t[:, 2 + g, :], runf_bf.t[:, g * 512:(g + 1) * 512], True, True, [bcl, runf_bf], [pss[2 + g]], inc=True)
            k.mm(pss[4 + g].t[:], bcl.t[:, 2 + g, :], hp.t[:, g * 512:(g + 1) * 512], True, True, [bcl, hp], [pss[4 + g]], inc=True)
        for g in range(2):
            sl = slice(g * 512, (g + 1) * 512)
            k.tt("dve", h3(t1.t[:, sl], 8), h3(pss[2 + g].t[:], 8), bc(ex.t[:, g * 8:(g + 1) * 8].unsqueeze(2), [128, 8, P]),
                 ALU.mult, [pss[2 + g], ex], [t1])
            k.tt("dve", h3(t2.t[:, sl], 8), h3(pss[4 + g].t[:], 8), bc(ex.t[:, 16 + g * 8:16 + (g + 1) * 8].unsqueeze(2), [128, 8, P]),
                 ALU.mult, [pss[4 + g], ex], [t2])
            k.tt("pool", t1.t[:, sl], t1.t[:, sl], t2.t[:, sl], ALU.add, [t1, t2], [t1])
            k.tt("dve", t1.t[:, sl], t1.t[:, sl], pss[g].t[:], ALU.add, [t1, pss[g]], [t1])
        if c == 0:
            k.dump("y0", t1, t1.t[:], [128, D])
        k.tt("dve", hh.t[:], t1.t[:], sz.t[:], ALU.mult, [t1, sz], [hh])
        k.act(sq.t[:], hh.t[:], AF.Square, [hh], [sq, ss], accum=ss.t[:])
        rsqrt_act(rstd, ss.t[:], ss, 1.0 / D)
        k.stt(yg.t[:], hh.t[:], rstd.t[:], ngrow.t[:], ALU.mult, ALU.mult, [hh, rstd, ngrow], [yg])

        transpose_to(ygT, yg, pss[6], "act")
        k.tt("dve", ddv.t[:, 0:16], ex.t[:, 32:48], dtc[:, 0:16], ALU.mult, [ex, dt_all], [ddv])
        k.tt("pool", h3(xdd.t[:]), h3(xs.t[:]), bc(ddv.t[:, 0:16].unsqueeze(2), [128, H, P]), ALU.mult, [xs, ddv], [xdd])
        for g in range(2):
            k.mm(pss[2 + g].t[:], bt.t[:, g * 128:(g + 1) * 128], xdd.t[:, g * 512:(g + 1) * 512], True, True, [bt, xdd], [pss[2 + g]], inc=True)
        k.tt("pool", h3(run_f.t[:]), h3(run_f.t[:]), bc(ex.t[:, 64:80].unsqueeze(2), [128, H, P]), ALU.mult, [run_f, ex], [run_f])
        for g in range(2):
            sl = slice(g * 512, (g + 1) * 512)
            k.tt("dve", run_f.t[:, sl], run_f.t[:, sl], pss[2 + g].t[:], ALU.add, [run_f, pss[2 + g]], [run_f])
        k.copy("act", runf_bf.t[:], run_f.t[:], [run_f], [runf_bf])
        for half in range(2):
            sl = slice(half * 512, (half + 1) * 512)
            pb = pss[4 + half]
            for kk in range(8):
                k.mm(pb.t[:], ygT.t[:, kk, :], wssd.t[:, kk, sl], kk == 0, kk == 7, [ygT, wssd], [pb], inc=(kk == 7))
            k.tt("dve", m1.t[:, sl], pb.t[:], g1.t[:, sl], ALU.mult, [pb, g1], [m1])
            k.tt("dve", mg.t[:, sl], m1.t[:, sl], g2l.t[:, sl], ALU.add, [m1, g2l], [mg])

        transpose_to(mgT, mg, pss[6], "dve")
        for half in range(2):
            sl = slice(half * 512, (half + 1) * 512)
            pb = pss[half]
            for kk in range(8):
                k.mm(pb.t[:], mgT.t[:, kk, :], wout.t[:, kk, sl], kk == 0, False, [mgT, wout], [pb], inc=False)
            k.mm(pb.t[:], ones_bf.t[0:2, :], b0_rows.t[0:2, sl], False, True, [ones_bf, b0_rows], [pb], inc=True)
            k.tt("dve", r1.t[:, sl], xo.t[:, sl], pb.t[:], ALU.add, [xo, pb], [r1])
        if c == 0:
            k.dump("r1", r1, r1.t[:], [128, D])
        ln_stats(r1, lst, lmv, lrs, lnm)
        k.act(xo.t[:], r1.t[:], AF.Identity, [r1, lrs, lnm], [xo], bias=lnm.t[:], scale=lrs.t[:])
        k.dma("pool", x1_s[c].t, xo.t[:], r=[xo], w=[x1_s[c]])
    k.pop()

    k.fill = None
    if KPASS < 3:
        k.barrier()
        k.es.close()
        return nc, k.dbg
    st = k.push()
    b1_rows = bias_rows(rows_d.t[6:7, :], D, ALPHA, "b1r")
    a1row = k.sb([128, D], F32, "a1row")
    k.dma("sp", a1row.t[:], rows_d.t[5:6, :].partition_broadcast(128), w=[a1row])
    k.ts("dve", a1row.t[:], a1row.t[:], float(ALPHA), None, ALU.mult, None, [a1row], [a1row])
    l2g = k.sb([128, D], F32, "l2g")
    k.dma("sp", l2g.t[:], rows_d.t[7:8, :].partition_broadcast(128), w=[l2g])
    l2b = k.sb([128, D], F32, "l2b")
    k.dma("sp", l2b.t[:], rows_d.t[8:9, :].partition_broadcast(128), w=[l2b])
    make_row_psum(40, "g2row", (pss[6], pss[7]))
    wf1 = k.sb([128, 8, DFF], BF16, "wf1")
    wf3 = k.sb([128, 8, DFF], BF16, "wf3")
    wf2 = k.sb([128, NFF, D], BF16, "wf2")
    FG = ((0, 6), (6, 12), (12, 17), (17, 22))
    wf1g = [Buf(wf1.t) for _ in FG]
    wf3g = [Buf(wf3.t) for _ in FG]
    for gi, (f0, f1) in enumerate(FG):
        for kk in range(8):
            k.dma("pool", wf1.t[:, kk, f0 * 128:f1 * 128], wff1_d.t[kk * 128:(kk + 1) * 128, f0 * 128:f1 * 128], w=[wf1g[gi]], partial=True)
        for kk in range(8):
            k.dma("pool", wf3.t[:, kk, f0 * 128:f1 * 128], wff3_d.t[kk * 128:(kk + 1) * 128, f0 * 128:f1 * 128], w=[wf3g[gi]], partial=True)
    load_w(wf2, wff2_d.t, NFF)

    def fgrp(f):
        return [gi for gi, (f0, f1) in enumerate(FG) if f0 <= f < f1][0]

    def scale_wf2():
        for kk in range(NFF):
            for half in range(2):
                sl = slice(half * 512, (half + 1) * 512)
                k.tt("dve", wf2.t[:, kk, sl], wf2.t[:, kk, sl], pss[6 + half].t[:], ALU.mult, [wf2, pss[6 + half]], [wf2])
    TB = 2
    NB = NR // TB
    x1t = [[k.sb([128, D], F32, "x1t") for _ in range(TB)] for _ in range(2)]
    xmT = [k.sb([128, 8, TB * 128], BF16, "xmT") for _ in range(2)]
    hid = k.sb([128, NFF, TB * 128], BF16, "hid")
    sl1 = [k.sb([128, TB * 128], F32, "sl1") for _ in range(2)]
    fst = k.sb([128, 2, 6], F32, "fst")
    fmv = k.sb([128, 2], F32, "fmv")
    frs = k.sb([128, 1], F32, "frs")
    fnm = k.sb([128, 1], F32, "fnm")
    NT = TB * 128
    print("SBUF free in pass 3:", nc.sbuf_bytes_remaining)

    def ffn_front(b):
        s = b % 2
        for t in range(TB):
            c = b * TB + t
            xt = x1t[s][t]
            k.dma("sp", xt.t[:], x1_s[c].t, r=[x1_s[c]], w=[xt])
            for cc in range(8):
                pb = pss[cc // 4]
                k.tr(pb.t[:, (cc % 4) * 128:(cc % 4 + 1) * 128], xt.t[:, cc * 128:(cc + 1) * 128], cm.t[:, IDN, :],
                     [xt, cm], [pb], inc=(cc % 4 == 3))
            for cc in range(8):
                pb = pss[cc // 4]
                src = pb.t[:, (cc % 4) * 128:(cc % 4 + 1) * 128]
                dst = xmT[s].t[:, cc, t * 128:(t + 1) * 128]
                if cc < 4:
                    k.ts("dve", dst, src, cols.t[:, S2, cc:cc + 1], cols.t[:, B2, cc:cc + 1], ALU.mult, ALU.add, [pb, cols], [xmT[s]])
                else:
                    k.act(dst, src, AF.Identity, [pb, cols], [xmT[s]], bias=cols.t[:, B2, cc:cc + 1], scale=cols.t[:, S2, cc:cc + 1])
            k.tt("dve", xt.t[:], xt.t[:], a1row.t[:], ALU.mult, [xt, a1row], [xt])

    def ffn_w13(b):
        s = b % 2
        for f in range(NFF):
            p1 = pss[2 + (f % 2) * 2]
            p3 = pss[3 + (f % 2) * 2]
            for kk in range(8):
                k.mm(p1.t[:, 0:NT], wf1.t[:, kk, f * 128:(f + 1) * 128], xmT[s].t[:, kk, :], kk == 0, kk == 7, [wf1g[fgrp(f)], xmT[s]], [p1], inc=(kk == 7))
            for kk in range(8):
                k.mm(p3.t[:, 0:NT], wf3.t[:, kk, f * 128:(f + 1) * 128], xmT[s].t[:, kk, :], kk == 0, kk == 7, [wf3g[fgrp(f)], xmT[s]], [p3], inc=(kk == 7))
            sv = sl1[f % 2]
            k.act(sv.t[:], p1.t[:, 0:NT], AF.Silu, [p1], [sv])
            k.tt("dve", hid.t[:, f, :], sv.t[:], p3.t[:, 0:NT], ALU.mult, [sv, p3], [hid])

    def ffn_w2(b):
        s = b % 2
        for t in range(TB):
            c = b * TB + t
            xt = x1t[s][t]
            r2 = xt
            for half in range(2):
                sl = slice(half * 512, (half + 1) * 512)
                pb = pss[6 + half]
                for f in range(NFF):
                    k.mm(pb.t[:], hid.t[:, f, t * 128:(t + 1) * 128], wf2.t[:, f, sl], f == 0, False, [hid, wf2], [pb], inc=False)
                k.mm(pb.t[:], ones_bf.t[0:2, :], b1_rows.t[0:2, sl], False, True, [ones_bf, b1_rows], [pb], inc=True)
                k.tt("dve", r2.t[:, sl], r2.t[:, sl], pb.t[:], ALU.add, [r2, pb], [r2])
            ln_stats(r2, fst, fmv, frs, fnm)
            o = r2
            k.act(o.t[:], r2.t[:], AF.Identity, [r2, frs, fnm], [o], bias=fnm.t[:], scale=frs.t[:])
            k.tt("dve", o.t[:], o.t[:], l2g.t[:], ALU.mult, [o, l2g], [o])
            k.tt("dve", o.t[:], o.t[:], l2b.t[:], ALU.add, [o, l2b], [o])
            k.dma("pool", out_d.t[c * 128:(c + 1) * 128, :], o.t[:], r=[o], w=[out_d])

    if NB > 0:
        ffn_front(0)
    for b in range(NB):
        ffn_w13(b)
        if b + 1 < NB:
            ffn_front(b + 1)
        if b == 0:
            scale_wf2()
        ffn_w2(b)
    k.pop()
    k.barrier()
    k.es.close()
    return nc, k.dbg


def _prep(inputs):
    f = lambda a: np.ascontiguousarray(np.asarray(a, dtype=np.float32))
    i = {kk: f(v) for kk, v in inputs.items()}
    kk = np.arange(128)
    cm = np.stack([np.eye(128), kk[:, None] <= kk[None, :], kk[:, None] >= kk[None, :],
                   kk[:, None] > kk[None, :], kk[:, None] < kk[None, :]], axis=1).astype(np.float32)
    col = lambda v: f(v.reshape(-1, 128).T)
    shared = dict(
        cm=f(cm), w_ada=i["w_ada"][0], b_ada_c=col(i["b_ada"][0]),
        colp=f(np.stack([col(i["ln0_g"]), col(i["ln0_b"]), col(i["ln1_g"][0]), col(i["ln1_b"][0])], axis=1)),
        convw_c=f(i["conv_w"][0].T.reshape(12, 128, 5).transpose(1, 0, 2)),
        convb_c=col(i["conv_b"][0]), w_in=i["w_in"][0],
        rows=f(np.stack([i["ln0_g"], i["ln0_b"], i["ssd_norm_g"][0], i["gm_norm_g"][0], i["gm_norm_b"][0],
                         i["ln1_g"][0], i["ln1_b"][0], i["ln2_g"][0], i["ln2_b"][0]])),
        b_gate=f(i["b_gate"][0][None, :]),
        small=f(np.concatenate([i["dt_bias"][0].reshape(-1), i["a_log"][0].reshape(-1), i["d_skip"][0].reshape(-1)])[None, :]),
        wspT=f(i["w_spatial"][0].transpose(2, 0, 1)), wsp=f(i["w_spatial"][0].transpose(1, 0, 2)),
        bsp_c=f(i["b_spatial"][0].T),
        w_ssd=i["w_ssd_proj"][0], w_gm=i["w_gm_proj"][0], w_out=i["w_out"][0],
        w_ff1=i["w_ff1"][0], w_ff3=i["w_ff3"][0], w_ff2=i["w_ff2"][0],
    )
    maps = []
    for b in range(i["x"].shape[0]):
        m = dict(shared)
        m["x"] = i["x"][b]
        m["ctx"] = i["ctx"][b]
        m["cT"] = f(np.stack([i["c"][b], i["c_ctx"]], axis=1).reshape(8, 128, 2).transpose(1, 0, 2))
        maps.append(m)
    return maps


def kernel(**inputs):
    maps = _prep(inputs)
    nc, _ = build()
    n = len(maps)
    res = run_bass_kernel_spmd(nc, maps, core_ids=list(range(n)))
    return np.stack([np.asarray(r["out"], dtype=np.float32) for r in res.results], axis=0)
```

```python
import os
from contextlib import ExitStack
import numpy as np
import concourse.bass as bass
import concourse.mybir as mybir
from concourse.bass_utils import run_bass_kernel_spmd

F32 = mybir.dt.float32
F32R = mybir.dt.float32r
BF16 = mybir.dt.bfloat16
AF = mybir.ActivationFunctionType
ALU = mybir.AluOpType

D = 1024
SEQ = 4096
CTX = 256
NCH = SEQ // 128
H = 16
P = 64
NST = 128
DFF = 2816
NFF = DFF // 128
XBC = 1536
DPROJ = 6688
O_Z, O_XBC, O_DT, O_U, O_V, O_G = 0, 1024, 2560, 2592, 3616, 4640
ALPHA = 2.0 ** 0.25
EPS = 1e-5
DEBUG = bool(int(os.environ.get("KDEBUG", "0")))
NCH_RUN = int(os.environ.get("KNCH", str(NCH)))
KPASS = int(os.environ.get("KPASS", "3"))
NFILL = int(os.environ.get("KFILL", "3"))


class Buf:
    def __init__(self, t):
        self.t = t
        self.w = None
        self.pw = []
        self.r = {}
        self.psum = False

    def __getitem__(self, idx):
        return self.t[idx]


class Eng:
    def __init__(self, e, sem, name):
        self.e, self.sem, self.name = e, sem, name
        self.count = 0
        self.seen = {}
        self.pend_r, self.pend_w = [], []


class KB:
    def __init__(self):
        self.nc = bass.Bass("TRN2", target_bir_lowering=False)
        nc = self.nc
        self.es = ExitStack()
        self.stacks = [self.es]
        self.engs = {}
        for name, e in (("pe", nc.tensor), ("act", nc.scalar), ("dve", nc.vector), ("pool", nc.gpsimd), ("sp", nc.sync)):
            sem = self.es.enter_context(nc.semaphore("s_" + name))
            self.engs[name] = Eng(e, sem, name)
        self.dpool = {}
        for q, n in (("sp", 20), ("pool", 20), ("act", 6)):
            self.dpool[q] = [[self.es.enter_context(nc.semaphore(f"d_{q}{i}")), 0] for i in range(n)]
        self.dnext = {q: 0 for q in self.dpool}
        self.uid = 0
        self.dbg = []
        self.fill = None

    def sb(self, shape, dt, name=None):
        self.uid += 1
        t = self.stacks[-1].enter_context(self.nc.sbuf_tensor(f"{name or 't'}_{self.uid}", list(shape), dt))
        return Buf(t)

    def ps(self, name):
        self.uid += 1
        t = self.stacks[-1].enter_context(self.nc.psum_tensor(f"{name}_{self.uid}", [128, 512], F32))
        b = Buf(t)
        b.psum = True
        return b

    def dram(self, name, shape, dt, kind="Internal"):
        return Buf(self.nc.dram_tensor(name, list(shape), dt, kind=kind).ap())

    def push(self):
        print("SBUF remaining at push:", self.nc.sbuf_bytes_remaining)
        st = ExitStack()
        self.stacks.append(st)
        return st

    def pop(self):
        self.barrier()
        self.stacks.pop().close()

    def _waits(self, E, r, w, partial=False):
        deps = {}
        raw = set()
        for b in r:
            if b.w is not None:
                deps[(b.w[0], b.w[1])] = b.w
                raw.add((b.w[0], b.w[1]))
            for t in b.pw:
                deps[(t[0], t[1])] = t
            if b.psum:
                for t in b.r.values():
                    if t[2] is not E:
                        deps[(t[0], t[1])] = t
        for b in w:
            if b.w is not None:
                deps[(b.w[0], b.w[1])] = b.w
            if not partial:
                for t in b.pw:
                    deps[(t[0], t[1])] = t
            for t in b.r.values():
                deps[(t[0], t[1])] = t
        for key, (sem, val, owner) in deps.items():
            if owner is E:
                if E.name == "pe":
                    continue
                if key not in raw:
                    continue
            if E.seen.get(sem, 0) >= val:
                continue
            if E.name == "pe" and self.fill is not None and owner is not None:
                for v in range(max(E.seen.get(sem, 0) + 1, val - NFILL), val):
                    E.e.wait_ge(sem, v)
                    self.nc.tensor.matmul(self.fill[0], self.fill[1], self.fill[2], start=True, stop=True)
            E.e.wait_ge(sem, val)
            E.seen[sem] = val

    def op(self, eng, fn, r=(), w=(), inc=True):
        E = self.engs[eng]
        self._waits(E, r, w)
        ins = fn()
        E.pend_r.extend(r)
        E.pend_w.extend(w)
        if inc:
            E.count += 1
            ins.then_inc(E.sem, 1)
            tok = (E.sem, E.count, E)
            for b in E.pend_r:
                b.r[E.sem] = tok
            for b in E.pend_w:
                b.w = tok
                b.pw = []
                b.r = {}
            E.pend_r, E.pend_w = [], []
        return ins

    def dma(self, q, out, in_, r=(), w=(), partial=False, **kw):
        E = self.engs[q]
        self._waits(E, r, w, partial)
        pool = self.dpool[q]
        i = self.dnext[q]
        self.dnext[q] = (i + 1) % len(pool)
        sem, val = pool[i]
        if val > 0 and E.seen.get(sem, 0) < val:
            E.e.wait_ge(sem, val)
            E.seen[sem] = val
        ins = E.e.dma_start(out=out, in_=in_, **kw)
        val += 16
        pool[i][1] = val
        ins.then_inc(sem, 16)
        tok = (sem, val, None)
        for b in r:
            b.r[sem] = tok
        for b in w:
            if partial:
                b.pw.append(tok)
            else:
                b.w = tok
                b.pw = []
            b.r = {}

    def barrier(self):
        toks = [(E.sem, E.count) for E in self.engs.values() if E.count > 0]
        for pool in self.dpool.values():
            toks += [(s, v) for s, v in pool if v > 0]
        for E in self.engs.values():
            assert not E.pend_r and not E.pend_w, E.name
            for sem, val in toks:
                if sem is E.sem or E.seen.get(sem, 0) >= val:
                    continue
                E.e.wait_ge(sem, val)
                E.seen[sem] = val

    def act(self, out, in_, func, r, w, bias=None, scale=None, accum=None):
        kw = {}
        if bias is not None:
            kw["bias"] = bias
        if scale is not None:
            kw["scale"] = scale
        if accum is not None:
            kw["accum_out"] = accum
        return self.op("act", lambda: self.nc.scalar.activation(out=out, in_=in_, func=func, **kw), r, w)

    def tt(self, eng, out, in0, in1, op, r, w):
        e = self.engs[eng].e
        return self.op(eng, lambda: e.tensor_tensor(out=out, in0=in0, in1=in1, op=op), r, w)

    def ts(self, eng, out, in0, s1, s2, op0, op1, r, w):
        e = self.engs[eng].e
        if s2 is None:
            return self.op(eng, lambda: e.tensor_scalar(out=out, in0=in0, scalar1=s1, scalar2=None, op0=op0), r, w)
        return self.op(eng, lambda: e.tensor_scalar(out=out, in0=in0, scalar1=s1, scalar2=s2, op0=op0, op1=op1), r, w)

    def stt(self, out, in0, scalar, in1, op0, op1, r, w):
        return self.op("dve", lambda: self.nc.vector.scalar_tensor_tensor(
            out=out, in0=in0, scalar=scalar, in1=in1, op0=op0, op1=op1), r, w)

    def copy(self, eng, out, in_, r, w):
        e = self.engs[eng].e
        if eng == "act":
            return self.act(out, in_, AF.Copy, r, w)
        return self.op(eng, lambda: e.tensor_copy(out=out, in_=in_), r, w)

    def memset(self, eng, ap, val, w):
        e = self.engs[eng].e
        return self.op(eng, lambda: e.memset(ap, val), (), w)

    def mm(self, out, lhsT, rhs, start, stop, r, w, inc):
        return self.op("pe", lambda: self.nc.tensor.matmul(out, lhsT, rhs, start=start, stop=stop), r, w, inc)

    def tr(self, out, in_, ident, r, w, inc):
        return self.op("pe", lambda: self.nc.tensor.transpose(out, in_, ident), r, w, inc)

    def dump(self, name, buf, ap, shape, dt=F32):
        if not DEBUG:
            return
        d = self.dram("dbg_" + name, shape, dt, kind="ExternalOutput")
        self.dma("sp", d.t, ap, r=[buf], w=[d])
        self.dbg.append("dbg_" + name)


def bc(ap, shape):
    return ap.to_broadcast(list(shape))


def build():
    k = KB()
    nc = k.nc

    def din(name, shape, dt=F32):
        return k.dram(name, shape, dt, kind="ExternalInput")

    x_d = din("x", [SEQ, D])
    ctx_d = din("ctx", [CTX, D])
    cT_d = din("cT", [128, 8, 2])
    cm_d = din("cm", [128, 5, 128])
    wada_d = din("w_ada", [D, 6 * D])
    badac_d = din("b_ada_c", [128, 48])
    colp_d = din("colp", [128, 4, 8])
    convw_d = din("convw_c", [128, 12, 5])
    convb_d = din("convb_c", [128, 12])
    win_d = din("w_in", [D, DPROJ])
    rows_d = din("rows", [9, D])
    bgate_d = din("b_gate", [1, 2 * D])
    small_d = din("small", [1, 96])
    wspT_d = din("wspT", [128, 8, 128])
    wsp_d = din("wsp", [128, 8, 128])
    bsp_d = din("bsp_c", [128, 8])
    wssd_d = din("w_ssd", [D, D])
    wgm_d = din("w_gm", [D, D])
    wout_d = din("w_out", [D, D])
    wff1_d = din("w_ff1", [D, DFF])
    wff3_d = din("w_ff3", [D, DFF])
    wff2_d = din("w_ff2", [DFF, D])
    out_d = k.dram("out", [SEQ, D], F32, kind="ExternalOutput")

    xs_s = [k.dram(f"xs_s{c}", [128, D], BF16) for c in range(NCH)]
    bt_s = [k.dram(f"bt_s{c}", [128, 256], BF16) for c in range(NCH)]
    bc_s = [k.dram(f"bc_s{c}", [128, 4, 128], BF16) for c in range(NCH)]
    hp_s = [k.dram(f"hp_s{c}", [128, D], BF16) for c in range(NCH)]
    g2_s = [k.dram(f"g2_s{c}", [128, D], BF16) for c in range(NCH)]
    x1_s = [k.dram(f"x1_s{c}", [128, D], F32) for c in range(NCH)]

    cm = k.sb([128, 5, 128], F32, "cm")
    k.dma("sp", cm.t[:], cm_d.t, w=[cm])
    IDN, MLE, MGE, MGT, MLT = range(5)
    ident_bf = k.sb([128, 128], BF16, "identbf")
    k.copy("dve", ident_bf.t[:], cm.t[:, IDN, :], [cm], [ident_bf])
    mgt_r = k.sb([128, 128], F32R, "mgtr")
    mlt_r = k.sb([128, 128], F32R, "mltr")
    k.copy("dve", mgt_r.t[:], cm.t[:, MGT, :], [cm], [mgt_r])
    k.copy("dve", mlt_r.t[:], cm.t[:, MLT, :], [cm], [mlt_r])
    ones_f = k.sb([128, 128], F32, "onesf")
    k.memset("dve", ones_f.t[:], 1.0, [ones_f])
    ones_bf = k.sb([2, 128], BF16, "onesbf")
    k.memset("dve", ones_bf.t[:], 1.0, [ones_bf])

    colp = k.sb([128, 4, 8], F32, "colp")
    k.dma("sp", colp.t[:], colp_d.t, w=[colp])
    convw = k.sb([128, 12, 5], F32, "convw")
    k.dma("sp", convw.t[:], convw_d.t, w=[convw])
    convb = k.sb([128, 12], F32, "convb")
    k.dma("sp", convb.t[:], convb_d.t, w=[convb])
    smallr = k.sb([128, 96], F32, "smallr")
    k.dma("sp", smallr.t[:], small_d.t.partition_broadcast(128), w=[smallr])
    arow = k.sb([128, 32], F32, "arow")
    k.act(arow.t[:], smallr.t[:, 32:64], AF.Exp, [smallr], [arow])
    k.ts("dve", arow.t[:], arow.t[:], -1.0, None, ALU.mult, None, [arow], [arow])
    dsum = k.sb([128, 16], F32, "dsum")
    k.tt("dve", dsum.t[:], smallr.t[:, 64:80], smallr.t[:, 80:96], ALU.add, [smallr], [dsum])
    dh = k.sb([128, H, 128], BF16, "dh")
    k.tt("dve", dh.t[:], bc(cm.t[:, IDN, :].unsqueeze(1), [128, H, 128]),
         bc(dsum.t[:].unsqueeze(2), [128, H, 128]), ALU.mult, [cm, dsum], [dh])
    dt_all = k.sb([128, NCH, 32], F32, "dtall")
    run_f = k.sb([128, D], F32, "runf")
    run_b = k.sb([128, D], F32, "runb")
    modc = k.sb([128, 48, 2], F32, "modc")
    cols = k.sb([128, 6, 8], F32, "cols")
    S1X, B1X, S1C, B1C, S2, B2 = range(6)
    lntmp = k.sb([128, 1], F32, "lntmp")
    epsc = k.sb([128, 1], F32, "epsc")
    k.memset("dve", epsc.t[:], EPS, [epsc])

    pss = [k.ps(f"ps{i}") for i in range(8)]
    fill_spec = (pss[7].t[:, 0:256], ident_bf.t[:], dh.t[:, 0:2, :].rearrange("p a i -> p (a i)"))

    st = k.push()
    w1 = k.sb([128, 8, 1568 + 2048 + 1024], BF16, "w1")
    W_XBC, W_DT, W_U, W_V, W_G2 = 0, 1536, 1568, 2592, 3616
    for kk in range(8):
        rs_ = slice(kk * 128, (kk + 1) * 128)
        k.dma("pool", w1.t[:, kk, 0:1568], win_d.t[rs_, O_XBC:O_XBC + 1568], r=[win_d], w=[w1], partial=True)
    for kk in range(8):
        rs_ = slice(kk * 128, (kk + 1) * 128)
        k.dma("pool", w1.t[:, kk, 1568:3616], win_d.t[rs_, O_U:O_U + 2048], r=[win_d], w=[w1], partial=True)
        k.dma("pool", w1.t[:, kk, 3616:4640], win_d.t[rs_, O_G + D:O_G + 2 * D], r=[win_d], w=[w1], partial=True)
    wgm = k.sb([128, 8, D], BF16, "wgm")
    for kk in range(8):
        k.dma("pool", wgm.t[:, kk, :], wgm_d.t[kk * 128:(kk + 1) * 128, :], w=[wgm], partial=True)
    wspT = k.sb([128, 8, 128], BF16, "wspT")
    k.dma("pool", wspT.t[:], wspT_d.t, w=[wspT])
    st = k.push()
    cT = k.sb([128, 8, 2], F32, "cT")
    k.dma("sp", cT.t[:], cT_d.t, w=[cT])
    scT = k.sb([128, 8, 2], F32, "scT")
    k.act(scT.t[:], cT.t[:], AF.Silu, [cT], [scT])
    badac = k.sb([128, 48], F32, "badac")
    k.dma("sp", badac.t[:], badac_d.t, w=[badac])
    wst = [k.sb([128, 8, 512], F32, f"wst{i}") for i in range(4)]
    wada_v = wada_d.t.rearrange("(k p) n -> p k n", p=128)
    modrow = k.sb([2, 6 * D], F32, "modrow")
    for cb in range(12):
        wb = wst[cb % 4]
        k.dma("sp", wb.t[:], wada_v[:, :, cb * 512:(cb + 1) * 512], r=[wada_d], w=[wb])
        pb = pss[cb % 2]
        for kk in range(8):
            k.mm(pb.t[0:2, :], scT.t[:, kk, :], wb.t[:, kk, :], kk == 0, kk == 7, [scT, wb], [pb], inc=(kk == 7))
        k.copy("act", modrow.t[:, cb * 512:(cb + 1) * 512], pb.t[0:2, :], [pb], [modrow])
    pm = pss[2]
    for j in range(48):
        k.mm(pm.t[:, 2 * j:2 * j + 2], modrow.t[0:2, j * 128:(j + 1) * 128], cm.t[0:2, IDN, 0:2], True, True,
             [modrow, cm], [pm], inc=(j == 47))
    k.tt("dve", modc.t[:], pm.t[:, 0:96].rearrange("p (j m) -> p j m", m=2),
         bc(badac.t[:].unsqueeze(2), [128, 48, 2]), ALU.add, [pm, badac], [modc])
    tmp8 = k.sb([128, 8], F32, "tmp8")
    for (si, bi, m, g_i, b_i, sc_o, sh_o) in ((S1X, B1X, 0, 0, 1, 8, 0), (S1C, B1C, 1, 0, 1, 8, 0), (S2, B2, 0, 2, 3, 32, 24)):
        k.ts("dve", tmp8.t[:], modc.t[:, sc_o:sc_o + 8, m], 1.0, None, ALU.add, None, [modc], [tmp8])
        k.tt("dve", cols.t[:, si, :], tmp8.t[:], colp.t[:, g_i, :], ALU.mult, [tmp8, colp], [cols])
        k.tt("dve", cols.t[:, bi, :], tmp8.t[:], colp.t[:, b_i, :], ALU.mult, [tmp8, colp], [cols])
        k.tt("dve", cols.t[:, bi, :], cols.t[:, bi, :], modc.t[:, sh_o:sh_o + 8, m], ALU.add, [cols, modc], [cols])
    k.pop()
    k.dump("modc", modc, modc.t[:].rearrange("p j m -> p (j m)"), [128, 96])

    def make_row_psum(off, name, banks):
        dg = k.sb([128, 128], F32, name + "dg")
        for c in range(8):
            k.ts("dve", dg.t[:], cm.t[:, IDN, :], modc.t[:, off + c, 0:1], None, ALU.mult, None, [cm, modc], [dg])
            pb = banks[c // 4]
            k.mm(pb.t[:, (c % 4) * 128:(c % 4 + 1) * 128], ones_f.t[:], dg.t[:], True, True, [ones_f, dg], [pb], True)

    def make_row(off, name):
        row = k.sb([128, D], F32, name)
        dg = k.sb([128, 128], F32, name + "dg")
        for c in range(8):
            k.ts("dve", dg.t[:], cm.t[:, IDN, :], modc.t[:, off + c, 0:1], None, ALU.mult, None, [cm, modc], [dg])
            pb = pss[1 + (c // 4)]
            k.mm(pb.t[:, (c % 4) * 128:(c % 4 + 1) * 128], ones_f.t[:], dg.t[:], True, True, [ones_f, dg], [pb], True)
        k.copy("act", row.t[:, 0:512], pss[1].t[:], [pss[1]], [row])
        k.copy("act", row.t[:, 512:1024], pss[2].t[:], [pss[2]], [row])
        return row

    def load_w(dst, src_ap, k_chunks, eng="pool"):
        for kk in range(k_chunks):
            k.dma(eng, dst.t[:, kk, :], src_ap[kk * 128:(kk + 1) * 128, :], w=[dst], partial=True)

    def bias_rows(src_row_ap, n, scale, name):
        rows = k.sb([2, n], BF16, name)
        k.push()
        b32 = k.sb([1, n], F32, name + "32")
        k.dma("sp", b32.t[:], src_row_ap, w=[b32])
        if scale != 1.0:
            k.ts("dve", b32.t[:], b32.t[:], float(scale), None, ALU.mult, None, [b32], [b32])
        k.copy("dve", rows.t[0:1, :], b32.t[:], [b32], [rows])
        lo32 = k.sb([1, n], F32, name + "lo32")
        k.tt("dve", lo32.t[:], b32.t[:], rows.t[0:1, :], ALU.subtract, [b32, rows], [lo32])
        lo = k.sb([1, n], BF16, name + "lo")
        k.copy("dve", lo.t[:], lo32.t[:], [lo32], [lo])
        k.dma("sp", rows.t[1:2, :], lo.t[:], r=[lo], w=[rows])
        k.pop()
        return rows

    def rsqrt_act(out_buf, in_ap, in_buf, scale):
        k.act(lntmp.t[:], in_ap, AF.Ln, [in_buf, epsc], [lntmp], bias=epsc.t[:], scale=float(scale))
        k.act(out_buf.t[:], lntmp.t[:], AF.Exp, [lntmp], [out_buf], scale=-0.5)

    def ln_stats(src, tmp_st, tmp_mv, rs, nm):
        for i in range(2):
            k.op("dve", lambda i=i: nc.vector.bn_stats(out=tmp_st.t[:, i, :], in_=src.t[:, i * 512:(i + 1) * 512]), [src], [tmp_st])
        k.op("dve", lambda: nc.vector.bn_aggr(out=tmp_mv.t[:], in_=tmp_st.t[:].rearrange("p a b -> p (a b)")), [tmp_st], [tmp_mv])
        rsqrt_act(rs, tmp_mv.t[:, 1:2], tmp_mv, 1.0)
        k.ts("dve", nm.t[:], tmp_mv.t[:, 0:1], rs.t[:], -1.0, ALU.mult, ALU.mult, [tmp_mv, rs], [nm])

    def proj_tok(hT, W, c0, n, pb, brow=None, b0=0):
        for kk in range(8):
            last = (kk == 7 and brow is None)
            k.mm(pb.t[:, 0:n], hT.t[:, kk, :], W.t[:, kk, c0:c0 + n], kk == 0, last, [hT, W], [pb], inc=last)
        if brow is not None:
            k.mm(pb.t[:, 0:n], ones_bf.t[0:2, :], brow.t[0:2, b0:b0 + n], False, True, [ones_bf, brow], [pb], inc=True)

    def transpose_to(dstT, src, pb, eng):
        pv = pb.t[:].bitcast(BF16)
        for c in range(8):
            k.tr(pv[:, c * 128:(c + 1) * 128], src.t[:, c * 128:(c + 1) * 128], ident_bf.t[:], [src, ident_bf], [pb], inc=(c == 7))
        k.copy(eng, dstT.t[:].rearrange("p a t -> p (a t)"), pv, [pb], [dstT])

    class LN0:
        def __init__(self, nh=2):
            self.nh = nh
            self.xt = [k.sb([128, D], F32, "xt") for _ in range(2)]
            self.xh = [k.sb([128, D], F32, "xh") for _ in range(nh)]
            self.hT = [k.sb([128, 8, 128], BF16, "hT") for _ in range(2)]
            self.st = k.sb([128, 2, 6], F32, "lnst")
            self.mv = k.sb([128, 2], F32, "lnmv")
            self.rs = k.sb([128, 1], F32, "lnrs")
            self.nm = k.sb([128, 1], F32, "lnnm")
            self.i = 0

        def run(self, src_buf, src_ap, si, bi, pbanks):
            s = self.i % 2
            self.i += 1
            xt, xh, hT = self.xt[s], self.xh[s % self.nh], self.hT[s]
            k.dma("sp", xt.t[:], src_ap, r=[src_buf], w=[xt])
            ln_stats(xt, self.st, self.mv, self.rs, self.nm)
            k.act(xh.t[:], xt.t[:], AF.Identity, [xt, self.rs, self.nm], [xh], bias=self.nm.t[:], scale=self.rs.t[:])
            for c in range(8):
                pb = pbanks[c // 4]
                k.tr(pb.t[:, (c % 4) * 128:(c % 4 + 1) * 128], xh.t[:, c * 128:(c + 1) * 128], cm.t[:, IDN, :],
                     [xh, cm], [pb], inc=(c % 4 == 3))
            for c in range(8):
                pb = pbanks[c // 4]
                src = pb.t[:, (c % 4) * 128:(c % 4 + 1) * 128]
                if c < 4:
                    k.ts("dve", hT.t[:, c, :], src, cols.t[:, si, c:c + 1], cols.t[:, bi, c:c + 1], ALU.mult, ALU.add, [pb, cols], [hT])
                else:
                    k.act(hT.t[:, c, :], src, AF.Identity, [pb, cols], [hT], bias=cols.t[:, bi, c:c + 1], scale=cols.t[:, si, c:c + 1])
            return xh, hT

    gmgrow = k.sb([128, D], F32, "gmgrow")
    k.dma("sp", gmgrow.t[:], rows_d.t[3:4, :].partition_broadcast(128), w=[gmgrow])
    biasM = k.sb([128, 8, 128], F32, "biasM")
    k.dma("sp", biasM.t[:].rearrange("p g c -> p (g c)"), rows_d.t[4:5, :].partition_broadcast(128), w=[biasM])
    gv = k.sb([128, D], F32, "gv")
    wsp_v = gv.t[:].rearrange("p (g q) -> p g q", g=8)
    k.dma("sp", wsp_v, wsp_d.t, w=[gv])
    rsum = k.sb([128, 8], F32, "rsum")
    k.op("dve", lambda: nc.vector.reduce_sum(out=rsum.t[:], in_=wsp_v, axis=mybir.AxisListType.X), [gv], [rsum])
    bspc = k.sb([128, 8], F32, "bspc")
    k.dma("sp", bspc.t[:], bsp_d.t, w=[bspc])
    k.tt("dve", biasM.t[:], biasM.t[:], bc(rsum.t[:].unsqueeze(2), [128, 8, 128]), ALU.mult, [biasM, rsum], [biasM])
    k.tt("dve", biasM.t[:], biasM.t[:], bc(bspc.t[:].unsqueeze(2), [128, 8, 128]), ALU.add, [biasM, bspc], [biasM])
    dtb_rows = bias_rows(small_d.t[0:1, 0:32], 32, 1.0, "dtb")
    bg2_rows = bias_rows(bgate_d.t[0:1, D:2 * D], D, 1.0, "bg2")

    if NFILL > 0:
        k.fill = fill_spec
    ln0 = LN0(2)
    pre = [k.sb([128, 12, 132], BF16, f"pre{i}") for i in range(3)]
    dgw = k.sb([128, 12, 5, 128], BF16, "dgw")
    for cc in range(12):
        for tap in range(5):
            k.ts("pool" if (cc + tap) % 2 else "dve", dgw.t[:, cc, tap, :], cm.t[:, IDN, :], convw.t[:, cc, tap:tap + 1], None,
                 ALU.mult, None, [cm, convw], [dgw])
    xbcT = k.sb([128, 12, 128], BF16, "xbcT")
    xs_tok = k.sb([128, D], BF16, "xs_tok")
    b_tok = k.sb([128, 256], BF16, "b_tok")
    a_sb = k.sb([128, 32], F32, "a_sb")
    ex = k.sb([128, 96], F32, "ex")
    ddv = k.sb([128, 32], F32, "ddv")
    xdd = k.sb([128, D], BF16, "xdd")
    hp_bf = k.sb([128, D], BF16, "hp_bf")
    e_t = k.sb([128, 32], F32, "e_t")
    decf1 = k.sb([128, 16], F32, "decf1")
    gu = k.sb([128, D], F32, "gu")
    sf1 = gu
    vhat = k.sb([128, D], BF16, "vhat")
    g2 = k.sb([128, D], F32, "g2")
    ygm = gv
    ygm_bf = k.sb([128, D], BF16, "ygm_bf")
    ygmT = k.sb([128, 8, 128], BF16, "ygmT")
    G2 = k.sb([128, D], BF16, "G2")
    vst = k.sb([128, 2, 6], F32, "vst")
    vmv = k.sb([128, 2], F32, "vmv")
    vrs = k.sb([128, 1], F32, "vrs")
    vnm = k.sb([128, 1], F32, "vnm")

    k.memset("dve", run_f.t[:], 0.0, [run_f])
    k.memset("dve", run_b.t[:], 0.0, [run_b])
    print("SBUF free in pass 1:", nc.sbuf_bytes_remaining)

    def small_exps(dt_ap, dt_buf):
        k.tt("dve", a_sb.t[:], dt_ap, arow.t[:], ALU.mult, [dt_buf, arow], [a_sb])
        pb = pss[4]
        specs = ((MLE, 0), (MGE, 16), (MGT, 0), (MLT, 16))
        for i, (mi, ao) in enumerate(specs):
            k.mm(pb.t[:, i * 16:(i + 1) * 16], cm.t[:, mi, :], a_sb.t[:, ao:ao + 16], True, True, [cm, a_sb], [pb], inc=False)
        k.mm(pb.t[:, 64:96], ones_f.t[:], a_sb.t[:, 0:32], True, True, [ones_f, a_sb], [pb], inc=True)
        k.act(ex.t[:], pb.t[:, 0:96], AF.Exp, [pb], [ex])

    order1 = [("c", 1), ("c", 0)] + [("x", cc_) for cc_ in range(NCH_RUN - 1, -1, -1)]
    lncache = {}

    def ln_for(i):
        if i >= len(order1) or i in lncache:
            return
        kind, cc_ = order1[i]
        if kind == "c":
            lncache[i] = ln0.run(ctx_d, ctx_d.t[cc_ * 128:(cc_ + 1) * 128, :], S1C, B1C, (pss[0], pss[1]))
        else:
            lncache[i] = ln0.run(x_d, x_d.t[cc_ * 128:(cc_ + 1) * 128, :], S1X, B1X, (pss[0], pss[1]))

    def s1(seq_buf, seq_ap, c, n, si, bi, slot_of, is_ctx):
        oi = order1.index(("c" if is_ctx else "x", c))
        ln_for(oi)
        xh, hT = lncache.pop(oi)
        for cc in range(12):
            pb = pss[2 + cc // 4]
            for kk in range(8):
                k.mm(pb.t[:, (cc % 4) * 128:(cc % 4 + 1) * 128], w1.t[:, kk, W_XBC + cc * 128:W_XBC + (cc + 1) * 128],
                     hT.t[:, kk, :], kk == 0, kk == 7, [w1, hT], [pb], inc=(kk == 7))
        me = pre[slot_of(c)]
        for q in range(3):
            pv = pss[2 + q].t[:].rearrange("p (a t) -> p a t", a=4)
            k.act(me.t[:, q * 4:(q + 1) * 4, 2:130], pv, AF.Copy, [pss[2 + q]], [me])
            KH = os.environ.get("KHALO", "")
            if c + 1 < n and KH != "skipL":
                k.copy("dve", pre[slot_of(c + 1)].t[:, q * 4:(q + 1) * 4, 0:2], pv[:, :, 126:128], [pss[2 + q]], [pre[slot_of(c + 1)]])
            if c - 1 >= 0 and KH != "skipR":
                k.copy("dve", pre[slot_of(c - 1)].t[:, q * 4:(q + 1) * 4, 130:132], pv[:, :, 0:2], [pss[2 + q]], [pre[slot_of(c - 1)]])
        if c == n - 1:
            k.memset("dve", me.t[:, :, 130:132], 0.0, [me])
        if c == 0:
            k.memset("dve", me.t[:, :, 0:2], 0.0, [me])
        yield "halo"
        pb = pss[5]
        proj_tok(hT, w1, W_DT, 32, pb, dtb_rows, 0)
        k.act(e_t.t[:], pb.t[:, 0:32], AF.Exp, [pb], [e_t])
        dts = dtc_all if is_ctx else dt_all
        k.act(dts.t[:, c, :], e_t.t[:], AF.Ln, [e_t], [dts], bias=1.0)
        if is_ctx or os.environ.get("KSKIP_GM"):
            ln_for(oi + 1)
            return
        yield
        for half in range(2):
            pb = pss[5 + half]
            proj_tok(hT, w1, W_U + half * 512, 512, pb)
            k.act(gu.t[:, half * 512:(half + 1) * 512], pb.t[:], AF.Gelu_apprx_tanh, [pb], [gu])
        yield
        for half in range(2):
            pb = pss[5 + half]
            proj_tok(hT, w1, W_V + half * 512, 512, pb)
            k.act(gv.t[:, half * 512:(half + 1) * 512], pb.t[:], AF.Gelu_apprx_tanh, [pb], [gv])
        yield
        ln_stats(gv, vst, vmv, vrs, vnm)
        k.act(vhat.t[:], gv.t[:], AF.Identity, [gv, vrs, vnm], [vhat], bias=vnm.t[:], scale=vrs.t[:])
        for half in range(2):
            pb = pss[5 + half]
            proj_tok(hT, w1, W_G2 + half * 512, 512, pb, bg2_rows, half * 512)
            k.act(g2.t[:, half * 512:(half + 1) * 512], pb.t[:], AF.Sigmoid, [pb], [g2])
        yield
        ln_for(oi + 1)
        yield
        for g in range(8):
            pb = pss[5 + g // 4]
            k.mm(pb.t[:, (g % 4) * 128:(g % 4 + 1) * 128], wspT.t[:, g, :], vhat.t[:, g * 128:(g + 1) * 128],
                 True, True, [wspT, vhat], [pb], inc=(g % 4 == 3))
        bM = biasM.t[:].rearrange("p g c -> p (g c)")
        for half in range(2):
            sl = slice(half * 512, (half + 1) * 512)
            pb = pss[5 + half]
            k.tt("dve", ygm.t[:, sl], pb.t[:], gmgrow.t[:, sl], ALU.mult, [pb, gmgrow], [ygm])
            k.tt("dve", ygm.t[:, sl], ygm.t[:, sl], bM[:, sl], ALU.add, [ygm, biasM], [ygm])
            k.tt("dve", ygm_bf.t[:, sl], ygm.t[:, sl], gu.t[:, sl], ALU.mult, [ygm, gu], [ygm_bf])
        yield
        transpose_to(ygmT, ygm_bf, pss[5], "dve")
        for half in range(2):
            pb = pss[5 + half]
            for kk in range(8):
                k.mm(pb.t[:], ygmT.t[:, kk, :], wgm.t[:, kk, half * 512:(half + 1) * 512], kk == 0, kk == 7,
                     [ygmT, wgm], [pb], inc=(kk == 7))
            k.tt("dve", G2.t[:, half * 512:(half + 1) * 512], pb.t[:], g2.t[:, half * 512:(half + 1) * 512], ALU.mult, [pb, g2], [G2])
        k.dma("pool", g2_s[c].t, G2.t[:], r=[G2], w=[g2_s[c]])

    def post(c, n, slot_of, is_ctx):
        if os.environ.get("KSKIP_POST") and not is_ctx:
            return
        me = pre[slot_of(c)]
        for cc in range(12):
            pb = pss[2 + cc // 4]
            for tap in range(5):
                k.mm(pb.t[:, (cc % 4) * 128:(cc % 4 + 1) * 128], dgw.t[:, cc, tap, :], me.t[:, cc, tap:tap + 128],
                     tap == 0, tap == 4, [dgw, me], [pb], inc=(tap == 4 and cc % 4 == 3))
        for cc in range(12):
            pb = pss[2 + cc // 4]
            k.act(xbcT.t[:, cc, :], pb.t[:, (cc % 4) * 128:(cc % 4 + 1) * 128], AF.Silu, [pb, convb], [xbcT], bias=convb.t[:, cc:cc + 1])
        yield

        pv0 = pss[0].t[:].bitcast(BF16)
        pv1 = pss[1].t[:].bitcast(BF16)
        for cix in range(8):
            k.tr(pv0[:, cix * 128:(cix + 1) * 128], xbcT.t[:, cix, :], ident_bf.t[:], [xbcT, ident_bf], [pss[0]], inc=(cix == 7))
        for cix in range(2):
            k.tr(pv1[:, cix * 128:(cix + 1) * 128], xbcT.t[:, 8 + cix, :], ident_bf.t[:], [xbcT, ident_bf], [pss[1]], inc=(cix == 1))
        k.copy("dve", xs_tok.t[:], pv0, [pss[0]], [xs_tok])
        k.copy("act", b_tok.t[:], pv1[:, 0:256], [pss[1]], [b_tok])
        if not is_ctx:
            k.dma("pool", xs_s[c].t, xs_tok.t[:], r=[xs_tok], w=[xs_s[c]])
            k.dma("pool", bt_s[c].t, b_tok.t[:], r=[b_tok], w=[bt_s[c]])
            k.dma("pool", bc_s[c].t, xbcT.t[:, 8:12, :], r=[xbcT], w=[bc_s[c]])
        yield
        dts = dtc_all if is_ctx else dt_all
        small_exps(dts.t[:, c, :], dts)
        k.tt("dve", ddv.t[:, 16:32], ex.t[:, 48:64], dts.t[:, c, 16:32], ALU.mult, [ex, dts], [ddv])
        if is_ctx:
            k.tt("dve", ddv.t[:, 0:16], ex.t[:, 32:48], dts.t[:, c, 0:16], ALU.mult, [ex, dts], [ddv])
        dirs = ((1, run_b),) + (((0, run_f),) if is_ctx else ())
        for d, run in dirs:
            yield
            k.tt("dve", xdd.t[:].rearrange("p (h q) -> p h q", h=H), xs_tok.t[:].rearrange("p (h q) -> p h q", h=H),
                 bc(ddv.t[:, d * 16:(d + 1) * 16].unsqueeze(2), [128, H, P]), ALU.mult, [xs_tok, ddv], [xdd])
            for g in range(2):
                pb = pss[2 + g]
                k.mm(pb.t[:], b_tok.t[:, g * 128:(g + 1) * 128], xdd.t[:, g * 512:(g + 1) * 512], True, True,
                     [b_tok, xdd], [pb], inc=True)
            dec = ex.t[:, 64 + d * 16:80 + d * 16]
            if is_ctx and d == 0:
                if c == 1:
                    for g in range(2):
                        k.copy("act", sf1.t[:, g * 512:(g + 1) * 512], pss[2 + g].t[:], [pss[2 + g]], [sf1])
                    k.copy("dve", decf1.t[:], dec, [ex], [decf1])
                else:
                    for g in range(2):
                        sl = slice(g * 512, (g + 1) * 512)
                        k.tt("dve", run_f.t[:, sl].rearrange("p (h q) -> p h q", h=8), pss[2 + g].t[:].rearrange("p (h q) -> p h q", h=8),
                             bc(decf1.t[:, g * 8:(g + 1) * 8].unsqueeze(2), [128, 8, P]), ALU.mult, [pss[2 + g], decf1], [run_f])
                        k.tt("dve", run_f.t[:, sl], run_f.t[:, sl], sf1.t[:, sl], ALU.add, [run_f, sf1], [run_f])
                continue
            if not is_ctx:
                k.copy("dve", hp_bf.t[:], run.t[:], [run], [hp_bf])
                k.dma("pool", hp_s[c].t, hp_bf.t[:], r=[hp_bf], w=[hp_s[c]])
            k.tt("dve", run.t[:].rearrange("p (h q) -> p h q", h=H), run.t[:].rearrange("p (h q) -> p h q", h=H),
                 bc(dec.unsqueeze(2), [128, H, P]), ALU.mult, [run, ex], [run])
            for g in range(2):
                sl = slice(g * 512, (g + 1) * 512)
                k.tt("dve", run.t[:, sl], run.t[:, sl], pss[2 + g].t[:], ALU.add, [run, pss[2 + g]], [run])

    def run_all(g):
        for _ in g:
            pass

    def interleave(ga, gb):
        da = db = False
        while not (da and db):
            if not da:
                try:
                    next(ga)
                except StopIteration:
                    da = True
            if not db:
                try:
                    next(gb)
                except StopIteration:
                    db = True

    dtc_all = k.sb([128, 2, 32], F32, "dtcall")
    for c in (1, 0):
        run_all(s1(ctx_d, ctx_d.t, c, 2, S1C, B1C, lambda cc: cc % 3, True))
        if c + 1 < 2:
            run_all(post(c + 1, 2, lambda cc: cc % 3, True))
    run_all(post(0, 2, lambda cc: cc % 3, True))
    k.dump("s_f", run_f, run_f.t[:], [128, D])
    k.dump("s_b", run_b, run_b.t[:], [128, D])
    NR = NCH_RUN
    KSTOP = int(os.environ.get("KSTOP", "1000"))
    steps = 0
    for c in range(NR - 1, -1, -1):
        if steps >= KSTOP:
            break
        ga = s1(x_d, x_d.t, c, NR, S1X, B1X, lambda cc: cc % 3, False)
        next(ga)
        steps += 1
        if c + 1 < NR:
            interleave(ga, post(c + 1, NR, lambda cc: cc % 3, False))
            steps += 1
        else:
            run_all(ga)
    if steps < KSTOP:
        run_all(post(0, NR, lambda cc: cc % 3, False))
    k.dump("dt_all", dt_all, dt_all.t[:].rearrange("p c d -> p (c d)"), [128, NCH * 32])
    k.pop()

    if KPASS < 2:
        k.barrier()
        k.es.close()
        return nc, k.dbg
    st = k.push()
    bg1_rows = bias_rows(bgate_d.t[0:1, 0:D], D, 1.0, "bg1")
    b0_rows = bias_rows(rows_d.t[1:2, :], D, ALPHA, "b0r")
    a0row = k.sb([128, D], F32, "a0row")
    k.dma("sp", a0row.t[:], rows_d.t[0:1, :].partition_broadcast(128), w=[a0row])
    k.ts("dve", a0row.t[:], a0row.t[:], float(ALPHA), None, ALU.mult, None, [a0row], [a0row])
    ngrow = k.sb([128, D], F32, "ngrow")
    k.dma("sp", ngrow.t[:], rows_d.t[2:3, :].partition_broadcast(128), w=[ngrow])
    make_row_psum(16, "g1row", (pss[4], pss[5]))
    wout = k.sb([128, 8, D], BF16, "wout")
    load_w(wout, wout_d.t, 8)
    w2 = k.sb([128, 8, 2048], BF16, "w2")
    for kk in range(8):
        rs_ = slice(kk * 128, (kk + 1) * 128)
        k.dma("pool", w2.t[:, kk, 0:1024], win_d.t[rs_, O_Z:O_Z + D], r=[win_d], w=[w2], partial=True)
        k.dma("pool", w2.t[:, kk, 1024:2048], win_d.t[rs_, O_G:O_G + D], r=[win_d], w=[w2], partial=True)
    wssd = k.sb([128, 8, D], BF16, "wssd")
    load_w(wssd, wssd_d.t, 8)

    def scale_wout():
        for kk in range(8):
            for half in range(2):
                sl = slice(half * 512, (half + 1) * 512)
                k.tt("dve", wout.t[:, kk, sl], wout.t[:, kk, sl], pss[4 + half].t[:], ALU.mult, [wout, pss[4 + half]], [wout])

    ln0 = LN0()
    xs_l = [k.sb([128, D], BF16, "xs_l") for _ in range(2)]
    bt_l = [k.sb([128, 256], BF16, "bt_l") for _ in range(2)]
    bc_l = [k.sb([128, 4, 128], BF16, "bc_l") for _ in range(2)]
    hp_l = [k.sb([128, D], BF16, "hp_l") for _ in range(2)]
    g2_l = [k.sb([128, D], BF16, "g2_l") for _ in range(2)]
    a_sb = k.sb([128, 32], F32, "a_sb2")
    ex = k.sb([128, 96], F32, "ex2")
    ddv = k.sb([128, 32], F32, "ddv2")
    sz_l = [k.sb([128, D], BF16, "sz") for _ in range(2)]
    g1_l = [k.sb([128, D], BF16, "g1") for _ in range(2)]
    Rf = k.sb([128, H, 128], F32R, "Rf")
    Rb = Rf
    Ef = k.sb([128, H, 128], F32, "Ef")
    cbm = k.sb([128, 2, 2, 128], F32, "cbm")
    wgt = [k.sb([128, H, 128], BF16, f"wgt{d}") for d in range(2)]
    xdt = [k.sb([128, D], BF16, f"xdt{d}") for d in range(2)]
    xdd = k.sb([128, D], BF16, "xdd2")
    runf_bf = k.sb([128, D], BF16, "runf_bf")
    t1 = k.sb([128, D], F32, "t1")
    t2 = k.sb([128, D], F32, "t2")
    hh = k.sb([128, D], F32, "hh")
    sq = t2
    ss = k.sb([128, 1], F32, "ss")
    rstd = k.sb([128, 1], F32, "rstd")
    yg = k.sb([128, D], BF16, "yg")
    ygT = k.sb([128, 8, 128], BF16, "ygT")
    m1 = hh
    mg = k.sb([128, D], BF16, "mg")
    mgT = k.sb([128, 8, 128], BF16, "mgT")
    r1 = t1
    x1h = [k.sb([128, D], F32, "x1h") for _ in range(2)]
    lst = k.sb([128, 2, 6], F32, "lst")
    lmv = k.sb([128, 2], F32, "lmv")
    lrs = k.sb([128, 1], F32, "lrs")
    lnm = k.sb([128, 1], F32, "lnm")

    k.copy("act", runf_bf.t[:], run_f.t[:], [run_f], [runf_bf])
    print("SBUF free in pass 2:", nc.sbuf_bytes_remaining)

    def h3(ap, h=H):
        return ap.rearrange("p (h q) -> p h q", h=h)

    KSTOP2 = int(os.environ.get("KSTOP2", "1000"))
    ln2cache = {}

    def front2a(c):
        ln2cache[c] = ln0.run(x_d, x_d.t[c * 128:(c + 1) * 128, :], S1X, B1X, (pss[0], pss[1]))

    def front2b(c):
        xh, hT = ln2cache[c]
        sz, g1 = sz_l[c % 2], g1_l[c % 2]
        for half in range(2):
            pb = pss[half]
            proj_tok(hT, w2, half * 512, 512, pb)
            k.act(sz.t[:, half * 512:(half + 1) * 512], pb.t[:], AF.Silu, [pb], [sz])
        for half in range(2):
            pb = pss[half]
            proj_tok(hT, w2, 1024 + half * 512, 512, pb, bg1_rows, half * 512)
            k.act(g1.t[:, half * 512:(half + 1) * 512], pb.t[:], AF.Sigmoid, [pb], [g1])

    front2a(0)
    scale_wout()
    front2b(0)
    for c in range(min(NR, KSTOP2)):
        s = c % 2
        xs, bt, bcl, hp, g2l = xs_l[s], bt_l[s], bc_l[s], hp_l[s], g2_l[s]
        k.dma("sp", xs.t[:], xs_s[c].t, r=[xs_s[c]], w=[xs])
        k.dma("sp", bt.t[:], bt_s[c].t, r=[bt_s[c]], w=[bt])
        k.dma("sp", bcl.t[:], bc_s[c].t, r=[bc_s[c]], w=[bcl])
        k.dma("sp", hp.t[:], hp_s[c].t, r=[hp_s[c]], w=[hp])
        k.dma("sp", g2l.t[:], g2_s[c].t, r=[g2_s[c]], w=[g2l])
        xh, hT = ln2cache.pop(c)
        sz, g1 = sz_l[s], g1_l[s]
        xo = x1h[s]
        k.tt("dve", xo.t[:], xh.t[:], a0row.t[:], ALU.mult, [xh, a0row], [xo])
        if c + 1 < min(NR, KSTOP2):
            front2a(c + 1)
        dtc = dt_all.t[:, c, :]
        k.tt("dve", a_sb.t[:], dtc, arow.t[:], ALU.mult, [dt_all, arow], [a_sb])
        pb = pss[6]
        for i, (mi, ao) in enumerate(((MLE, 0), (MGE, 16), (MGT, 0), (MLT, 16))):
            k.mm(pb.t[:, i * 16:(i + 1) * 16], cm.t[:, mi, :], a_sb.t[:, ao:ao + 16], True, True, [cm, a_sb], [pb], inc=False)
        k.mm(pb.t[:, 64:96], ones_f.t[:], a_sb.t[:, 0:32], True, True, [ones_f, a_sb], [pb], inc=True)
        k.act(ex.t[:], pb.t[:, 0:96], AF.Exp, [pb], [ex])
        pb = pss[6]
        for g in range(2):
            k.mm(pb.t[:, 256 + g * 128:256 + (g + 1) * 128], bcl.t[:, g, :], bcl.t[:, 2 + g, :], True, True, [bcl], [pb], inc=(g == 1))
        pv = pb.t[:, 256:512].rearrange("p (g i) -> p g i", g=2)
        k.tt("dve", cbm.t[:, 0, :, :], pv, bc(cm.t[:, MLE, :].unsqueeze(1), [128, 2, 128]), ALU.mult, [pb, cm], [cbm])
        k.tt("dve", cbm.t[:, 1, :, :], pv, bc(cm.t[:, MGE, :].unsqueeze(1), [128, 2, 128]), ALU.mult, [pb, cm], [cbm])
        for d in range(2):
            k.tt("pool", h3(xdt[d].t[:]), h3(xs.t[:]), bc(dtc[:, d * 16:(d + 1) * 16].unsqueeze(2), [128, H, P]),
                 ALU.mult, [xs, dt_all], [xdt[d]])
        for d, (R, Lm) in enumerate(((Rf, mgt_r), (Rb, mlt_r))):
            k.tt("pool", R.t[:], bc(a_sb.t[:, d * 16:(d + 1) * 16].unsqueeze(2), [128, H, 128]),
                 bc(cm.t[:, MGE if d else MLE, :].unsqueeze(1), [128, H, 128]), ALU.mult, [a_sb, cm], [R])
            R2 = R.t[:].rearrange("p h i -> p (h i)")
            E2 = Ef.t[:].rearrange("p h i -> p (h i)")
            for q in range(4):
                pb = pss[2 + q]
                k.mm(pb.t[:], Lm.t[:], R2[:, q * 512:(q + 1) * 512], True, True, [Lm, R], [pb], inc=True)
                k.act(E2[:, q * 512:(q + 1) * 512], pb.t[:], AF.Exp, [pb], [Ef])
            for g in range(2):
                k.tt("dve", wgt[d].t[:, g * 8:(g + 1) * 8, :], Ef.t[:, g * 8:(g + 1) * 8, :],
                     bc(cbm.t[:, d, g, :].unsqueeze(1), [128, 8, 128]), ALU.mult, [Ef, cbm], [wgt[d]])
        if c + 1 < min(NR, KSTOP2):
            front2b(c + 1)
        for h in range(H):
            pb = pss[h // 8]
            o = pb.t[:, (h % 8) * 64:(h % 8 + 1) * 64]
            cs = slice(h * 64, (h + 1) * 64)
            k.mm(o, wgt[0].t[:, h, :], xdt[0].t[:, cs], True, False, [wgt[0], xdt[0]], [pb], inc=False)
            k.mm(o, wgt[1].t[:, h, :], xdt[1].t[:, cs], False, False, [wgt[1], xdt[1]], [pb], inc=False)
            k.mm(o, dh.t[:, h, :], xs.t[:, cs], False, True, [dh, xs], [pb], inc=(h % 8 == 7))
        for g in range(2):
            k.mm(pss[2 + g].t[:], bcl.t[:, 2 + g, :], runf_bf.t[:, g * 512:(g + 1) * 512], True, True, [bcl, runf_bf], [pss[2 + g]], inc=True)
            k.mm(pss[4 + g].t[:], bcl.t[:, 2 + g, :], hp.t[:, g * 512:(g + 1) * 512], True, True, [bcl, hp], [pss[4 + g]], inc=True)
        for g in range(2):
            sl = slice(g * 512, (g + 1) * 512)
            k.tt("dve", h3(t1.t[:, sl], 8), h3(pss[2 + g].t[:], 8), bc(ex.t[:, g * 8:(g + 1) * 8].unsqueeze(2), [128, 8, P]),
                 ALU.mult, [pss[2 + g], ex], [t1])
            k.tt("dve", h3(t2.t[:, sl], 8), h3(pss[4 + g].t[:], 8), bc(ex.t[:, 16 + g * 8:16 + (g + 1) * 8].unsqueeze(2), [128, 8, P]),
                 ALU.mult, [pss[4 + g], ex], [t2])
            k.tt("pool", t1.t[:, sl], t1.t[:, sl], t2.t[:, sl], ALU.add, [t1, t2], [t1])
            k.tt("dve", t1.t[:, sl], t1.t[:, sl], pss[g].t[:], ALU.add, [t1, pss[g]], [t1])
        if c == 0:
            k.dump("y0", t1, t1.t[:], [128, D])
        k.tt("dve", hh.t[:], t1.t[:], sz.t[:], ALU.mult, [t1, sz], [hh])
        k.act(sq.t[:], hh.t[:], AF.Square, [hh], [sq, ss], accum=ss.t[:])
        rsqrt_act(rstd, ss.t[:], ss, 1.0 / D)
        k.stt(yg.t[:], hh.t[:], rstd.t[:], ngrow.t[:], ALU.mult, ALU.mult, [hh, rstd, ngrow], [yg])

        transpose_to(ygT, yg, pss[6], "act")
        k.tt("dve", ddv.t[:, 0:16], ex.t[:, 32:48], dtc[:, 0:16], ALU.mult, [ex, dt_all], [ddv])
        k.tt("dve", h3(xdd.t[:]), h3(xs.t[:]), bc(ddv.t[:, 0:16].unsqueeze(2), [128, H, P]), ALU.mult, [xs, ddv], [xdd])
        for g in range(2):
            k.mm(pss[2 + g].t[:], bt.t[:, g * 128:(g + 1) * 128], xdd.t[:, g * 512:(g + 1) * 512], True, True, [bt, xdd], [pss[2 + g]], inc=True)
        k.tt("dve", h3(run_f.t[:]), h3(run_f.t[:]), bc(ex.t[:, 64:80].unsqueeze(2), [128, H, P]), ALU.mult, [run_f, ex], [run_f])
        for g in range(2):
            sl = slice(g * 512, (g + 1) * 512)
            k.tt("dve", run_f.t[:, sl], run_f.t[:, sl], pss[2 + g].t[:], ALU.add, [run_f, pss[2 + g]], [run_f])
        k.copy("act", runf_bf.t[:], run_f.t[:], [run_f], [runf_bf])
        for half in range(2):
            sl = slice(half * 512, (half + 1) * 512)
            pb = pss[4 + half]
            for kk in range(8):
                k.mm(pb.t[:], ygT.t[:, kk, :], wssd.t[:, kk, sl], kk == 0, kk == 7, [ygT, wssd], [pb], inc=(kk == 7))
            k.tt("dve", m1.t[:, sl], pb.t[:], g1.t[:, sl], ALU.mult, [pb, g1], [m1])
            k.tt("dve", mg.t[:, sl], m1.t[:, sl], g2l.t[:, sl], ALU.add, [m1, g2l], [mg])

        transpose_to(mgT, mg, pss[6], "dve")
        for half in range(2):
            sl = slice(half * 512, (half + 1) * 512)
            pb = pss[half]
            for kk in range(8):
                k.mm(pb.t[:], mgT.t[:, kk, :], wout.t[:, kk, sl], kk == 0, False, [mgT, wout], [pb], inc=False)
            k.mm(pb.t[:], ones_bf.t[0:2, :], b0_rows.t[0:2, sl], False, True, [ones_bf, b0_rows], [pb], inc=True)
            k.tt("dve", r1.t[:, sl], xo.t[:, sl], pb.t[:], ALU.add, [xo, pb], [r1])
        if c == 0:
            k.dump("r1", r1, r1.t[:], [128, D])
        ln_stats(r1, lst, lmv, lrs, lnm)
        k.act(xo.t[:], r1.t[:], AF.Identity, [r1, lrs, lnm], [xo], bias=lnm.t[:], scale=lrs.t[:])
        k.dma("pool", x1_s[c].t, xo.t[:], r=[xo], w=[x1_s[c]])
    k.pop()

    k.fill = None
    if KPASS < 3:
        k.barrier()
        k.es.close()
        return nc, k.dbg
    st = k.push()
    b1_rows = bias_rows(rows_d.t[6:7, :], D, ALPHA, "b1r")
    a1row = k.sb([128, D], F32, "a1row")
    k.dma("sp", a1row.t[:], rows_d.t[5:6, :].partition_broadcast(128), w=[a1row])
    k.ts("dve", a1row.t[:], a1row.t[:], float(ALPHA), None, ALU.mult, None, [a1row], [a1row])
    l2g = k.sb([128, D], F32, "l2g")
    k.dma("sp", l2g.t[:], rows_d.t[7:8, :].partition_broadcast(128), w=[l2g])
    l2b = k.sb([128, D], F32, "l2b")
    k.dma("sp", l2b.t[:], rows_d.t[8:9, :].partition_broadcast(128), w=[l2b])
    make_row_psum(40, "g2row", (pss[6], pss[7]))
    wf1 = k.sb([128, 8, DFF], BF16, "wf1")
    wf3 = k.sb([128, 8, DFF], BF16, "wf3")
    wf2 = k.sb([128, NFF, D], BF16, "wf2")
    FG = ((0, 6), (6, 12), (12, 17), (17, 22))
    wf1g = [Buf(wf1.t) for _ in FG]
    wf3g = [Buf(wf3.t) for _ in FG]
    for gi, (f0, f1) in enumerate(FG):
        for kk in range(8):
            k.dma("pool", wf1.t[:, kk, f0 * 128:f1 * 128], wff1_d.t[kk * 128:(kk + 1) * 128, f0 * 128:f1 * 128], w=[wf1g[gi]], partial=True)
        for kk in range(8):
            k.dma("pool", wf3.t[:, kk, f0 * 128:f1 * 128], wff3_d.t[kk * 128:(kk + 1) * 128, f0 * 128:f1 * 128], w=[wf3g[gi]], partial=True)
    load_w(wf2, wff2_d.t, NFF)

    def fgrp(f):
        return [gi for gi, (f0, f1) in enumerate(FG) if f0 <= f < f1][0]

    def scale_wf2():
        for kk in range(NFF):
            for half in range(2):
                sl = slice(half * 512, (half + 1) * 512)
                k.tt("dve", wf2.t[:, kk, sl], wf2.t[:, kk, sl], pss[6 + half].t[:], ALU.mult, [wf2, pss[6 + half]], [wf2])
    TB = 2
    NB = NR // TB
    x1t = [[k.sb([128, D], F32, "x1t") for _ in range(TB)] for _ in range(2)]
    xmT = [k.sb([128, 8, TB * 128], BF16, "xmT") for _ in range(2)]
    hid = k.sb([128, NFF, TB * 128], BF16, "hid")
    sl1 = [k.sb([128, TB * 128], F32, "sl1") for _ in range(2)]
    fst = k.sb([128, 2, 6], F32, "fst")
    fmv = k.sb([128, 2], F32, "fmv")
    frs = k.sb([128, 1], F32, "frs")
    fnm = k.sb([128, 1], F32, "fnm")
    NT = TB * 128
    print("SBUF free in pass 3:", nc.sbuf_bytes_remaining)

    def ffn_front(b):
        s = b % 2
        for t in range(TB):
            c = b * TB + t
            xt = x1t[s][t]
            k.dma("sp", xt.t[:], x1_s[c].t, r=[x1_s[c]], w=[xt])
            for cc in range(8):
                pb = pss[cc // 4]
                k.tr(pb.t[:, (cc % 4) * 128:(cc % 4 + 1) * 128], xt.t[:, cc * 128:(cc + 1) * 128], cm.t[:, IDN, :],
                     [xt, cm], [pb], inc=(cc % 4 == 3))
            for cc in range(8):
                pb = pss[cc // 4]
                src = pb.t[:, (cc % 4) * 128:(cc % 4 + 1) * 128]
                dst = xmT[s].t[:, cc, t * 128:(t + 1) * 128]
                if cc < 4:
                    k.ts("dve", dst, src, cols.t[:, S2, cc:cc + 1], cols.t[:, B2, cc:cc + 1], ALU.mult, ALU.add, [pb, cols], [xmT[s]])
                else:
                    k.act(dst, src, AF.Identity, [pb, cols], [xmT[s]], bias=cols.t[:, B2, cc:cc + 1], scale=cols.t[:, S2, cc:cc + 1])
            k.tt("dve", xt.t[:], xt.t[:], a1row.t[:], ALU.mult, [xt, a1row], [xt])

    def ffn_w13(b):
        s = b % 2
        for f in range(NFF):
            p1 = pss[2 + (f % 2) * 2]
            p3 = pss[3 + (f % 2) * 2]
            for kk in range(8):
                k.mm(p1.t[:, 0:NT], wf1.t[:, kk, f * 128:(f + 1) * 128], xmT[s].t[:, kk, :], kk == 0, kk == 7, [wf1g[fgrp(f)], xmT[s]], [p1], inc=(kk == 7))
            for kk in range(8):
                k.mm(p3.t[:, 0:NT], wf3.t[:, kk, f * 128:(f + 1) * 128], xmT[s].t[:, kk, :], kk == 0, kk == 7, [wf3g[fgrp(f)], xmT[s]], [p3], inc=(kk == 7))
            sv = sl1[f % 2]
            k.act(sv.t[:], p1.t[:, 0:NT], AF.Silu, [p1], [sv])
            k.tt("dve", hid.t[:, f, :], sv.t[:], p3.t[:, 0:NT], ALU.mult, [sv, p3], [hid])

    def ffn_w2(b):
        s = b % 2
        for t in range(TB):
            c = b * TB + t
            xt = x1t[s][t]
            r2 = xt
            for half in range(2):
                sl = slice(half * 512, (half + 1) * 512)
                pb = pss[6 + half]
                for f in range(NFF):
                    k.mm(pb.t[:], hid.t[:, f, t * 128:(t + 1) * 128], wf2.t[:, f, sl], f == 0, False, [hid, wf2], [pb], inc=False)
                k.mm(pb.t[:], ones_bf.t[0:2, :], b1_rows.t[0:2, sl], False, True, [ones_bf, b1_rows], [pb], inc=True)
                k.tt("dve", r2.t[:, sl], r2.t[:, sl], pb.t[:], ALU.add, [r2, pb], [r2])
            ln_stats(r2, fst, fmv, frs, fnm)
            o = r2
            k.act(o.t[:], r2.t[:], AF.Identity, [r2, frs, fnm], [o], bias=fnm.t[:], scale=frs.t[:])
            k.tt("dve", o.t[:], o.t[:], l2g.t[:], ALU.mult, [o, l2g], [o])
            k.tt("dve", o.t[:], o.t[:], l2b.t[:], ALU.add, [o, l2b], [o])
            k.dma("pool", out_d.t[c * 128:(c + 1) * 128, :], o.t[:], r=[o], w=[out_d])

    if NB > 0:
        ffn_front(0)
    for b in range(NB):
        ffn_w13(b)
        if b + 1 < NB:
            ffn_front(b + 1)
        if b == 0:
            scale_wf2()
        ffn_w2(b)
    k.pop()
    k.barrier()
    k.es.close()
    return nc, k.dbg


def _prep(inputs):
    f = lambda a: np.ascontiguousarray(np.asarray(a, dtype=np.float32))
    i = {kk: f(v) for kk, v in inputs.items()}
    kk = np.arange(128)
    cm = np.stack([np.eye(128), kk[:, None] <= kk[None, :], kk[:, None] >= kk[None, :],
                   kk[:, None] > kk[None, :], kk[:, None] < kk[None, :]], axis=1).astype(np.float32)
    col = lambda v: f(v.reshape(-1, 128).T)
    shared = dict(
        cm=f(cm), w_ada=i["w_ada"][0], b_ada_c=col(i["b_ada"][0]),
        colp=f(np.stack([col(i["ln0_g"]), col(i["ln0_b"]), col(i["ln1_g"][0]), col(i["ln1_b"][0])], axis=1)),
        convw_c=f(i["conv_w"][0].T.reshape(12, 128, 5).transpose(1, 0, 2)),
        convb_c=col(i["conv_b"][0]), w_in=i["w_in"][0],
        rows=f(np.stack([i["ln0_g"], i["ln0_b"], i["ssd_norm_g"][0], i["gm_norm_g"][0], i["gm_norm_b"][0],
                         i["ln1_g"][0], i["ln1_b"][0], i["ln2_g"][0], i["ln2_b"][0]])),
        b_gate=f(i["b_gate"][0][None, :]),
        small=f(np.concatenate([i["dt_bias"][0].reshape(-1), i["a_log"][0].reshape(-1), i["d_skip"][0].reshape(-1)])[None, :]),
        wspT=f(i["w_spatial"][0].transpose(2, 0, 1)), wsp=f(i["w_spatial"][0].transpose(1, 0, 2)),
        bsp_c=f(i["b_spatial"][0].T),
        w_ssd=i["w_ssd_proj"][0], w_gm=i["w_gm_proj"][0], w_out=i["w_out"][0],
        w_ff1=i["w_ff1"][0], w_ff3=i["w_ff3"][0], w_ff2=i["w_ff2"][0],
    )
    maps = []
    for b in range(i["x"].shape[0]):
        m = dict(shared)
        m["x"] = i["x"][b]
        m["ctx"] = i["ctx"][b]
        m["cT"] = f(np.stack([i["c"][b], i["c_ctx"]], axis=1).reshape(8, 128, 2).transpose(1, 0, 2))
        maps.append(m)
    return maps


def kernel(**inputs):
    maps = _prep(inputs)
    nc, _ = build()
    n = len(maps)
    res = run_bass_kernel_spmd(nc, maps, core_ids=list(range(n)))
    return np.stack([np.asarray(r["out"], dtype=np.float32) for r in res.results], axis=0)
```
